# Optimizing a Trainium2 kernel written in Bass

```python
import jax, jax.numpy as jnp
from jax import lax
import numpy as np

D_MODEL = 1024
BATCH = 32
SEQ = 256
DEPTH = 2
DEC_BATCH = 2
DEC_SEQ = 2048
PAST_LEN = 256

GRID_W = 64
HEAD_DIM = 64
N_MOD = 9
D_FF = 2816
FFN_RES = 0.5
H_A = 8
NA_ROWS = 8
NA_COLS = 16
NA_KEY_COLS = 2 * NA_COLS
C_B = 512
CONV_W = 3
C_POOL = 512
POOL_WINDOWS = (2, 4, 8, 16)
N_POOL = 4
HQ_D = 8
HKV_D = 2
GQA_GROUP = HQ_D // HKV_D
ROPE_BASE = 10000.0
Q_BLOCK = 128
N_EVEN = (DEPTH + 1) // 2
N_ODD = DEPTH // 2
EVEN_IN = 3 * H_A * HEAD_DIM + 3 * C_B
EVEN_MIX = H_A * HEAD_DIM + C_B
ODD_IN = C_POOL + (HQ_D + 2 * HKV_D) * HEAD_DIM
ODD_MIX = C_POOL + HQ_D * HEAD_DIM
RMS_EPS = 1e-6
NEG_INF = -1e30

kernel_name = "hybrid_diffusion_prefix_trunk_step"


def rms_norm(x, g):
    xf = x.astype(jnp.float32)
    y = xf * lax.rsqrt(jnp.mean(xf * xf, axis=-1, keepdims=True) + RMS_EPS)
    return (y * g.astype(jnp.float32)).astype(x.dtype)


def modulate_norm(x, g, shift, scale):
    return rms_norm(x, g) * (1.0 + scale[:, None, :]) + shift[:, None, :]


def split_heads(t, n_heads):
    b, l, _ = t.shape
    return t.reshape(b, l, n_heads, HEAD_DIM).transpose(0, 2, 1, 3)


def merge_heads(t):
    b, h, l, d = t.shape
    return t.transpose(0, 2, 1, 3).reshape(b, l, h * d)


def swiglu(h, w1, w2):
    g, u = jnp.split(h @ w1, 2, axis=-1)
    return (jax.nn.silu(g) * u) @ w2


def axial_rope_tables(n_tokens):
    t = jnp.arange(n_tokens)
    row = (t // GRID_W).astype(jnp.float32)
    col = (t % GRID_W).astype(jnp.float32)
    n_freq = HEAD_DIM // 4
    inv = ROPE_BASE ** (-jnp.arange(n_freq, dtype=jnp.float32) / n_freq)
    ang = jnp.concatenate([row[:, None] * inv, col[:, None] * inv], axis=-1)
    return jnp.cos(ang), jnp.sin(ang)


def apply_rope(x, cos, sin):
    xr = x.astype(jnp.float32).reshape(*x.shape[:-1], HEAD_DIM // 2, 2)
    x0, x1 = xr[..., 0], xr[..., 1]
    out = jnp.stack([x0 * cos - x1 * sin, x0 * sin + x1 * cos], axis=-1)
    return out.reshape(x.shape).astype(x.dtype)


def dense_attention(q, k, v):
    b, hkv, g, lq, d = q.shape
    nb = lq // Q_BLOCK
    qb = q.reshape(b, hkv, g, nb, Q_BLOCK, d).transpose(3, 0, 1, 2, 4, 5)
    scale = HEAD_DIM ** -0.5

    def one_block(qi):
        s = jnp.einsum('bhgqd,bhkd->bhgqk', qi, k).astype(jnp.float32) * scale
        p = jax.nn.softmax(s, axis=-1).astype(v.dtype)
        return jnp.einsum('bhgqk,bhkd->bhgqd', p, v)

    o = lax.map(one_block, qb)
    return o.transpose(1, 2, 3, 0, 4, 5).reshape(b, hkv, g, lq, d)


def neighbourhood_attention(q, k, v, ck, cv, rpb):
    b, h, l, d = q.shape
    rows = l // GRID_W
    kr = min(NA_ROWS, rows)
    ncb = GRID_W // NA_COLS
    r = jnp.arange(rows)
    rs = jnp.clip(r - kr // 2, 0, rows - kr)
    rows_idx = rs[:, None] + jnp.arange(kr)[None, :]
    cq = jnp.arange(GRID_W).reshape(ncb, NA_COLS)
    cs_q = jnp.clip(cq - NA_COLS // 2, 0, GRID_W - NA_COLS)
    kb0 = jnp.clip(jnp.arange(ncb) * NA_COLS - NA_COLS // 2, 0, GRID_W - NA_KEY_COLS)
    cols_idx = kb0[:, None] + jnp.arange(NA_KEY_COLS)[None, :]
    kg = k.reshape(b, h, rows, GRID_W, d)
    vg = v.reshape(b, h, rows, GRID_W, d)
    ridx = rows_idx[:, None, :, None]
    cidx = cols_idx[None, :, None, :]
    kb = kg[:, :, ridx, cidx].reshape(b, h, rows, ncb, kr * NA_KEY_COLS, d)
    vb = vg[:, :, ridx, cidx].reshape(b, h, rows, ncb, kr * NA_KEY_COLS, d)
    kcol = cols_idx[:, None, :]
    valid = (kcol >= cs_q[..., None]) & (kcol < cs_q[..., None] + NA_COLS)
    col_off = jnp.clip(kcol - cq[..., None] + NA_COLS - 1, 0, 2 * NA_COLS - 2)
    row_off = rows_idx - r[:, None] + NA_ROWS - 1
    bias = rpb.astype(jnp.float32)[:, row_off[:, None, None, :, None], col_off[None, :, :, None, :]]
    bias = jnp.where(valid[None, None, :, :, None, :], bias, NEG_INF)
    bias = bias.reshape(h, rows, ncb, NA_COLS, kr * NA_KEY_COLS)
    qg = q.reshape(b, h, rows, ncb, NA_COLS, d)
    scale = HEAD_DIM ** -0.5
    s_loc = jnp.einsum('bhrjqd,bhrjkd->bhrjqk', qg, kb).astype(jnp.float32) * scale + bias[None]
    s_ctx = jnp.einsum('bhrjqd,bhkd->bhrjqk', qg, ck).astype(jnp.float32) * scale
    p = jax.nn.softmax(jnp.concatenate([s_loc, s_ctx], axis=-1), axis=-1).astype(v.dtype)
    n_loc = kr * NA_KEY_COLS
    o = (jnp.einsum('bhrjqk,bhrjkd->bhrjqd', p[..., :n_loc], vb)
         + jnp.einsum('bhrjqk,bhkd->bhrjqd', p[..., n_loc:], cv))
    return o.reshape(b, h, l, d)


def short_conv(x, w, bias):
    l = x.shape[1]
    pad = CONV_W // 2
    xp = jnp.pad(x, ((0, 0), (pad, CONV_W - 1 - pad), (0, 0)))
    y = xp[:, 0:l] * w[0]
    for j in range(1, CONV_W):
        y = y + xp[:, j:j + l] * w[j]
    return y + bias


def window_mean(csum, l, win):
    t = jnp.arange(l)
    lo = jnp.maximum(t - win // 2, 0)
    hi = jnp.minimum(t + win - win // 2, l)
    return (csum[:, hi] - csum[:, lo]) / (hi - lo).astype(jnp.float32)[None, :, None]


def even_mixer(h, w_in, rpb, conv_w, conv_b, w_out, ctx_kv):
    u = h @ w_in
    qa, ka, va, bg, cg, xb = jnp.split(u, 6, axis=-1)
    qa, ka, va = split_heads(qa, H_A), split_heads(ka, H_A), split_heads(va, H_A)
    if ctx_kv is None:
        a = dense_attention(qa[:, :, None], ka, va)[:, :, 0]
        new_kv = (ka, va)
    else:
        a = neighbourhood_attention(qa, ka, va, ctx_kv[0], ctx_kv[1], rpb)
        new_kv = None
    y_b = bg * short_conv(cg * xb, conv_w, conv_b)
    out = jnp.concatenate([merge_heads(a), y_b], axis=-1) @ w_out
    return out, new_kv


def odd_mixer(h, w_in, pool_w, pool_scale, q_norm, k_norm, w_out, ctx_kv):
    b, l, _ = h.shape
    u = h @ w_in
    uc, qd, kd, vd = jnp.split(u, [C_POOL, C_POOL + HQ_D * HEAD_DIM, C_POOL + (HQ_D + HKV_D) * HEAD_DIM], axis=-1)
    ug = uc.reshape(b, l, N_POOL, C_POOL // N_POOL)
    csum = jnp.concatenate([jnp.zeros_like(ug[:, :1], dtype=jnp.float32),
                            jnp.cumsum(ug.astype(jnp.float32), axis=1)], axis=1)
    pooled = jnp.stack([window_mean(csum[:, :, gi], l, w) for gi, w in enumerate(POOL_WINDOWS)], axis=2)
    pooled = (pooled - ug.astype(jnp.float32)).astype(h.dtype)
    yc = jnp.einsum('blgc,gce->blge', pooled, pool_w).reshape(b, l, C_POOL) * pool_scale
    q = rms_norm(split_heads(qd, HQ_D), q_norm)
    k = rms_norm(split_heads(kd, HKV_D), k_norm)
    v = split_heads(vd, HKV_D)
    if ctx_kv is None:
        k_all, v_all = k, v
        new_kv = (k, v)
    else:
        cos, sin = axial_rope_tables(l)
        q = apply_rope(q, cos, sin)
        k_all = jnp.concatenate([apply_rope(k, cos, sin), ctx_kv[0]], axis=2)
        v_all = jnp.concatenate([v, ctx_kv[1]], axis=2)
        new_kv = None
    o = dense_attention(q.reshape(b, HKV_D, GQA_GROUP, l, HEAD_DIM), k_all, v_all).reshape(b, HQ_D, l, HEAD_DIM)
    out = jnp.concatenate([yc, merge_heads(o)], axis=-1) @ w_out
    return out, new_kv


def trunk(x, cond, cache, mod_w, mod_b, norm_w, ffn_w1, ffn_w2,
          ev_w_in, ev_rpb, ev_conv_w, ev_conv_b, ev_w_out,
          od_w_in, od_pool_w, od_pool_scale, od_q_norm, od_k_norm, od_w_out):
    a_ks, a_vs, d_ks, d_vs = [], [], [], []
    for l in range(DEPTH):
        m = (jax.nn.silu(cond) @ mod_w[l] + mod_b[l]).reshape(cond.shape[0], N_MOD, D_MODEL)
        g = norm_w[l]
        hf = modulate_norm(x, g[0], m[:, 0], m[:, 1])
        x = x + FFN_RES * m[:, 2, None] * rms_norm(swiglu(hf, ffn_w1[l, 0], ffn_w2[l, 0]), g[1])
        hm = modulate_norm(x, g[2], m[:, 3], m[:, 4])
        i = l // 2
        if l % 2 == 0:
            ctx = None if cache is None else (cache[0][:, i], cache[1][:, i])
            out, kv = even_mixer(hm, ev_w_in[i], ev_rpb[i], ev_conv_w[i], ev_conv_b[i], ev_w_out[i], ctx)
            if kv is not None:
                a_ks.append(kv[0])
                a_vs.append(kv[1])
        else:
            ctx = None if cache is None else (cache[2][:, i], cache[3][:, i])
            out, kv = odd_mixer(hm, od_w_in[i], od_pool_w[i], od_pool_scale[i], od_q_norm[i], od_k_norm[i],
                                od_w_out[i], ctx)
            if kv is not None:
                d_ks.append(kv[0])
                d_vs.append(kv[1])
        x = x + m[:, 5, None] * rms_norm(out, g[3])
        hf = modulate_norm(x, g[4], m[:, 6], m[:, 7])
        x = x + FFN_RES * m[:, 8, None] * rms_norm(swiglu(hf, ffn_w1[l, 1], ffn_w2[l, 1]), g[5])
    return x, (a_ks, a_vs, d_ks, d_vs)


def setup_inputs(seed: int = 0) -> dict:
    key = jax.random.key(seed)
    ks = jax.random.split(key, 32)
    D = D_MODEL

    def nrm(k, shape, s):
        return jax.random.normal(k, shape, jnp.float32) * s

    return {
        "x_prompt": nrm(ks[0], (BATCH, SEQ, D), 1.0),
        "x_sample": nrm(ks[1], (DEC_BATCH, DEC_SEQ, D), 1.0),
        "cache_a_k": nrm(ks[2], (DEC_BATCH, N_EVEN, H_A, PAST_LEN, HEAD_DIM), 1.0),
        "cache_a_v": nrm(ks[3], (DEC_BATCH, N_EVEN, H_A, PAST_LEN, HEAD_DIM), 1.0),
        "cache_d_k": nrm(ks[4], (DEC_BATCH, N_ODD, HKV_D, PAST_LEN, HEAD_DIM), 1.0),
        "cache_d_v": nrm(ks[5], (DEC_BATCH, N_ODD, HKV_D, PAST_LEN, HEAD_DIM), 1.0),
        "c": nrm(ks[6], (DEC_BATCH, D), 1.0),
        "c_ctx": nrm(ks[7], (D,), 1.0),
        "mod_w": nrm(ks[8], (DEPTH, D, N_MOD * D), 0.5 * D ** -0.5),
        "mod_b": nrm(ks[9], (DEPTH, N_MOD * D), 0.01),
        "norm_w": 1.0 + nrm(ks[10], (DEPTH, 6, D), 0.05),
        "ffn_w1": nrm(ks[11], (DEPTH, 2, D, 2 * D_FF), D ** -0.5),
        "ffn_w2": nrm(ks[12], (DEPTH, 2, D_FF, D), D_FF ** -0.5),
        "ev_w_in": nrm(ks[13], (N_EVEN, D, EVEN_IN), D ** -0.5),
        "ev_rpb": nrm(ks[14], (N_EVEN, H_A, 2 * NA_ROWS - 1, 2 * NA_COLS - 1), 0.1),
        "ev_conv_w": nrm(ks[15], (N_EVEN, CONV_W, C_B), CONV_W ** -0.5),
        "ev_conv_b": nrm(ks[16], (N_EVEN, C_B), 0.01),
        "ev_w_out": nrm(ks[17], (N_EVEN, EVEN_MIX, D), EVEN_MIX ** -0.5),
        "od_w_in": nrm(ks[18], (N_ODD, D, ODD_IN), D ** -0.5),
        "od_pool_w": nrm(ks[19], (N_ODD, N_POOL, C_POOL // N_POOL, C_POOL // N_POOL), (C_POOL // N_POOL) ** -0.5),
        "od_pool_scale": 1.0 + nrm(ks[20], (N_ODD, C_POOL), 0.05),
        "od_q_norm": 1.0 + nrm(ks[21], (N_ODD, HEAD_DIM), 0.05),
        "od_k_norm": 1.0 + nrm(ks[22], (N_ODD, HEAD_DIM), 0.05),
        "od_w_out": nrm(ks[23], (N_ODD, ODD_MIX, D), ODD_MIX ** -0.5),
    }


def reference(x_prompt, x_sample, cache_a_k, cache_a_v, cache_d_k, cache_d_v, c, c_ctx,
              mod_w, mod_b, norm_w, ffn_w1, ffn_w2,
              ev_w_in, ev_rpb, ev_conv_w, ev_conv_b, ev_w_out,
              od_w_in, od_pool_w, od_pool_scale, od_q_norm, od_k_norm, od_w_out):
    weights = (mod_w, mod_b, norm_w, ffn_w1, ffn_w2,
               ev_w_in, ev_rpb, ev_conv_w, ev_conv_b, ev_w_out,
               od_w_in, od_pool_w, od_pool_scale, od_q_norm, od_k_norm, od_w_out)
    y_prompt, (a_ks, a_vs, d_ks, d_vs) = trunk(x_prompt, c_ctx[None, :], None, *weights)
    state_a_k = jnp.stack(a_ks, axis=1)
    state_a_v = jnp.stack(a_vs, axis=1)
    state_d_k = jnp.stack(d_ks, axis=1)
    state_d_v = jnp.stack(d_vs, axis=1)
    y_sample, _ = trunk(x_sample, c, (cache_a_k, cache_a_v, cache_d_k, cache_d_v), *weights)
    return (y_prompt, y_sample, state_a_k, state_a_v, state_d_k, state_d_v)
```

```python
import numpy as np
import concourse.bass as bass
import concourse.mybir as mybir
from concourse.bass_utils import run_bass_kernel_spmd

F32 = mybir.dt.float32
F32R = mybir.dt.float32r
ALU = mybir.AluOpType
AF = mybir.ActivationFunctionType

D = 1024
DFF = 2816
NEG = -1e30
EPS = 1e-6
NB = 32
N_PROMPT_PASS = 2
DO_SAMPLE = True
N_CORES = 8


class Prog:
    def __init__(self, nc):
        self.nc = nc
        self.ops = {e: [] for e in ("pe", "act", "dve", "pool", "sp")}
        self.sems = {}
        self.cnt = {}
        self.known = {e: {} for e in self.ops}
        self.last_w = {}
        self.readers = {}
        self.pe_pending = False

    def _sem(self, key):
        if key not in self.sems:
            self.sems[key] = self.nc.alloc_semaphore("s_" + str(key))
            self.cnt[key] = 0
        return self.sems[key]

    def _deps(self, eng, reads, writes):
        need = {}

        def add(st):
            if st is None:
                return
            k, v = st
            if need.get(k, 0) < v:
                need[k] = v
        for r in reads:
            add(self.last_w.get(r))
        for w in writes:
            add(self.last_w.get(w))
            for st in self.readers.get(w, ()):
                add(st)
        waits = []
        for k, v in need.items():
            if k == "pe" and eng == "pe":
                continue
            if self.known[eng].get(k, 0) >= v:
                continue
            self.known[eng][k] = v
            waits.append((k, v))
        return waits

    def _record(self, stamp, reads, writes):
        for r in reads:
            self.readers.setdefault(r, []).append(stamp)
        for w in writes:
            self.last_w[w] = stamp
            self.readers[w] = []

    def op(self, eng, fn, reads=(), writes=(), signal=True):
        waits = self._deps(eng, reads, writes)
        self._sem(eng)
        if signal:
            self.cnt[eng] += 1
            stamp = (eng, self.cnt[eng])
            inc = (eng, 1)
            if eng == "pe":
                self.pe_pending = False
        else:
            assert eng == "pe"
            stamp = (eng, self.cnt[eng] + 1)
            inc = None
            self.pe_pending = True
        self.ops[eng].append((waits, fn, inc))
        self._record(stamp, reads, writes)

    def dma(self, q, semkey, out, in_, reads=(), writes=(), **kw):
        if semkey is None:
            semkey = ("d",) + tuple(writes[0]) if isinstance(writes[0], tuple) else ("d", writes[0])
        waits = self._deps(q, reads, writes)
        self._sem(semkey)
        self.cnt[semkey] += 16
        stamp = (semkey, self.cnt[semkey])
        self.ops[q].append((waits, lambda e: e.dma_start(out=out, in_=in_, **kw), (semkey, 16)))
        self._record(stamp, reads, writes)

    def final_wait(self, eng, keys):
        waits = self._deps(eng, keys, ())
        self.ops[eng].append((waits, None, None))

    def emit(self):
        nc = self.nc
        assert not self.pe_pending
        with nc.Block() as blk:
            def run(name):
                def _f(e):
                    for waits, fn, inc in self.ops[name]:
                        for k, v in waits:
                            e.wait_ge(self.sems[k], v)
                        if fn is None:
                            continue
                        ins = fn(e)
                        if inc is not None:
                            ins.then_inc(self.sems[inc[0]], inc[1])
                return _f
            blk.tensor(run("pe"))
            blk.scalar(run("act"))
            blk.vector(run("dve"))
            blk.gpsimd(run("pool"))
            blk.sync(run("sp"))


def L(method, **kw):
    return lambda e: getattr(e, method)(**kw)


def na_ktiles(qt):
    lo = min(max(8 * qt - 4, 0), 24)
    hi = min(max(8 * qt + 3, 0), 24) + 7
    return list(range(lo // 2, hi // 2 + 1))


def build_program():
    nc = bass.Bass("TRN2", target_bir_lowering=False)
    nc.dge_precook = False
    P = Prog(nc)

    def din(name, shape, dt=F32):
        return nc.dram_tensor(name, list(shape), dt, kind="ExternalInput").ap()

    def dout(name, shape):
        return nc.dram_tensor(name, list(shape), F32, kind="ExternalOutput").ap()

    def dint(name, shape, dt=F32):
        return nc.dram_tensor(name, list(shape), dt).ap()

    xp = din("xp", [1024, D], F32R)
    xs = din("xs", [2048, D], F32R)
    ca_k = din("ca_k", [8, 256, 64], F32R)
    ca_v = din("ca_v", [8, 256, 64], F32R)
    cd_k = din("cd_k", [2, 256, 64], F32R)
    cd_v = din("cd_v", [2, 256, 64], F32R)
    mod_w = din("mod_w", [2, D, 9 * D], F32R)
    w1_d = din("ffn_w1", [2, 2, D, 2 * DFF], F32R)
    w2_d = din("ffn_w2", [2, 2, DFF, D], F32R)
    ev_w_in = din("ev_w_in", [D, 3072], F32R)
    ev_w_out = din("ev_w_out", [D, D], F32R)
    od_w_in = din("od_w_in", [D, 1280], F32R)
    od_w_out = din("od_w_out", [D, D], F32R)
    pool_w_d = din("pool_w", [128, 4, 128], F32R)
    cst_d = din("cst", [128, 4, 128], F32R)
    identf_d = din("identf", [128, 128])
    small_d = din("small", [128, 400])
    gk_row_d = din("gk_row", [128, 64])
    nab_d = din("nab", [8, 4, 8, 128, 512], F32R)
    cos_d = din("cosT", [128, 2048], F32R)
    sin_d = din("sinT", [128, 2048], F32R)
    invp_d = din("invp", [4, 128, 512], F32R)
    invs_d = din("invs", [4, 128, 2048], F32R)

    yp = dout("yp", [1024, D])
    ys = dout("ys", [2048, D])
    sak = dout("sak", [4, 8, 256, 64])
    sav = dout("sav", [4, 8, 256, 64])
    sdk = dout("sdk", [4, 2, 256, 64])
    sdv = dout("sdv", [4, 2, 256, 64])

    LP = 2048 + 32
    q_s = dint("q_s", [8, 128, 2048], F32R)
    k_s = dint("k_s", [512, 2048], F32R)
    v_s = dint("v_s", [2048, 512], F32R)
    z_s = dint("z_s", [512, LP], F32R)
    g_s = dint("g_s", [512, 2048], F32R)

    X = nc.alloc_sbuf_tensor("X", [128, 8, 2048], F32)
    Hb = nc.alloc_sbuf_tensor("Hb", [128, 8, 512], F32R)
    WS = nc.alloc_sbuf_tensor("WS", [128, 3, 4096], F32R)
    BIG = nc.alloc_sbuf_tensor("BIG", [128, NB, 512], F32)
    CST = nc.alloc_sbuf_tensor("CST", [128, 4, 128], F32R)
    IDF = nc.alloc_sbuf_tensor("IDF", [128, 128], F32)
    SM = nc.alloc_sbuf_tensor("SM", [128, 400], F32)
    GKR = nc.alloc_sbuf_tensor("GKR", [128, 64], F32)
    PW = nc.alloc_sbuf_tensor("PW", [128, 4, 128], F32R)
    SC = nc.alloc_sbuf_tensor("SC", [128, 16], F32R)
    MODT = nc.alloc_sbuf_tensor("MODT", [128, 2, 72, 2], F32)
    COEF = nc.alloc_sbuf_tensor("COEF", [128, 2, 9, 8, 2], F32)
    TMPS = nc.alloc_sbuf_tensor("TMPS", [128, 8], F32)
    RS = nc.alloc_sbuf_tensor("RS", [128, 2, 512], F32)
    ZT = nc.alloc_sbuf_tensor("ZT", [128, 520], F32)
    ps = [nc.alloc_psum_tensor(f"ps{i}", [128, 512], F32) for i in range(8)]

    IDR = CST[:, 0, :]
    ONES = CST[:, 1, :]
    ONES64 = CST[:, 2, :]
    PSWAP = CST[:, 3, :]
    Hf = Hb[:].bitcast(F32)
    KC = ["CONST"]

    NORMW, MODB, CONVW, CONVB, PSCALE, QN, KN, CONDC = 0, 96, 240, 252, 256, 260, 261, 262
    EPS64C, EPSC = 398, 399

    def sm(col, n=1):
        return SM[:, col:col + n]

    def bf(i):
        return BIG[:, i, :]

    def br(i):
        return BIG[:, i, :].bitcast(F32R)

    def bfs(i, n):
        return BIG[:, i:i + n, :]

    def brs(i, n):
        return BIG[:, i:i + n, :].bitcast(F32R)

    def flat(ap3):
        return ap3.rearrange("p a b -> p (a b)")

    def bk(i, n=1):
        return [("B", j) for j in range(i, i + n)]

    bank_ctr = [0]

    def nb():
        b = bank_ctr[0] % 4
        bank_ctr[0] += 1
        return b

    ws_ctr = [0]

    def wstage():
        s = ws_ctr[0] % 3
        ws_ctr[0] += 1
        return s

    def mm(bank, cols, lhsT, rhs, start, stop, reads, signal=None):
        if signal is None:
            signal = stop
        P.op("pe", L("matmul", out=ps[bank][:, cols[0]:cols[1]], lhsT=lhsT, rhs=rhs, start=start, stop=stop),
             reads=reads, writes=[("ps", bank)], signal=signal)

    def tr(bank, cols, in_, reads, signal):
        P.op("pe", L("transpose", out=ps[bank][:, cols[0]:cols[1]], in_=in_, identity=IDF[:]),
             reads=reads + KC, writes=[("ps", bank)], signal=signal)

    evac_ctr = [0]

    def evac(out, bank, cols, writes, scale=None, eng=None, rows=(0, 128)):
        src = ps[bank][rows[0]:rows[1], cols[0]:cols[1]]
        if eng is None:
            eng = "act" if evac_ctr[0] % 2 == 0 else "dve"
            evac_ctr[0] += 1
        if eng == "act":
            kw = dict(out=out, in_=src, func=AF.Copy)
            if scale is not None:
                kw["scale"] = scale
            P.op("act", L("activation", **kw), reads=[("ps", bank)], writes=writes)
        else:
            if scale is None:
                P.op("dve", L("tensor_copy", out=out, in_=src), reads=[("ps", bank)], writes=writes)
            else:
                P.op("dve", L("tensor_scalar", out=out, in0=src, scalar1=scale, scalar2=None, op0=ALU.mult),
                     reads=[("ps", bank)], writes=writes)

    def dve(method, reads, writes, **kw):
        P.op("dve", L(method, **kw), reads=reads, writes=writes)

    def act(reads, writes, **kw):
        P.op("act", L("activation", **kw), reads=reads, writes=writes)

    def ld(out, in_, reads, writes):
        P.dma("pool", None, out, in_, reads=reads, writes=writes)

    P.dma("pool", "c0", CST[:], cst_d, writes=KC)
    P.dma("pool", "c0", IDF[:], identf_d, writes=KC)
    P.dma("pool", "c0", SM[:], small_d, writes=KC)
    P.dma("pool", "c0", GKR[:], gk_row_d, writes=KC)
    P.dma("pool", "c0", PW[:], pool_w_d, writes=KC)

    P.op("pool", L("memset", ap=ZT[:], constant=0.0), writes=["ZT"])

    def zero_z():
        for r in range(4):
            for q4 in range(4):
                P.dma("pool", "zs", z_s[r * 128:(r + 1) * 128, q4 * 520:(q4 + 1) * 520], ZT[:].bitcast(F32R),
                      reads=["ZT"], writes=["ZS"])

    def zero_half(blk_id, r0):
        dve("tensor_copy", ["ZT"], bk(blk_id), out=br(blk_id)[r0:r0 + 64, :], in_=ZT[r0:r0 + 64, 0:512])

    act(KC, ["SC"], out=SC[:], in_=sm(CONDC, 16), func=AF.Silu)
    SC3 = SC[:].rearrange("p (k n) -> p k n", n=2)
    def mod_layer(l):
        scp = flat(brs(0, 2)).rearrange("p (k m) -> p k m", k=8)
        for i in range(2):
            dve("tensor_copy", ["ZT"], bk(i), out=br(i), in_=ZT[:, 0:512])
        dve("tensor_copy", ["SC"], bk(0, 2), out=scp[:, :, 0:2], in_=SC[:].bitcast(F32).rearrange("p (k n) -> p k n", n=2))
        for nblk in range(18):
            st = wstage()
            wv = WS[:, st, :].rearrange("p (k n) -> p k n", k=8)
            P.dma("sp", ("ws", st), wv, mod_w[l, :, nblk * 512:(nblk + 1) * 512].rearrange("(k p) n -> p k n", p=128),
                  writes=[("W", st)])
            b = nb()
            for kc in range(8):
                mm(b, (0, 512), scp[:, kc, :], wv[:, kc, :], kc == 0, kc == 7, reads=[("W", st)] + bk(0, 2))
            slot = nblk % 2
            rkey = ["RSTD", "RDEN"][slot]
            dve("tensor_copy", [("ps", b)], [rkey], out=RS[0:2, slot, :], in_=ps[b][0:2, :])
            b2 = nb()
            for mi in range(4):
                P.op("pe", L("transpose", out=ps[b2][:, mi * 2:mi * 2 + 2], in_=RS[0:2, slot, mi * 128:(mi + 1) * 128],
                             identity=IDF[0:2, 0:2]), reads=[rkey] + KC, writes=[("ps", b2)], signal=(mi == 3))
            for mi in range(4):
                j = nblk * 4 + mi
                dve("tensor_scalar", [("ps", b2)] + KC, [f"MODT{l}"], out=MODT[:, l, j, :], in0=ps[b2][:, mi * 2:mi * 2 + 2],
                    scalar1=sm(MODB + l * 72 + j), scalar2=None, op0=ALU.add)
        for ci in range(2):
            for sub in range(3):
                sh, scl, gt = 3 * sub, 3 * sub + 1, 3 * sub + 2
                gpre, gpost = 2 * sub, 2 * sub + 1
                half = 0.5 if sub != 1 else 1.0
                gw_pre = sm(NORMW + l * 48 + gpre * 8, 8)
                gw_post = sm(NORMW + l * 48 + gpost * 8, 8)
                dve("scalar_tensor_tensor", [f"MODT{l}"] + KC, [f"COEF{l}"], out=COEF[:, l, 3 * sub, :, ci],
                    in0=MODT[:, l, scl * 8:(scl + 1) * 8, ci], scalar=1.0, in1=gw_pre, op0=ALU.add, op1=ALU.mult)
                dve("tensor_copy", [f"MODT{l}"], [f"COEF{l}"], out=COEF[:, l, 3 * sub + 1, :, ci],
                    in_=MODT[:, l, sh * 8:(sh + 1) * 8, ci])
                dve("scalar_tensor_tensor", [f"MODT{l}"] + KC, [f"COEF{l}"], out=COEF[:, l, 3 * sub + 2, :, ci],
                    in0=MODT[:, l, gt * 8:(gt + 1) * 8, ci], scalar=half, in1=gw_post, op0=ALU.mult, op1=ALU.mult)

    mod_layer(0)

    def coef(l, k, c, ci):
        return COEF[:, l, k, c, ci:ci + 1]

    SQ0 = 24
    RSTD = 32

    def Xc(c, col0):
        return X[:, c, col0:col0 + 512]

    def stats(src, skey, nch=8, sq0=None, presq=False):
        sq0 = SQ0 if sq0 is None else sq0
        if not presq:
            for c in range(nch):
                act([skey(c)], bk(sq0 + c), out=br(sq0 + c), in_=src(c), func=AF.Square)
        b = nb()
        for c in range(nch):
            mm(b, (0, 512), ONES, br(sq0 + c), c == 0, c == nch - 1, reads=bk(sq0 + c) + KC)
        act([("ps", b)] + KC, ["RSTD"], out=RS[:, 0, :], in_=ps[b][:], func=AF.Sqrt, scale=1.0 / 1024.0, bias=sm(EPSC))
        dve("reciprocal", ["RSTD"], ["RSTD"], out=RS[:, 0, :], in_=RS[:, 0, :])

    def pre_norm(l, sub, col0, ci):
        stats(lambda c: Xc(c, col0), lambda c: ("X", c, col0))
        for c in range(8):
            dve("tensor_tensor", [("X", c, col0)] + ["RSTD"], bk(SQ0 + c), out=br(SQ0 + c), in0=Xc(c, col0),
                in1=RS[:, 0, :], op=ALU.mult)
            act(bk(SQ0 + c) + [f"COEF{l}"], [("H", c)], out=Hb[:, c, :], in_=bf(SQ0 + c), func=AF.Identity,
                scale=coef(l, 3 * sub, c, ci), bias=coef(l, 3 * sub + 1, c, ci))

    def post_norm(l, sub, col0, ci, sq0=None):
        stats(lambda c: Hf[:, c, :], lambda c: ("H", c), sq0=sq0, presq=(sq0 is not None))
        for c in range(8):
            dve("tensor_tensor", [("H", c), "RSTD"], bk(SQ0 + c), out=br(SQ0 + c), in0=Hf[:, c, :], in1=RS[:, 0, :],
                op=ALU.mult)
            dve("scalar_tensor_tensor", bk(SQ0 + c) + [f"COEF{l}", ("X", c, col0)], [("X", c, col0)], out=Xc(c, col0),
                in0=bf(SQ0 + c), scalar=coef(l, 3 * sub + 2, c, ci), in1=Xc(c, col0), op0=ALU.mult, op1=ALU.add)

    def evac_sq(m, bank, sqblk):
        P.op("dve", L("tensor_copy", out=Hb[:, m, :], in_=ps[bank][:]), reads=[("ps", bank)], writes=[("H", m)])
        act([("H", m)], bk(sqblk), out=br(sqblk), in_=Hf[:, m, :], func=AF.Square)

    def ffn(l, s, col0, ci):
        sub = 0 if s == 0 else 2
        pre_norm(l, sub, col0, ci)
        w1 = w1_d[l, s]
        w2 = w2_d[l, s]
        for jb in range(11):
            st = wstage()
            wv = WS[:, st, :].rearrange("p (k n) -> p k n", k=8)
            P.dma("sp", ("ws", st), wv[:, :, 0:256], w1[:, jb * 256:(jb + 1) * 256].rearrange("(k p) n -> p k n", p=128),
                  writes=[("W", st)])
            P.dma("sp", ("ws", st), wv[:, :, 256:512],
                  w1[:, DFF + jb * 256:DFF + (jb + 1) * 256].rearrange("(k p) n -> p k n", p=128), writes=[("W", st)])
            for jj in range(2):
                j = jb * 2 + jj
                bg_, bu_ = nb(), nb()
                for kc in range(8):
                    mm(bg_, (0, 512), wv[:, kc, jj * 128:(jj + 1) * 128], Hb[:, kc, :], kc == 0, kc == 7,
                       reads=[("W", st), ("H", kc)])
                for kc in range(8):
                    mm(bu_, (0, 512), wv[:, kc, 256 + jj * 128:256 + (jj + 1) * 128], Hb[:, kc, :], kc == 0, kc == 7,
                       reads=[("W", st), ("H", kc)])
                tb = 22 + (j % 2)
                act([("ps", bg_)], bk(tb), out=br(tb), in_=ps[bg_][:], func=AF.Silu)
                dve("tensor_tensor", [("ps", bu_)] + bk(tb), bk(j), out=br(j), in0=bf(tb), in1=ps[bu_][:], op=ALU.mult)
        for mb in range(8):
            st = wstage()
            wv = WS[:, st, 0:2816].rearrange("p (j n) -> p j n", j=22)
            P.dma("sp", ("ws", st), wv, w2[:, mb * 128:(mb + 1) * 128].rearrange("(j p) n -> p j n", p=128),
                  writes=[("W", st)])
            b = nb()
            for j in range(22):
                mm(b, (0, 512), wv[:, j, :], br(j), j == 0, j == 21, reads=[("W", st)] + bk(j))
            evac(Hb[:, mb, :], b, (0, 512), [("H", mb)])
        post_norm(l, sub, col0, ci)

    def stg_view(r=False):
        v = BIG[:, 0:8, :]
        if r:
            v = v.bitcast(F32R)
        return v.rearrange("p (t h) n -> p t (h n)", t=4)

    def load_x(src, r0, col0):
        stg = stg_view()
        P.dma("pool", "xin", stg_view(True), src[r0:r0 + 512, :].rearrange("(t p) d -> p t d", p=128), writes=bk(0, 8))
        for c in range(8):
            b = nb()
            for tt in range(4):
                tr(b, (tt * 128, (tt + 1) * 128), stg[:, tt, c * 128:(c + 1) * 128], bk(2 * tt, 2), tt == 3)
            evac(Xc(c, col0), b, (0, 512), [("X", c, col0)])

    def store_y(dst, name, r0, col0):
        stg = stg_view()
        for tt in range(4):
            for hf in range(2):
                b = nb()
                for i in range(4):
                    c = hf * 4 + i
                    tr(b, (i * 128, (i + 1) * 128), X[:, c, col0 + tt * 128:col0 + (tt + 1) * 128], [("X", c, col0)], i == 3)
                evac(stg_view(True)[:, tt, hf * 512:(hf + 1) * 512], b, (0, 512), bk(2 * tt + hf))
        key = ("Y", name, r0)
        P.dma("pool", "yout", dst[r0:r0 + 512, :].rearrange("(t p) d -> p t d", p=128), stg, reads=bk(0, 8), writes=[key])
        return key

    def proj_fm(wv, col_lo, wkey):
        b = nb()
        for kc in range(8):
            mm(b, (0, 512), wv[:, kc, col_lo:col_lo + 128], Hb[:, kc, :], kc == 0, kc == 7, reads=[wkey, ("H", kc)])
        return b

    def proj_tm(wv, col_lo, ncol, tt, wkey):
        b = nb()
        for kc in range(8):
            mm(b, (0, ncol), Hb[:, kc, tt * 128:(tt + 1) * 128], wv[:, kc, col_lo:col_lo + ncol], kc == 0, kc == 7,
               reads=[wkey, ("H", kc)])
        return b

    def load_wblock(w, col_lo, ncol):
        st = wstage()
        wv = WS[:, st, 0:8 * ncol].rearrange("p (k n) -> p k n", k=8)
        P.dma("sp", ("ws", st), wv, w[:, col_lo:col_lo + ncol].rearrange("(k p) n -> p k n", p=128), writes=[("W", st)])
        return wv, ("W", st)

    bias_ctr = [0]
    acc_ctr = [0]

    def attention(jobs, mix_blk, pts=(18, 19), bias_blks=(20, 21, 22)):
        npt = len(pts)
        for ji, (qblk, half, q0, q1, ktiles) in enumerate(jobs):
            n = q1 - q0
            ob, db = (4, 5) if acc_ctr[0] % 2 == 0 else (6, 7)
            acc_ctr[0] += 1
            nk = len(ktiles)
            bias_blk = {}
            sbank = {}

            def prefetch(ki):
                if ki < nk and ktiles[ki][2] is not None:
                    bb = bias_blks[bias_ctr[0] % len(bias_blks)]
                    bias_ctr[0] += 1
                    ld(br(bb), ktiles[ki][2], [], bk(bb))
                    bias_blk[ki] = bb

            def score(ki):
                if ki >= nk:
                    return
                kT, vv, bias, rkeys = ktiles[ki]
                b = nb()
                sbank[ki] = b
                mm(b, (0, n), kT, br(qblk)[:, q0:q1], True, bias is None, reads=rkeys + bk(qblk))
                if bias is not None:
                    bb = bias_blk[ki]
                    mm(b, (0, n), IDR, br(bb)[:, 0:n], False, True, reads=bk(bb) + KC)
            prefetch(0)
            prefetch(1)
            score(0)
            score(1)
            for ki, (kT, vv, bias, rkeys) in enumerate(ktiles):
                prefetch(ki + 2)
                b = sbank[ki]
                pt = pts[ki % npt]
                act([("ps", b)], bk(pt), out=br(pt)[:, 0:n], in_=ps[b][:, 0:n], func=AF.Exp)
                score(ki + 2)
                mm(ob, (0, n), vv, br(pt)[:, 0:n], ki == 0, ki == nk - 1, reads=rkeys + bk(pt))
                mm(db, (0, n), ONES, br(pt)[:, 0:n], ki == 0, ki == nk - 1, reads=bk(pt) + KC, signal=True)
            r0, r1 = half * 64, half * 64 + 64
            dve("reciprocal", [("ps", db)], ["RDEN"], out=RS[r0:r1, 1, 0:n], in_=ps[db][r0:r1, 0:n])
            dve("tensor_tensor", [("ps", ob), "RDEN"], bk(mix_blk), out=br(mix_blk)[r0:r1, q0:q1],
                in0=ps[ob][r0:r1, 0:n], in1=RS[r0:r1, 1, 0:n], op=ALU.mult)

    def mixer_out(l, w_out, col0, ci):
        for nblk in range(2):
            wv, wk = load_wblock(w_out, nblk * 512, 512)
            for mi in range(4):
                m = nblk * 4 + mi
                b = nb()
                for kc in range(8):
                    mm(b, (0, 512), wv[:, kc, mi * 128:(mi + 1) * 128], br(24 + kc), kc == 0, kc == 7, reads=[wk] + bk(24 + kc))
                evac_sq(m, b, m)
        post_norm(l, 1, col0, ci, sq0=0)

    def scr_w(dst, src, reads, key):
        P.dma("pool", ("scr", key), dst, src, reads=reads, writes=[key])

    def even_loop1(l, kind, col0, ci, tok0, zoff, bl):
        pre_norm(l, 1, col0, ci)
        wv, wk = load_wblock(ev_w_in, 0, 512)
        for m in range(4):
            b = proj_fm(wv, m * 128, wk)
            evac(br(m), b, (0, 512), bk(m), scale=0.125)
        scr_w(q_s[0:4, :, tok0:tok0 + 512].rearrange("c p t -> p c t"), brs(0, 4), bk(0, 4), "QS")
        wv, wk = load_wblock(ev_w_in, 512, 512)
        for m in range(4):
            b = proj_fm(wv, m * 128, wk)
            evac(br(4 + m), b, (0, 512), bk(4 + m))
        scr_w(k_s[:, tok0:tok0 + 512].rearrange("(c p) t -> p c t", p=128), brs(4, 4), bk(4, 4), "KS")
        if kind == "p":
            for tt in range(4):
                b = proj_tm(wv, 0, 512, tt, wk)
                evac(br(8 + tt), b, (0, 512), bk(8 + tt))
            for tt in range(4):
                P.dma("pool", ("so", "ak", tt), sak[bl + tt // 2][:, (tt % 2) * 128:(tt % 2 + 1) * 128, :].rearrange("h p d -> p h d"),
                      bf(8 + tt).rearrange("p (h d) -> p h d", d=64), reads=bk(8 + tt), writes=[("SAK", bl, tt)])
        wv, wk = load_wblock(ev_w_in, 1024, 512)
        for tt in range(4):
            b = proj_tm(wv, 0, 512, tt, wk)
            evac(br(12 + tt), b, (0, 512), bk(12 + tt))
        scr_w(v_s[tok0:tok0 + 512, :].rearrange("(t p) f -> p t f", p=128), brs(12, 4), bk(12, 4), "VS")
        if kind == "p":
            for tt in range(4):
                P.dma("pool", ("so", "av", tt), sav[bl + tt // 2][:, (tt % 2) * 128:(tt % 2 + 1) * 128, :].rearrange("h p d -> p h d"),
                      bf(12 + tt).rearrange("p (h d) -> p h d", d=64), reads=bk(12 + tt), writes=[("SAV", bl, tt)])
        wv, wk = load_wblock(ev_w_in, 1536, 512)
        for m in range(4):
            b = proj_fm(wv, m * 128, wk)
            evac(br(16 + m), b, (0, 512), bk(16 + m))
        scr_w(g_s[:, tok0:tok0 + 512].rearrange("(c p) t -> p c t", p=128), brs(16, 4), bk(16, 4), "GS")
        wv, wk = load_wblock(ev_w_in, 2048, 512)
        for m in range(4):
            b = proj_fm(wv, m * 128, wk)
            evac(br(8 + m), b, (0, 512), bk(8 + m))
        wv, wk = load_wblock(ev_w_in, 2560, 512)
        for m in range(4):
            b = proj_fm(wv, m * 128, wk)
            dve("tensor_tensor", [("ps", b)] + bk(8 + m), bk(8 + m), out=br(8 + m), in0=bf(8 + m), in1=ps[b][:], op=ALU.mult)
        for (seg_tok, seg_len, zcol) in zoff:
            scr_w(z_s[:, zcol:zcol + seg_len].rearrange("(c p) t -> p c t", p=128),
                  BIG[:, 8:12, seg_tok:seg_tok + seg_len].bitcast(F32R), bk(8, 4), "ZS")

    def even_loop2(l, kind, col0, ci, tok0, zoff, qt):
        kts_q = na_ktiles(qt)
        klo, nkt = kts_q[0], len(kts_q)

        def conv_chunk(c):
            ld(br(15), g_s[c * 128:(c + 1) * 128, tok0:tok0 + 512], ["GS"], bk(15))
            mixb = 28 + c
            for (seg_tok, seg_len, zcol) in zoff:
                zl = seg_len + 2
                ztr = flat(brs(16, 2))[:, 0:zl]
                zt = flat(bfs(16, 2))[:, 0:zl]
                zk = bk(16, 2)
                ld(ztr, z_s[c * 128:(c + 1) * 128, zcol - 1:zcol - 1 + zl], ["ZS"], zk)
                acc = bf(mixb)[:, seg_tok:seg_tok + seg_len]
                accr = br(mixb)[:, seg_tok:seg_tok + seg_len]
                dve("tensor_scalar", zk + KC, bk(mixb), out=accr, in0=zt[:, 0:seg_len], scalar1=sm(CONVW + c), scalar2=None,
                    op0=ALU.mult)
                for jx in (1, 2):
                    dve("scalar_tensor_tensor", zk + KC + bk(mixb), bk(mixb), out=accr, in0=zt[:, jx:jx + seg_len],
                        scalar=sm(CONVW + jx * 4 + c), in1=acc, op0=ALU.mult, op1=ALU.add)
            dve("scalar_tensor_tensor", bk(mixb) + bk(15) + KC, bk(mixb), out=br(mixb), in0=bf(mixb), scalar=sm(CONVB + c),
                in1=bf(15), op0=ALU.add, op1=ALU.mult)

        for qa_, qb_ in ((0, 1), (2, 3)):
            zero_half(qa_, 64)
            zero_half(qb_, 0)

        def chunk_loads(c):
            sset = c % 2
            qa, qb = (0, 1) if sset == 0 else (2, 3)
            kb0, vb0 = 4 + 2 * sset, 8 + 2 * sset
            cx0, cx1 = (12, 13) if sset == 0 else (14, 23)
            ld(br(qa)[0:64, :], q_s[c, 0:64, tok0:tok0 + 512], ["QS"], bk(qa))
            ld(br(qb)[64:128, :], q_s[c, 64:128, tok0:tok0 + 512], ["QS"], bk(qb))
            if kind == "p":
                ld(br(kb0), k_s[c * 128:(c + 1) * 128, tok0:tok0 + 512], ["KS"], bk(kb0))
                v8 = br(vb0).rearrange("p (t f) -> p t f", f=128)
                ld(v8, v_s[tok0:tok0 + 512, c * 128:(c + 1) * 128].rearrange("(t p) f -> p t f", p=128), ["VS"], bk(vb0))
            else:
                k2 = flat(brs(kb0, 2))[:, 0:nkt * 128]
                v2 = brs(vb0, 2).rearrange("p a (t f) -> p (a t) f", f=128)[:, 0:nkt, :]
                ld(k2, k_s[c * 128:(c + 1) * 128, klo * 128:(klo + nkt) * 128], ["KS"], bk(kb0, 2))
                ld(v2, v_s[klo * 128:(klo + nkt) * 128, c * 128:(c + 1) * 128].rearrange("(t p) f -> p t f", p=128), ["VS"],
                   bk(vb0, 2))
                for tt in range(2):
                    ld(br(cx0)[:, tt * 128:(tt + 1) * 128].rearrange("p (h d) -> p h d", h=2),
                       ca_k[2 * c:2 * c + 2, tt * 128:(tt + 1) * 128, :].rearrange("h p d -> p h d"), [], bk(cx0))
                    ld(br(cx1)[:, tt * 128:(tt + 1) * 128].rearrange("p (h d) -> p h d", h=2),
                       ca_v[2 * c:2 * c + 2, tt * 128:(tt + 1) * 128, :].rearrange("h p d -> p h d"), [], bk(cx1))

        chunk_loads(0)
        for c in range(4):
            if c + 1 < 4:
                chunk_loads(c + 1)
            sset = c % 2
            qa, qb = (0, 1) if sset == 0 else (2, 3)
            kb0, vb0 = 4 + 2 * sset, 8 + 2 * sset
            cx0, cx1 = (12, 13) if sset == 0 else (14, 23)
            jobs = []
            if kind == "p":
                v8 = br(vb0).rearrange("p (t f) -> p t f", f=128)
                for half in range(2):
                    for sg in range(2):
                        kts = []
                        for kt in range(2):
                            ktile = sg * 2 + kt
                            kts.append((br(kb0)[:, ktile * 128:(ktile + 1) * 128], v8[:, ktile, :], None, bk(kb0) + bk(vb0)))
                        jobs.append((qa if half == 0 else qb, half, sg * 256, (sg + 1) * 256, kts))
            else:
                k2 = flat(brs(kb0, 2))[:, 0:nkt * 128]
                v2 = brs(vb0, 2).rearrange("p a (t f) -> p (a t) f", f=128)[:, 0:nkt, :]
                b = nb()
                for tt in range(2):
                    tr(b, (tt * 128, (tt + 1) * 128), bf(cx0)[:, tt * 128:(tt + 1) * 128], bk(cx0), tt == 1)
                evac(br(cx1)[:, 256:512], b, (0, 256), bk(cx1))
                vc = br(cx1)[:, 0:256].rearrange("p (t f) -> p t f", f=128)
                for half in range(2):
                    h = 2 * c + half
                    kts = []
                    for j in range(nkt):
                        kts.append((k2[:, j * 128:(j + 1) * 128], v2[:, j, :], nab_d[h, qt, j], bk(kb0, 2) + bk(vb0, 2)))
                    for tt in range(2):
                        kts.append((br(cx1)[:, 256 + tt * 128:256 + (tt + 1) * 128], vc[:, tt, :], None, bk(cx1)))
                    jobs.append((qa if half == 0 else qb, half, 0, 512, kts))
            attention(jobs, 24 + c)
            conv_chunk(c)
        mixer_out(l, ev_w_out, col0, ci)

    def qk_norm_rope(blks, gcol, rope, qscale):
        nbk = len(blks)
        banks = []
        for i, bid in enumerate(blks):
            act(bk(bid), bk(SQ0 + i), out=br(SQ0 + i), in_=bf(bid), func=AF.Square)
        for i, bid in enumerate(blks):
            b = nb()
            banks.append(b)
            mm(b, (0, 512), ONES64, br(SQ0 + i), True, True, reads=bk(SQ0 + i) + KC)
            if qscale:
                act([("ps", b)] + KC, bk(28 + i), out=br(28 + i), in_=ps[b][:], func=AF.Sqrt, scale=1.0, bias=sm(EPS64C))
            else:
                act([("ps", b)] + KC, bk(28 + i), out=br(28 + i), in_=ps[b][:], func=AF.Sqrt, scale=1.0 / 64.0, bias=sm(EPSC))
        assert nbk <= 2
        rkeys = ["RSTD", "RDEN"][:nbk]
        dve("reciprocal", bk(28, nbk), rkeys, out=RS[:, 0:nbk, :], in_=bfs(28, nbk))
        for i, bid in enumerate(blks):
            if not rope:
                dve("scalar_tensor_tensor", bk(bid) + rkeys + KC, bk(bid), out=br(bid), in0=bf(bid), scalar=sm(gcol),
                    in1=RS[:, i, :], op0=ALU.mult, op1=ALU.mult)
                continue
            s1, t1, t2 = SQ0 + i, 15 + i, 19 + i
            dve("scalar_tensor_tensor", bk(bid) + rkeys + KC, bk(s1), out=br(s1), in0=bf(bid), scalar=sm(gcol),
                in1=RS[:, i, :], op0=ALU.mult, op1=ALU.mult)
            b = nb()
            mm(b, (0, 512), PSWAP, br(s1), True, True, reads=bk(s1) + KC)
            dve("tensor_tensor", bk(s1) + bk(12), bk(t1), out=br(t1), in0=bf(s1), in1=bf(12), op=ALU.mult)
            dve("tensor_tensor", [("ps", b)] + bk(13), bk(t2), out=br(t2), in0=ps[b][:], in1=bf(13), op=ALU.mult)
            dve("tensor_tensor", bk(t1) + bk(t2), bk(bid), out=br(bid), in0=bf(t1), in1=bf(t2), op=ALU.add)

    def odd_loop1(l, kind, col0, ci, tok0, zoff, bl):
        pre_norm(l, 1, col0, ci)
        rope = (kind == "s")
        if rope:
            ld(br(12), cos_d[:, tok0:tok0 + 512], [], bk(12))
            ld(br(13), sin_d[:, tok0:tok0 + 512], [], bk(13))
        wv, wk = load_wblock(od_w_in, 0, 512)
        for m in range(4):
            b = proj_fm(wv, m * 128, wk)
            evac(br(8 + m), b, (0, 512), bk(8 + m))
        for (seg_tok, seg_len, zcol) in zoff:
            scr_w(z_s[:, zcol:zcol + seg_len].rearrange("(c p) t -> p c t", p=128),
                  BIG[:, 8:12, seg_tok:seg_tok + seg_len].bitcast(F32R), bk(8, 4), "ZS")
        wv, wk = load_wblock(od_w_in, 512, 512)
        for m in range(4):
            b = proj_fm(wv, m * 128, wk)
            evac(br(m), b, (0, 512), bk(m))
        wv, wk = load_wblock(od_w_in, 1024, 256)
        b = proj_fm(wv, 0, wk)
        evac(br(4), b, (0, 512), bk(4))
        vdup = brs(5, 2).rearrange("p a (t f) -> p (a t) f", f=256)
        for tt in range(4):
            b = proj_tm(wv, 128, 128, tt, wk)
            veng = "act" if tt % 2 == 0 else "dve"
            for g in range(2):
                for dup in range(2):
                    evac(vdup[:, tt, (2 * g + dup) * 64:(2 * g + dup + 1) * 64], b, (g * 64, g * 64 + 64), bk(5, 2), eng=veng)
            if kind == "p":
                evac(br(7)[:, tt * 128:(tt + 1) * 128], b, (0, 128), bk(7), eng=veng)
        scr_w(v_s[tok0:tok0 + 512, 0:256].rearrange("(t p) f -> p t f", p=128), vdup, bk(5, 2), "VS")
        ktm_banks = []
        if kind == "p":
            for tt in range(4):
                ktm_banks.append(proj_tm(wv, 0, 128, tt, wk))
                b = ktm_banks[-1]
                for g in range(2):
                    act([("ps", b)], bk(15) + ["TMPS"], out=br(15)[:, g * 64:(g + 1) * 64], in_=ps[b][:, g * 64:(g + 1) * 64],
                        func=AF.Square, accum_out=TMPS[:, g:g + 1])
                act(["TMPS"] + KC, ["TMPS"], out=TMPS[:, 2:4], in_=TMPS[:, 0:2], func=AF.Sqrt, scale=1.0 / 64.0, bias=sm(EPSC))
                dve("reciprocal", ["TMPS"], ["TMPS"], out=TMPS[:, 4:6], in_=TMPS[:, 2:4])
                for g in range(2):
                    dve("scalar_tensor_tensor", [("ps", b), "TMPS"] + KC, bk(14),
                        out=br(14)[:, tt * 128 + g * 64:tt * 128 + (g + 1) * 64], in0=ps[b][:, g * 64:(g + 1) * 64],
                        scalar=TMPS[:, 4 + g:5 + g], in1=GKR[:], op0=ALU.mult, op1=ALU.mult)
        qk_norm_rope([0, 1], QN, rope, True)
        qk_norm_rope([2, 3], QN, rope, True)
        scr_w(q_s[0:4, :, tok0:tok0 + 512].rearrange("c p t -> p c t"), brs(0, 4), bk(0, 4), "QS")
        qk_norm_rope([4], KN, rope, False)
        scr_w(k_s[0:128, tok0:tok0 + 512], br(4), bk(4), "KS")
        if kind == "p":
            for tt in range(4):
                P.dma("pool", ("so", "dv", tt), sdv[bl + tt // 2][:, (tt % 2) * 128:(tt % 2 + 1) * 128, :].rearrange("g p d -> p g d"),
                      bf(7)[:, tt * 128:(tt + 1) * 128].rearrange("p (g d) -> p g d", g=2),
                      reads=bk(7), writes=[("SDV", bl, tt)])
            for tt in range(4):
                P.dma("pool", ("so", "dk", tt), sdk[bl + tt // 2][:, (tt % 2) * 128:(tt % 2 + 1) * 128, :].rearrange("g p d -> p g d"),
                      bf(14)[:, tt * 128:(tt + 1) * 128].rearrange("p (g d) -> p g d", g=2),
                      reads=bk(14), writes=[("SDK", bl, tt)])

    def odd_loop2(l, kind, col0, ci, tok0, zoff):
        inv_src = (invp_d.rearrange("g p t -> p g t") if kind == "p" else invs_d[:, :, tok0:tok0 + 512].rearrange("g p t -> p g t"))
        ld(brs(19, 4), inv_src, [], bk(19, 4))
        for (seg_tok, seg_len, zcol) in zoff:
            Ls = seg_len
            Lp = Ls + 16

            def v3(b0, nblk, ng, r=False):
                base = flat(brs(b0, nblk)) if r else flat(bfs(b0, nblk))
                return base[:, 0:ng * Lp].rearrange("p (g t) -> p g t", g=ng)
            ku, ka, kb, kc_, kd = bk(0, 5), bk(5, 5), bk(10, 4), bk(15, 3), bk(18)
            ld(v3(0, 5, 4, True), z_s[:, zcol - 8:zcol + Ls + 8].rearrange("(g p) t -> p g t", p=128), ["ZS"], ku)
            U, A, B, C = v3(0, 5, 4), v3(5, 5, 4), v3(10, 4, 3), v3(15, 3, 2)
            Ar, Br, Cr = v3(5, 5, 4, True), v3(10, 4, 3, True), v3(15, 3, 2, True)
            dve("tensor_tensor", ku, ka, out=Ar[:, :, 0:Ls + 15], in0=U[:, :, 0:Ls + 15], in1=U[:, :, 1:Ls + 16], op=ALU.add)
            dve("tensor_tensor", ka, kb, out=Br[:, :, 0:Ls + 13], in0=A[:, 1:4, 0:Ls + 13], in1=A[:, 1:4, 2:Ls + 15], op=ALU.add)
            dve("tensor_tensor", kb, kc_, out=Cr[:, :, 0:Ls + 9], in0=B[:, 1:3, 0:Ls + 9], in1=B[:, 1:3, 4:Ls + 13], op=ALU.add)
            dve("tensor_tensor", kc_, kd, out=br(18)[:, 0:Ls], in0=C[:, 1, 0:Ls], in1=C[:, 1, 8:8 + Ls], op=ALU.add)
            ssrc = [(A[:, 0, 7:7 + Ls], ka), (B[:, 0, 6:6 + Ls], kb), (C[:, 0, 4:4 + Ls], kc_), (bf(18)[:, 0:Ls], kd)]
            for gi in range(4):
                sv, skey = ssrc[gi]
                pb = 23 if gi % 2 == 0 else 14
                tmpb = 14 if gi % 2 == 0 else 23
                dve("tensor_tensor", skey + bk(19 + gi), bk(24 + gi), out=br(24 + gi)[:, seg_tok:seg_tok + Ls], in0=sv,
                    in1=bf(19 + gi)[:, seg_tok:seg_tok + Ls], op=ALU.mult)
                dve("tensor_tensor", bk(24 + gi) + ku, bk(28 + gi), out=br(28 + gi)[:, seg_tok:seg_tok + Ls],
                    in0=bf(24 + gi)[:, seg_tok:seg_tok + Ls], in1=U[:, gi, 8:8 + Ls], op=ALU.subtract)
        for gi in range(4):
            b = nb()
            mm(b, (0, 512), PW[:, gi, :], br(28 + gi), True, True, reads=bk(28 + gi) + KC)
            dve("tensor_scalar", [("ps", b)] + KC, bk(24 + gi), out=br(24 + gi), in0=ps[b][:], scalar1=sm(PSCALE + gi),
                scalar2=None, op0=ALU.mult)
        v8 = brs(8, 8).rearrange("p a (t f) -> p (a t) f", f=256)
        if kind == "p":
            ld(br(4), k_s[0:128, tok0:tok0 + 512], ["KS"], bk(4))
            ld(v8[:, 0:4, :], v_s[tok0:tok0 + 512, 0:256].rearrange("(t p) f -> p t f", p=128), ["VS"], bk(8, 2))
            k4 = br(4)
        else:
            k4 = flat(brs(4, 4))
            ld(k4, k_s[0:128, 0:2048], ["KS"], bk(4, 4))
            ld(v8, v_s[0:2048, 0:256].rearrange("(t p) f -> p t f", p=128), ["VS"], bk(8, 8))
            for tt in range(2):
                ld(br(16)[:, tt * 128:(tt + 1) * 128].rearrange("p (g d) -> p g d", g=2),
                   cd_k[:, tt * 128:(tt + 1) * 128, :].rearrange("g p d -> p g d"), [], bk(16))
            vc = br(17).rearrange("p (t f) -> p t f", f=256)
            for g in range(2):
                for dup in range(2):
                    P.dma("pool", ("d", "B", 17), vc[:, :, (2 * g + dup) * 64:(2 * g + dup + 1) * 64],
                          cd_v[g].rearrange("(t p) d -> p t d", p=128), writes=bk(17))
            b = nb()
            for tt in range(2):
                tr(b, (tt * 128, (tt + 1) * 128), bf(16)[:, tt * 128:(tt + 1) * 128], bk(16), tt == 1)
            evac(br(16)[:, 256:512], b, (0, 256), bk(16))
        for qblk_ in (0, 1):
            zero_half(qblk_, 64)
        for qblk_ in (2, 3):
            zero_half(qblk_, 0)
        for h in range(8):
            g = h // 4
            half = h % 2
            c = h // 2
            qb_ = 2 * g + (h % 2)
            ld(br(qb_)[g * 64:(g + 1) * 64, :], q_s[h // 2, (h % 2) * 64:(h % 2) * 64 + 64, tok0:tok0 + 512], ["QS"], bk(qb_))
            vcol = 2 * g * 64
            jobs = []
            if kind == "p":
                for sg in range(2):
                    kts = []
                    for kt in range(2):
                        ktile = sg * 2 + kt
                        kts.append((k4[:, ktile * 128:(ktile + 1) * 128], v8[:, ktile, vcol:vcol + 128], None, bk(4) + bk(8, 2)))
                    jobs.append((qb_, half, sg * 256, (sg + 1) * 256, kts))
            else:
                kts = []
                for kt in range(16):
                    kts.append((k4[:, kt * 128:(kt + 1) * 128], v8[:, kt, vcol:vcol + 128], None, bk(4, 4) + bk(8, 8)))
                for tt in range(2):
                    kts.append((br(16)[:, 256 + tt * 128:256 + (tt + 1) * 128], vc[:, tt, vcol:vcol + 128], None, bk(16) + bk(17)))
                jobs.append((qb_, half, 0, 512, kts))
            attention(jobs, 28 + c, pts=(18, 19, 20))
        mixer_out(l, od_w_out, col0, ci)

    out_keys = []

    def zoff_for(kind, tile_idx, pad):
        if kind == "p":
            return [(0, 256, pad), (256, 256, 3 * pad + 256)]
        return [(0, 512, pad + tile_idx * 512)]

    for pp in range(N_PROMPT_PASS):
        bl = pp * 2
        load_x(xp, pp * 512, 0)
        ffn(0, 0, 0, 0)
        if pp == 0:
            mod_layer(1)
        zero_z()
        zo = zoff_for("p", 0, 1)
        even_loop1(0, "p", 0, 0, 0, zo, bl)
        even_loop2(0, "p", 0, 0, 0, zo, 0)
        ffn(0, 1, 0, 0)
        ffn(1, 0, 0, 0)
        zero_z()
        zo = zoff_for("p", 0, 8)
        odd_loop1(1, "p", 0, 0, 0, zo, bl)
        odd_loop2(1, "p", 0, 0, 0, zo)
        ffn(1, 1, 0, 0)
        out_keys.append(store_y(yp, "yp", pp * 512, 0))
        out_keys += [(nm, bl, tt) for nm in ("SAK", "SAV", "SDK", "SDV") for tt in range(4)]
    if DO_SAMPLE:
        for t in range(4):
            load_x(xs, t * 512, t * 512)
        zero_z()
        for t in range(4):
            ffn(0, 0, t * 512, 1)
            even_loop1(0, "s", t * 512, 1, t * 512, zoff_for("s", t, 1), 0)
        for t in range(4):
            even_loop2(0, "s", t * 512, 1, t * 512, zoff_for("s", t, 1), t)
            ffn(0, 1, t * 512, 1)
        zero_z()
        for t in range(4):
            ffn(1, 0, t * 512, 1)
            odd_loop1(1, "s", t * 512, 1, t * 512, zoff_for("s", t, 8), 0)
        for t in range(4):
            odd_loop2(1, "s", t * 512, 1, t * 512, zoff_for("s", t, 8))
            if t > 0:
                out_keys.append(store_y(ys, "ys", (t - 1) * 512, (t - 1) * 512))
            ffn(1, 1, t * 512, 1)
        out_keys.append(store_y(ys, "ys", 3 * 512, 3 * 512))
    P.final_wait("pool", out_keys)
    P.emit()
    return nc


def _na_bias(rpb):
    H = rpb.shape[0]
    r = np.arange(32)
    rs = np.clip(r - 4, 0, 24)
    cq = np.arange(64)
    cs = np.clip(cq - 8, 0, 48)
    out = np.full((H, 4, 8, 128, 512), NEG, np.float32)
    for qt in range(4):
        kts = na_ktiles(qt)
        for j, kt in enumerate(kts):
            kr = (2 * kt + np.arange(2))[:, None, None, None]
            kc = np.arange(64)[None, :, None, None]
            qr = (8 * qt + np.arange(8))[None, None, :, None]
            qc = np.arange(64)[None, None, None, :]
            valid = (kr >= rs[qr]) & (kr < rs[qr] + 8) & (kc >= cs[qc]) & (kc < cs[qc] + 16)
            ro = np.clip(kr - qr + 7, 0, 14)
            co = np.clip(kc - qc + 15, 0, 30)
            ro_b, co_b, valid_b = np.broadcast_arrays(ro, co, valid)
            for h in range(H):
                vals = rpb[h][ro_b, co_b]
                tile = np.where(valid_b, vals, np.float32(NEG)).astype(np.float32)
                out[h, qt, j] = tile.reshape(128, 512)
    return out


def _consts():
    ident = np.eye(128, dtype=np.float32)
    ones = np.ones((128, 128), np.float32)
    o64 = np.zeros((128, 128), np.float32)
    o64[:64, :64] = 1.0
    o64[64:, 64:] = 1.0
    psw = np.zeros((128, 128), np.float32)
    for i in range(64):
        psw[2 * i + 1, 2 * i] = -1.0
        psw[2 * i, 2 * i + 1] = 1.0
    cst = np.stack([ident, ones, o64, psw], axis=1)
    t = np.arange(2048)
    row = (t // 64).astype(np.float32)
    col = (t % 64).astype(np.float32)
    inv = (np.float32(10000.0) ** (-np.arange(16, dtype=np.float32) / np.float32(16))).astype(np.float32)
    ang = np.concatenate([row[:, None] * inv, col[:, None] * inv], axis=-1).astype(np.float32)
    idx = (np.arange(128) % 64) // 2
    cosT = np.cos(ang).astype(np.float32)[:, idx].T.copy()
    sinT = np.sin(ang).astype(np.float32)[:, idx].T.copy()

    def invcnt(L, w):
        tt = np.arange(L)
        lo = np.maximum(tt - w // 2, 0)
        hi = np.minimum(tt + w - w // 2, L)
        return (1.0 / (hi - lo).astype(np.float32)).astype(np.float32)
    invp = np.stack([np.tile(np.tile(invcnt(256, w), 2)[None, :], (128, 1)) for w in (2, 4, 8, 16)]).astype(np.float32)
    invs = np.stack([np.tile(invcnt(2048, w)[None, :], (128, 1)) for w in (2, 4, 8, 16)]).astype(np.float32)
    return cst, ident, cosT, sinT, invp, invs


def kernel(x_prompt, x_sample, cache_a_k, cache_a_v, cache_d_k, cache_d_v, c, c_ctx,
           mod_w, mod_b, norm_w, ffn_w1, ffn_w2,
           ev_w_in, ev_rpb, ev_conv_w, ev_conv_b, ev_w_out,
           od_w_in, od_pool_w, od_pool_scale, od_q_norm, od_k_norm, od_w_out):
    f = lambda a: np.ascontiguousarray(np.asarray(a, dtype=np.float32))
    x_prompt, x_sample = f(x_prompt), f(x_sample)
    cst, ident, cosT, sinT, invp, invs = _consts()
    nab = _na_bias(f(ev_rpb)[0])
    norm_w, mod_b = f(norm_w), f(mod_b)
    base = np.zeros((128, 400), np.float32)
    base[:, 0:96] = norm_w.reshape(2, 6, 8, 128).transpose(3, 0, 1, 2).reshape(128, 96)
    base[:, 96:240] = mod_b.reshape(2, 72, 128).transpose(2, 0, 1).reshape(128, 144)
    base[:, 240:252] = f(ev_conv_w)[0].reshape(3, 4, 128).transpose(2, 0, 1).reshape(128, 12)
    base[:, 252:256] = f(ev_conv_b)[0].reshape(4, 128).T
    base[:, 256:260] = f(od_pool_scale)[0].reshape(4, 128).T
    qn = f(od_q_norm)[0]
    kn = f(od_k_norm)[0]
    base[:, 260] = np.tile(qn, 2)
    base[:, 261] = np.tile(kn, 2)
    base[:, 399] = EPS
    base[:, 398] = 64.0 * EPS
    gk_row = np.tile(kn[None, :], (128, 1)).astype(np.float32)
    pool_w = f(od_pool_w)[0].transpose(1, 0, 2).copy()
    shared = {
        "mod_w": f(mod_w), "ffn_w1": f(ffn_w1), "ffn_w2": f(ffn_w2),
        "ev_w_in": f(ev_w_in)[0], "ev_w_out": f(ev_w_out)[0], "od_w_in": f(od_w_in)[0], "od_w_out": f(od_w_out)[0],
        "pool_w": pool_w, "cst": cst, "identf": ident, "gk_row": gk_row, "nab": nab,
        "cosT": cosT, "sinT": sinT, "invp": invp, "invs": invs,
    }
    c, c_ctx = f(c), f(c_ctx)
    in_maps = []
    for core in range(N_CORES):
        b = core % 2
        sm_ = base.copy()
        cond = np.stack([c_ctx.reshape(8, 128).T, c[b].reshape(8, 128).T], axis=-1)
        sm_[:, 262:278] = cond.reshape(128, 16)
        m = dict(shared)
        m["small"] = sm_
        m["xp"] = x_prompt[4 * core:4 * core + 4].reshape(1024, D)
        m["xs"] = x_sample[b]
        m["ca_k"] = f(cache_a_k)[b, 0]
        m["ca_v"] = f(cache_a_v)[b, 0]
        m["cd_k"] = f(cache_d_k)[b, 0]
        m["cd_v"] = f(cache_d_v)[b, 0]
        in_maps.append(m)
    nc = build_program()
    res = run_bass_kernel_spmd(nc, in_maps, core_ids=list(range(N_CORES)))
    r = res.results
    y_prompt = np.concatenate([r[i]["yp"].reshape(4, 256, D) for i in range(N_CORES)], axis=0)
    y_sample = np.stack([r[0]["ys"], r[1]["ys"]], axis=0)
    sak = np.concatenate([r[i]["sak"] for i in range(N_CORES)], axis=0)[:, None]
    sav = np.concatenate([r[i]["sav"] for i in range(N_CORES)], axis=0)[:, None]
    sdk = np.concatenate([r[i]["sdk"] for i in range(N_CORES)], axis=0)[:, None]
    sdv = np.concatenate([r[i]["sdv"] for i in range(N_CORES)], axis=0)[:, None]
    return (y_prompt.astype(np.float32), y_sample.astype(np.float32), sak.astype(np.float32),
            sav.astype(np.float32), sdk.astype(np.float32), sdv.astype(np.float32))
```

```python
import numpy as np
import concourse.bass as bass
import concourse.mybir as mybir
from concourse.bass_utils import run_bass_kernel_spmd

F32 = mybir.dt.float32
F32R = mybir.dt.float32r
ALU = mybir.AluOpType
AF = mybir.ActivationFunctionType

D = 1024
DFF = 2816
NEG = -1e30
EPS = 1e-6
NB = 32
N_PROMPT_PASS = 2
DO_SAMPLE = True
N_CORES = 8


class Prog:
    def __init__(self, nc):
        self.nc = nc
        self.ops = {e: [] for e in ("pe", "act", "dve", "pool", "sp")}
        self.sems = {}
        self.cnt = {}
        self.known = {e: {} for e in self.ops}
        self.last_w = {}
        self.readers = {}
        self.pe_pending = False

    def _sem(self, key):
        if key not in self.sems:
            self.sems[key] = self.nc.alloc_semaphore("s_" + str(key))
            self.cnt[key] = 0
        return self.sems[key]

    def _deps(self, eng, reads, writes):
        need = {}

        def add(st):
            if st is None:
                return
            k, v = st
            if need.get(k, 0) < v:
                need[k] = v
        for r in reads:
            add(self.last_w.get(r))
        for w in writes:
            add(self.last_w.get(w))
            for st in self.readers.get(w, ()):
                add(st)
        waits = []
        for k, v in need.items():
            if k == "pe" and eng == "pe":
                continue
            if self.known[eng].get(k, 0) >= v:
                continue
            self.known[eng][k] = v
            waits.append((k, v))
        return waits

    def _record(self, stamp, reads, writes):
        for r in reads:
            self.readers.setdefault(r, []).append(stamp)
        for w in writes:
            self.last_w[w] = stamp
            self.readers[w] = []

    def op(self, eng, fn, reads=(), writes=(), signal=True):
        waits = self._deps(eng, reads, writes)
        self._sem(eng)
        if signal:
            self.cnt[eng] += 1
            stamp = (eng, self.cnt[eng])
            inc = (eng, 1)
            if eng == "pe":
                self.pe_pending = False
        else:
            assert eng == "pe"
            stamp = (eng, self.cnt[eng] + 1)
            inc = None
            self.pe_pending = True
        self.ops[eng].append((waits, fn, inc))
        self._record(stamp, reads, writes)

    def dma(self, q, semkey, out, in_, reads=(), writes=(), **kw):
        if semkey is None:
            semkey = ("d",) + tuple(writes[0]) if isinstance(writes[0], tuple) else ("d", writes[0])
        waits = self._deps(q, reads, writes)
        self._sem(semkey)
        self.cnt[semkey] += 16
        stamp = (semkey, self.cnt[semkey])
        self.ops[q].append((waits, lambda e: e.dma_start(out=out, in_=in_, **kw), (semkey, 16)))
        self._record(stamp, reads, writes)

    def final_wait(self, eng, keys):
        waits = self._deps(eng, keys, ())
        self.ops[eng].append((waits, None, None))

    def emit(self):
        nc = self.nc
        assert not self.pe_pending
        with nc.Block() as blk:
            def run(name):
                def _f(e):
                    for waits, fn, inc in self.ops[name]:
                        for k, v in waits:
                            e.wait_ge(self.sems[k], v)
                        if fn is None:
                            continue
                        ins = fn(e)
                        if inc is not None:
                            ins.then_inc(self.sems[inc[0]], inc[1])
                return _f
            blk.tensor(run("pe"))
            blk.scalar(run("act"))
            blk.vector(run("dve"))
            blk.gpsimd(run("pool"))
            blk.sync(run("sp"))


def L(method, **kw):
    return lambda e: getattr(e, method)(**kw)


def na_ktiles(qt):
    lo = min(max(8 * qt - 4, 0), 24)
    hi = min(max(8 * qt + 3, 0), 24) + 7
    return list(range(lo // 2, hi // 2 + 1))


def build_program():
    nc = bass.Bass("TRN2", target_bir_lowering=False)
    nc.dge_precook = False
    P = Prog(nc)

    def din(name, shape, dt=F32):
        return nc.dram_tensor(name, list(shape), dt, kind="ExternalInput").ap()

    def dout(name, shape):
        return nc.dram_tensor(name, list(shape), F32, kind="ExternalOutput").ap()

    def dint(name, shape, dt=F32):
        return nc.dram_tensor(name, list(shape), dt).ap()

    xp = din("xp", [1024, D], F32R)
    xs = din("xs", [2048, D], F32R)
    ca_k = din("ca_k", [8, 256, 64], F32R)
    ca_v = din("ca_v", [8, 256, 64], F32R)
    cd_k = din("cd_k", [2, 256, 64], F32R)
    cd_v = din("cd_v", [2, 256, 64], F32R)
    mod_w = din("mod_w", [2, D, 9 * D], F32R)
    w1_d = din("ffn_w1", [2, 2, D, 2 * DFF], F32R)
    w2_d = din("ffn_w2", [2, 2, DFF, D], F32R)
    ev_w_in = din("ev_w_in", [D, 3072], F32R)
    ev_w_out = din("ev_w_out", [D, D], F32R)
    od_w_in = din("od_w_in", [D, 1280], F32R)
    od_w_out = din("od_w_out", [D, D], F32R)
    pool_w_d = din("pool_w", [128, 4, 128], F32R)
    cst_d = din("cst", [128, 4, 128], F32R)
    identf_d = din("identf", [128, 128])
    small_d = din("small", [128, 400])
    gk_row_d = din("gk_row", [128, 64])
    nab_d = din("nab", [8, 4, 8, 128, 512], F32R)
    cos_d = din("cosT", [128, 2048], F32R)
    sin_d = din("sinT", [128, 2048], F32R)
    invp_d = din("invp", [4, 128, 512], F32R)
    invs_d = din("invs", [4, 128, 2048], F32R)

    yp = dout("yp", [1024, D])
    ys = dout("ys", [2048, D])
    sak = dout("sak", [4, 8, 256, 64])
    sav = dout("sav", [4, 8, 256, 64])
    sdk = dout("sdk", [4, 2, 256, 64])
    sdv = dout("sdv", [4, 2, 256, 64])

    LP = 2048 + 32
    q_s = dint("q_s", [8, 128, 2048], F32R)
    k_s = dint("k_s", [512, 2048], F32R)
    v_s = dint("v_s", [2048, 512], F32R)
    z_s = dint("z_s", [512, LP], F32R)
    g_s = dint("g_s", [512, 2048], F32R)

    X = nc.alloc_sbuf_tensor("X", [128, 8, 2048], F32)
    Hb = nc.alloc_sbuf_tensor("Hb", [128, 8, 512], F32R)
    WS = nc.alloc_sbuf_tensor("WS", [128, 3, 4096], F32R)
    BIG = nc.alloc_sbuf_tensor("BIG", [128, NB, 512], F32)
    CST = nc.alloc_sbuf_tensor("CST", [128, 4, 128], F32R)
    IDF = nc.alloc_sbuf_tensor("IDF", [128, 128], F32)
    SM = nc.alloc_sbuf_tensor("SM", [128, 400], F32)
    GKR = nc.alloc_sbuf_tensor("GKR", [128, 64], F32)
    PW = nc.alloc_sbuf_tensor("PW", [128, 4, 128], F32R)
    SC = nc.alloc_sbuf_tensor("SC", [128, 16], F32R)
    MODT = nc.alloc_sbuf_tensor("MODT", [128, 2, 72, 2], F32)
    COEF = nc.alloc_sbuf_tensor("COEF", [128, 2, 9, 8, 2], F32)
    TMPS = nc.alloc_sbuf_tensor("TMPS", [128, 8], F32)
    RS = nc.alloc_sbuf_tensor("RS", [128, 2, 512], F32)
    ZT = nc.alloc_sbuf_tensor("ZT", [128, 520], F32)
    ps = [nc.alloc_psum_tensor(f"ps{i}", [128, 512], F32) for i in range(8)]

    IDR = CST[:, 0, :]
    ONES = CST[:, 1, :]
    ONES64 = CST[:, 2, :]
    PSWAP = CST[:, 3, :]
    Hf = Hb[:].bitcast(F32)
    KC = ["CONST"]

    NORMW, MODB, CONVW, CONVB, PSCALE, QN, KN, CONDC = 0, 96, 240, 252, 256, 260, 261, 262
    EPS64C, EPSC = 398, 399

    def sm(col, n=1):
        return SM[:, col:col + n]

    def bf(i):
        return BIG[:, i, :]

    def br(i):
        return BIG[:, i, :].bitcast(F32R)

    def bfs(i, n):
        return BIG[:, i:i + n, :]

    def brs(i, n):
        return BIG[:, i:i + n, :].bitcast(F32R)

    def flat(ap3):
        return ap3.rearrange("p a b -> p (a b)")

    def bk(i, n=1):
        return [("B", j) for j in range(i, i + n)]

    bank_ctr = [0]

    wide = [True]

    def nb():
        b = bank_ctr[0] % (8 if wide[0] else 4)
        bank_ctr[0] += 1
        return b

    ws_ctr = [0]

    def wstage():
        s = ws_ctr[0] % 3
        ws_ctr[0] += 1
        return s

    def mm(bank, cols, lhsT, rhs, start, stop, reads, signal=None):
        if signal is None:
            signal = stop
        P.op("pe", L("matmul", out=ps[bank][:, cols[0]:cols[1]], lhsT=lhsT, rhs=rhs, start=start, stop=stop),
             reads=reads, writes=[("ps", bank)], signal=signal)

    def tr(bank, cols, in_, reads, signal):
        P.op("pe", L("transpose", out=ps[bank][:, cols[0]:cols[1]], in_=in_, identity=IDF[:]),
             reads=reads + KC, writes=[("ps", bank)], signal=signal)

    evac_ctr = [0]

    def evac(out, bank, cols, writes, scale=None, eng=None, rows=(0, 128)):
        src = ps[bank][rows[0]:rows[1], cols[0]:cols[1]]
        if eng is None:
            eng = "act" if evac_ctr[0] % 2 == 0 else "dve"
            evac_ctr[0] += 1
        if eng == "act":
            kw = dict(out=out, in_=src, func=AF.Copy)
            if scale is not None:
                kw["scale"] = scale
            P.op("act", L("activation", **kw), reads=[("ps", bank)], writes=writes)
        else:
            if scale is None:
                P.op("dve", L("tensor_copy", out=out, in_=src), reads=[("ps", bank)], writes=writes)
            else:
                P.op("dve", L("tensor_scalar", out=out, in0=src, scalar1=scale, scalar2=None, op0=ALU.mult),
                     reads=[("ps", bank)], writes=writes)

    def dve(method, reads, writes, **kw):
        P.op("dve", L(method, **kw), reads=reads, writes=writes)

    def act(reads, writes, **kw):
        P.op("act", L("activation", **kw), reads=reads, writes=writes)

    def ld(out, in_, reads, writes):
        P.dma("pool", None, out, in_, reads=reads, writes=writes)

    P.dma("pool", "c0", CST[:], cst_d, writes=KC)
    P.dma("pool", "c0", IDF[:], identf_d, writes=KC)
    P.dma("pool", "c0", SM[:], small_d, writes=KC)
    P.dma("pool", "c0", GKR[:], gk_row_d, writes=KC)
    P.dma("pool", "c0", PW[:], pool_w_d, writes=KC)

    P.op("pool", L("memset", ap=ZT[:], constant=0.0), writes=["ZT"])

    def zero_z():
        for r in range(4):
            for q4 in range(4):
                P.dma("pool", "zs", z_s[r * 128:(r + 1) * 128, q4 * 520:(q4 + 1) * 520], ZT[:].bitcast(F32R),
                      reads=["ZT"], writes=["ZS"])

    def zero_half(blk_id, r0):
        dve("tensor_copy", ["ZT"], bk(blk_id), out=br(blk_id)[r0:r0 + 64, :], in_=ZT[r0:r0 + 64, 0:512])

    act(KC, ["SC"], out=SC[:], in_=sm(CONDC, 16), func=AF.Silu)
    SC3 = SC[:].rearrange("p (k n) -> p k n", n=2)
    def mod_layer(l):
        for nblk in range(18):
            st = wstage()
            wv = WS[:, st, :].rearrange("p (k n) -> p k n", k=8)
            P.dma("sp", ("ws", st), wv, mod_w[l, :, nblk * 512:(nblk + 1) * 512].rearrange("(k p) n -> p k n", p=128),
                  writes=[("W", st)])
            for mi in range(4):
                j = nblk * 4 + mi
                b = nb()
                for kc in range(8):
                    mm(b, (0, 2), wv[:, kc, mi * 128:(mi + 1) * 128], SC3[:, kc, :], kc == 0, kc == 7,
                       reads=[("W", st), "SC"])
                dve("tensor_scalar", [("ps", b)] + KC, [f"MODT{l}"], out=MODT[:, l, j, :], in0=ps[b][:, 0:2],
                    scalar1=sm(MODB + l * 72 + j), scalar2=None, op0=ALU.add)
        for ci in range(2):
            for sub in range(3):
                sh, scl, gt = 3 * sub, 3 * sub + 1, 3 * sub + 2
                gpre, gpost = 2 * sub, 2 * sub + 1
                half = 0.5 if sub != 1 else 1.0
                gw_pre = sm(NORMW + l * 48 + gpre * 8, 8)
                gw_post = sm(NORMW + l * 48 + gpost * 8, 8)
                dve("scalar_tensor_tensor", [f"MODT{l}"] + KC, [f"COEF{l}"], out=COEF[:, l, 3 * sub, :, ci],
                    in0=MODT[:, l, scl * 8:(scl + 1) * 8, ci], scalar=1.0, in1=gw_pre, op0=ALU.add, op1=ALU.mult)
                dve("tensor_copy", [f"MODT{l}"], [f"COEF{l}"], out=COEF[:, l, 3 * sub + 1, :, ci],
                    in_=MODT[:, l, sh * 8:(sh + 1) * 8, ci])
                dve("scalar_tensor_tensor", [f"MODT{l}"] + KC, [f"COEF{l}"], out=COEF[:, l, 3 * sub + 2, :, ci],
                    in0=MODT[:, l, gt * 8:(gt + 1) * 8, ci], scalar=half, in1=gw_post, op0=ALU.mult, op1=ALU.mult)

    mod_layer(0)

    def coef(l, k, c, ci):
        return COEF[:, l, k, c, ci:ci + 1]

    SQ0 = 24
    RSTD = 32

    def Xc(c, col0):
        return X[:, c, col0:col0 + 512]

    def stats(src, skey, nch=8, sq0=None, presq=False):
        sq0 = SQ0 if sq0 is None else sq0
        if not presq:
            for c in range(nch):
                act([skey(c)], bk(sq0 + c), out=br(sq0 + c), in_=src(c), func=AF.Square)
        b = nb()
        for c in range(nch):
            mm(b, (0, 512), ONES, br(sq0 + c), c == 0, c == nch - 1, reads=bk(sq0 + c) + KC)
        act([("ps", b)] + KC, ["RSTD"], out=RS[:, 0, :], in_=ps[b][:], func=AF.Sqrt, scale=1.0 / 1024.0, bias=sm(EPSC))
        dve("reciprocal", ["RSTD"], ["RSTD"], out=RS[:, 0, :], in_=RS[:, 0, :])

    def pre_norm(l, sub, col0, ci):
        stats(lambda c: Xc(c, col0), lambda c: ("X", c, col0))
        for c in range(8):
            dve("tensor_tensor", [("X", c, col0)] + ["RSTD"], bk(SQ0 + c), out=br(SQ0 + c), in0=Xc(c, col0),
                in1=RS[:, 0, :], op=ALU.mult)
            act(bk(SQ0 + c) + [f"COEF{l}"], [("H", c)], out=Hb[:, c, :], in_=bf(SQ0 + c), func=AF.Identity,
                scale=coef(l, 3 * sub, c, ci), bias=coef(l, 3 * sub + 1, c, ci))

    def post_norm(l, sub, col0, ci, sq0=None):
        stats(lambda c: Hf[:, c, :], lambda c: ("H", c), sq0=sq0, presq=(sq0 is not None))
        for c in range(8):
            dve("tensor_tensor", [("H", c), "RSTD"], bk(SQ0 + c), out=br(SQ0 + c), in0=Hf[:, c, :], in1=RS[:, 0, :],
                op=ALU.mult)
            dve("scalar_tensor_tensor", bk(SQ0 + c) + [f"COEF{l}", ("X", c, col0)], [("X", c, col0)], out=Xc(c, col0),
                in0=bf(SQ0 + c), scalar=coef(l, 3 * sub + 2, c, ci), in1=Xc(c, col0), op0=ALU.mult, op1=ALU.add)

    def evac_sq(m, bank, sqblk):
        P.op("dve", L("tensor_copy", out=Hb[:, m, :], in_=ps[bank][:]), reads=[("ps", bank)], writes=[("H", m)])
        act([("H", m)], bk(sqblk), out=br(sqblk), in_=Hf[:, m, :], func=AF.Square)

    def ffn(l, s, col0, ci):
        sub = 0 if s == 0 else 2
        pre_norm(l, sub, col0, ci)
        w1 = w1_d[l, s]
        w2 = w2_d[l, s]
        for jb in range(11):
            st = wstage()
            wv = WS[:, st, :].rearrange("p (k n) -> p k n", k=8)
            P.dma("sp", ("ws", st), wv[:, :, 0:256], w1[:, jb * 256:(jb + 1) * 256].rearrange("(k p) n -> p k n", p=128),
                  writes=[("W", st)])
            P.dma("sp", ("ws", st), wv[:, :, 256:512],
                  w1[:, DFF + jb * 256:DFF + (jb + 1) * 256].rearrange("(k p) n -> p k n", p=128), writes=[("W", st)])
            for jj in range(2):
                j = jb * 2 + jj
                bg_, bu_ = nb(), nb()
                for kc in range(8):
                    mm(bg_, (0, 512), wv[:, kc, jj * 128:(jj + 1) * 128], Hb[:, kc, :], kc == 0, kc == 7,
                       reads=[("W", st), ("H", kc)])
                for kc in range(8):
                    mm(bu_, (0, 512), wv[:, kc, 256 + jj * 128:256 + (jj + 1) * 128], Hb[:, kc, :], kc == 0, kc == 7,
                       reads=[("W", st), ("H", kc)])
                tb = 22 + (j % 2)
                act([("ps", bg_)], bk(tb), out=br(tb), in_=ps[bg_][:], func=AF.Silu)
                dve("tensor_tensor", [("ps", bu_)] + bk(tb), bk(j), out=br(j), in0=bf(tb), in1=ps[bu_][:], op=ALU.mult)
        for mb in range(8):
            st = wstage()
            wv = WS[:, st, 0:2816].rearrange("p (j n) -> p j n", j=22)
            P.dma("sp", ("ws", st), wv, w2[:, mb * 128:(mb + 1) * 128].rearrange("(j p) n -> p j n", p=128),
                  writes=[("W", st)])
            b = nb()
            for j in range(22):
                mm(b, (0, 512), wv[:, j, :], br(j), j == 0, j == 21, reads=[("W", st)] + bk(j))
            evac(Hb[:, mb, :], b, (0, 512), [("H", mb)])
        post_norm(l, sub, col0, ci)

    def stg_view(r=False):
        v = BIG[:, 0:8, :]
        if r:
            v = v.bitcast(F32R)
        return v.rearrange("p (t h) n -> p t (h n)", t=4)

    def load_x(src, r0, col0):
        stg = stg_view()
        P.dma("pool", "xin", stg_view(True), src[r0:r0 + 512, :].rearrange("(t p) d -> p t d", p=128), writes=bk(0, 8))
        for c in range(8):
            b = nb()
            for tt in range(4):
                tr(b, (tt * 128, (tt + 1) * 128), stg[:, tt, c * 128:(c + 1) * 128], bk(2 * tt, 2), tt == 3)
            evac(Xc(c, col0), b, (0, 512), [("X", c, col0)])

    def store_y(dst, name, r0, col0):
        stg = stg_view()
        for tt in range(4):
            for hf in range(2):
                b = nb()
                for i in range(4):
                    c = hf * 4 + i
                    tr(b, (i * 128, (i + 1) * 128), X[:, c, col0 + tt * 128:col0 + (tt + 1) * 128], [("X", c, col0)], i == 3)
                evac(stg_view(True)[:, tt, hf * 512:(hf + 1) * 512], b, (0, 512), bk(2 * tt + hf))
        key = ("Y", name, r0)
        P.dma("pool", "yout", dst[r0:r0 + 512, :].rearrange("(t p) d -> p t d", p=128), stg, reads=bk(0, 8), writes=[key])
        return key

    def proj_fm(wv, col_lo, wkey):
        b = nb()
        for kc in range(8):
            mm(b, (0, 512), wv[:, kc, col_lo:col_lo + 128], Hb[:, kc, :], kc == 0, kc == 7, reads=[wkey, ("H", kc)])
        return b

    def proj_tm(wv, col_lo, ncol, tt, wkey):
        b = nb()
        for kc in range(8):
            mm(b, (0, ncol), Hb[:, kc, tt * 128:(tt + 1) * 128], wv[:, kc, col_lo:col_lo + ncol], kc == 0, kc == 7,
               reads=[wkey, ("H", kc)])
        return b

    def load_wblock(w, col_lo, ncol):
        st = wstage()
        wv = WS[:, st, 0:8 * ncol].rearrange("p (k n) -> p k n", k=8)
        P.dma("sp", ("ws", st), wv, w[:, col_lo:col_lo + ncol].rearrange("(k p) n -> p k n", p=128), writes=[("W", st)])
        return wv, ("W", st)

    bias_ctr = [0]
    acc_ctr = [0]

    def attention(jobs, mix_blk, pts=(18, 19), bias_blks=(20, 21, 22)):
        npt = len(pts)
        for ji, (qblk, half, q0, q1, ktiles) in enumerate(jobs):
            n = q1 - q0
            ob, db = (4, 5) if acc_ctr[0] % 2 == 0 else (6, 7)
            acc_ctr[0] += 1
            nk = len(ktiles)
            bias_blk = {}
            sbank = {}

            def prefetch(ki):
                if ki < nk and ktiles[ki][2] is not None:
                    bb = bias_blks[bias_ctr[0] % len(bias_blks)]
                    bias_ctr[0] += 1
                    ld(br(bb), ktiles[ki][2], [], bk(bb))
                    bias_blk[ki] = bb

            def score(ki):
                if ki >= nk:
                    return
                kT, vv, bias, rkeys = ktiles[ki]
                b = nb()
                sbank[ki] = b
                mm(b, (0, n), kT, br(qblk)[:, q0:q1], True, bias is None, reads=rkeys + bk(qblk))
                if bias is not None:
                    bb = bias_blk[ki]
                    mm(b, (0, n), IDR, br(bb)[:, 0:n], False, True, reads=bk(bb) + KC)
            prefetch(0)
            prefetch(1)
            score(0)
            score(1)
            for ki, (kT, vv, bias, rkeys) in enumerate(ktiles):
                prefetch(ki + 2)
                b = sbank[ki]
                pt = pts[ki % npt]
                act([("ps", b)], bk(pt), out=br(pt)[:, 0:n], in_=ps[b][:, 0:n], func=AF.Exp)
                score(ki + 2)
                mm(ob, (0, n), vv, br(pt)[:, 0:n], ki == 0, ki == nk - 1, reads=rkeys + bk(pt))
                mm(db, (0, n), ONES, br(pt)[:, 0:n], ki == 0, ki == nk - 1, reads=bk(pt) + KC, signal=True)
            r0, r1 = half * 64, half * 64 + 64
            dve("reciprocal", [("ps", db)], ["RDEN"], out=RS[r0:r1, 1, 0:n], in_=ps[db][r0:r1, 0:n])
            dve("tensor_tensor", [("ps", ob), "RDEN"], bk(mix_blk), out=br(mix_blk)[r0:r1, q0:q1],
                in0=ps[ob][r0:r1, 0:n], in1=RS[r0:r1, 1, 0:n], op=ALU.mult)

    def mixer_out(l, w_out, col0, ci):
        for nblk in range(2):
            wv, wk = load_wblock(w_out, nblk * 512, 512)
            for mi in range(4):
                m = nblk * 4 + mi
                b = nb()
                for kc in range(8):
                    mm(b, (0, 512), wv[:, kc, mi * 128:(mi + 1) * 128], br(24 + kc), kc == 0, kc == 7, reads=[wk] + bk(24 + kc))
                evac_sq(m, b, m)
        post_norm(l, 1, col0, ci, sq0=0)

    def scr_w(dst, src, reads, key):
        P.dma("pool", ("scr", key), dst, src, reads=reads, writes=[key])

    def even_loop1(l, kind, col0, ci, tok0, zoff, bl):
        pre_norm(l, 1, col0, ci)
        wv, wk = load_wblock(ev_w_in, 0, 512)
        for m in range(4):
            b = proj_fm(wv, m * 128, wk)
            evac(br(m), b, (0, 512), bk(m), scale=0.125)
        scr_w(q_s[0:4, :, tok0:tok0 + 512].rearrange("c p t -> p c t"), brs(0, 4), bk(0, 4), "QS")
        wv, wk = load_wblock(ev_w_in, 512, 512)
        for m in range(4):
            b = proj_fm(wv, m * 128, wk)
            evac(br(4 + m), b, (0, 512), bk(4 + m))
        scr_w(k_s[:, tok0:tok0 + 512].rearrange("(c p) t -> p c t", p=128), brs(4, 4), bk(4, 4), "KS")
        if kind == "p":
            for tt in range(4):
                b = proj_tm(wv, 0, 512, tt, wk)
                evac(br(8 + tt), b, (0, 512), bk(8 + tt))
            for tt in range(4):
                P.dma("pool", ("so", "ak", tt), sak[bl + tt // 2][:, (tt % 2) * 128:(tt % 2 + 1) * 128, :].rearrange("h p d -> p h d"),
                      bf(8 + tt).rearrange("p (h d) -> p h d", d=64), reads=bk(8 + tt), writes=[("SAK", bl, tt)])
        wv, wk = load_wblock(ev_w_in, 1024, 512)
        for tt in range(4):
            b = proj_tm(wv, 0, 512, tt, wk)
            evac(br(12 + tt), b, (0, 512), bk(12 + tt))
        scr_w(v_s[tok0:tok0 + 512, :].rearrange("(t p) f -> p t f", p=128), brs(12, 4), bk(12, 4), "VS")
        if kind == "p":
            for tt in range(4):
                P.dma("pool", ("so", "av", tt), sav[bl + tt // 2][:, (tt % 2) * 128:(tt % 2 + 1) * 128, :].rearrange("h p d -> p h d"),
                      bf(12 + tt).rearrange("p (h d) -> p h d", d=64), reads=bk(12 + tt), writes=[("SAV", bl, tt)])
        wv, wk = load_wblock(ev_w_in, 1536, 512)
        for m in range(4):
            b = proj_fm(wv, m * 128, wk)
            evac(br(16 + m), b, (0, 512), bk(16 + m))
        scr_w(g_s[:, tok0:tok0 + 512].rearrange("(c p) t -> p c t", p=128), brs(16, 4), bk(16, 4), "GS")
        wv, wk = load_wblock(ev_w_in, 2048, 512)
        for m in range(4):
            b = proj_fm(wv, m * 128, wk)
            evac(br(8 + m), b, (0, 512), bk(8 + m))
        wv, wk = load_wblock(ev_w_in, 2560, 512)
        for m in range(4):
            b = proj_fm(wv, m * 128, wk)
            dve("tensor_tensor", [("ps", b)] + bk(8 + m), bk(8 + m), out=br(8 + m), in0=bf(8 + m), in1=ps[b][:], op=ALU.mult)
        for (seg_tok, seg_len, zcol) in zoff:
            scr_w(z_s[:, zcol:zcol + seg_len].rearrange("(c p) t -> p c t", p=128),
                  BIG[:, 8:12, seg_tok:seg_tok + seg_len].bitcast(F32R), bk(8, 4), "ZS")

    def even_loop2(l, kind, col0, ci, tok0, zoff, qt):
        wide[0] = False
        kts_q = na_ktiles(qt)
        klo, nkt = kts_q[0], len(kts_q)

        def conv_chunk(c):
            ld(br(15), g_s[c * 128:(c + 1) * 128, tok0:tok0 + 512], ["GS"], bk(15))
            mixb = 28 + c
            for (seg_tok, seg_len, zcol) in zoff:
                zl = seg_len + 2
                ztr = flat(brs(16, 2))[:, 0:zl]
                zt = flat(bfs(16, 2))[:, 0:zl]
                zk = bk(16, 2)
                ld(ztr, z_s[c * 128:(c + 1) * 128, zcol - 1:zcol - 1 + zl], ["ZS"], zk)
                acc = bf(mixb)[:, seg_tok:seg_tok + seg_len]
                accr = br(mixb)[:, seg_tok:seg_tok + seg_len]
                dve("tensor_scalar", zk + KC, bk(mixb), out=accr, in0=zt[:, 0:seg_len], scalar1=sm(CONVW + c), scalar2=None,
                    op0=ALU.mult)
                for jx in (1, 2):
                    dve("scalar_tensor_tensor", zk + KC + bk(mixb), bk(mixb), out=accr, in0=zt[:, jx:jx + seg_len],
                        scalar=sm(CONVW + jx * 4 + c), in1=acc, op0=ALU.mult, op1=ALU.add)
            dve("scalar_tensor_tensor", bk(mixb) + bk(15) + KC, bk(mixb), out=br(mixb), in0=bf(mixb), scalar=sm(CONVB + c),
                in1=bf(15), op0=ALU.add, op1=ALU.mult)

        for qa_, qb_ in ((0, 1), (2, 3)):
            zero_half(qa_, 64)
            zero_half(qb_, 0)

        def chunk_loads(c):
            sset = c % 2
            qa, qb = (0, 1) if sset == 0 else (2, 3)
            kb0, vb0 = 4 + 2 * sset, 8 + 2 * sset
            cx0, cx1 = (12, 13) if sset == 0 else (14, 23)
            ld(br(qa)[0:64, :], q_s[c, 0:64, tok0:tok0 + 512], ["QS"], bk(qa))
            ld(br(qb)[64:128, :], q_s[c, 64:128, tok0:tok0 + 512], ["QS"], bk(qb))
            if kind == "p":
                ld(br(kb0), k_s[c * 128:(c + 1) * 128, tok0:tok0 + 512], ["KS"], bk(kb0))
                v8 = br(vb0).rearrange("p (t f) -> p t f", f=128)
                ld(v8, v_s[tok0:tok0 + 512, c * 128:(c + 1) * 128].rearrange("(t p) f -> p t f", p=128), ["VS"], bk(vb0))
            else:
                k2 = flat(brs(kb0, 2))[:, 0:nkt * 128]
                v2 = brs(vb0, 2).rearrange("p a (t f) -> p (a t) f", f=128)[:, 0:nkt, :]
                ld(k2, k_s[c * 128:(c + 1) * 128, klo * 128:(klo + nkt) * 128], ["KS"], bk(kb0, 2))
                ld(v2, v_s[klo * 128:(klo + nkt) * 128, c * 128:(c + 1) * 128].rearrange("(t p) f -> p t f", p=128), ["VS"],
                   bk(vb0, 2))
                for tt in range(2):
                    ld(br(cx0)[:, tt * 128:(tt + 1) * 128].rearrange("p (h d) -> p h d", h=2),
                       ca_k[2 * c:2 * c + 2, tt * 128:(tt + 1) * 128, :].rearrange("h p d -> p h d"), [], bk(cx0))
                    ld(br(cx1)[:, tt * 128:(tt + 1) * 128].rearrange("p (h d) -> p h d", h=2),
                       ca_v[2 * c:2 * c + 2, tt * 128:(tt + 1) * 128, :].rearrange("h p d -> p h d"), [], bk(cx1))

        chunk_loads(0)
        for c in range(4):
            if c + 1 < 4:
                chunk_loads(c + 1)
            sset = c % 2
            qa, qb = (0, 1) if sset == 0 else (2, 3)
            kb0, vb0 = 4 + 2 * sset, 8 + 2 * sset
            cx0, cx1 = (12, 13) if sset == 0 else (14, 23)
            jobs = []
            if kind == "p":
                v8 = br(vb0).rearrange("p (t f) -> p t f", f=128)
                for half in range(2):
                    for sg in range(2):
                        kts = []
                        for kt in range(2):
                            ktile = sg * 2 + kt
                            kts.append((br(kb0)[:, ktile * 128:(ktile + 1) * 128], v8[:, ktile, :], None, bk(kb0) + bk(vb0)))
                        jobs.append((qa if half == 0 else qb, half, sg * 256, (sg + 1) * 256, kts))
            else:
                k2 = flat(brs(kb0, 2))[:, 0:nkt * 128]
                v2 = brs(vb0, 2).rearrange("p a (t f) -> p (a t) f", f=128)[:, 0:nkt, :]
                b = nb()
                for tt in range(2):
                    tr(b, (tt * 128, (tt + 1) * 128), bf(cx0)[:, tt * 128:(tt + 1) * 128], bk(cx0), tt == 1)
                evac(br(cx1)[:, 256:512], b, (0, 256), bk(cx1))
                vc = br(cx1)[:, 0:256].rearrange("p (t f) -> p t f", f=128)
                for half in range(2):
                    h = 2 * c + half
                    kts = []
                    for j in range(nkt):
                        kts.append((k2[:, j * 128:(j + 1) * 128], v2[:, j, :], nab_d[h, qt, j], bk(kb0, 2) + bk(vb0, 2)))
                    for tt in range(2):
                        kts.append((br(cx1)[:, 256 + tt * 128:256 + (tt + 1) * 128], vc[:, tt, :], None, bk(cx1)))
                    jobs.append((qa if half == 0 else qb, half, 0, 512, kts))
            attention(jobs, 24 + c)
            conv_chunk(c)
        wide[0] = True
        mixer_out(l, ev_w_out, col0, ci)

    def qk_norm_rope(blks, gcol, rope, qscale):
        nbk = len(blks)
        banks = []
        for i, bid in enumerate(blks):
            act(bk(bid), bk(SQ0 + i), out=br(SQ0 + i), in_=bf(bid), func=AF.Square)
        for i, bid in enumerate(blks):
            b = nb()
            banks.append(b)
            mm(b, (0, 512), ONES64, br(SQ0 + i), True, True, reads=bk(SQ0 + i) + KC)
            if qscale:
                act([("ps", b)] + KC, bk(28 + i), out=br(28 + i), in_=ps[b][:], func=AF.Sqrt, scale=1.0, bias=sm(EPS64C))
            else:
                act([("ps", b)] + KC, bk(28 + i), out=br(28 + i), in_=ps[b][:], func=AF.Sqrt, scale=1.0 / 64.0, bias=sm(EPSC))
        assert nbk <= 2
        rkeys = ["RSTD", "RDEN"][:nbk]
        dve("reciprocal", bk(28, nbk), rkeys, out=RS[:, 0:nbk, :], in_=bfs(28, nbk))
        for i, bid in enumerate(blks):
            if not rope:
                dve("scalar_tensor_tensor", bk(bid) + rkeys + KC, bk(bid), out=br(bid), in0=bf(bid), scalar=sm(gcol),
                    in1=RS[:, i, :], op0=ALU.mult, op1=ALU.mult)
                continue
            s1, t1, t2 = SQ0 + i, 15 + i, 19 + i
            dve("scalar_tensor_tensor", bk(bid) + rkeys + KC, bk(s1), out=br(s1), in0=bf(bid), scalar=sm(gcol),
                in1=RS[:, i, :], op0=ALU.mult, op1=ALU.mult)
            b = nb()
            mm(b, (0, 512), PSWAP, br(s1), True, True, reads=bk(s1) + KC)
            dve("tensor_tensor", bk(s1) + bk(12), bk(t1), out=br(t1), in0=bf(s1), in1=bf(12), op=ALU.mult)
            dve("tensor_tensor", [("ps", b)] + bk(13), bk(t2), out=br(t2), in0=ps[b][:], in1=bf(13), op=ALU.mult)
            dve("tensor_tensor", bk(t1) + bk(t2), bk(bid), out=br(bid), in0=bf(t1), in1=bf(t2), op=ALU.add)

    def odd_loop1(l, kind, col0, ci, tok0, zoff, bl):
        pre_norm(l, 1, col0, ci)
        rope = (kind == "s")
        if rope:
            ld(br(12), cos_d[:, tok0:tok0 + 512], [], bk(12))
            ld(br(13), sin_d[:, tok0:tok0 + 512], [], bk(13))
        wv, wk = load_wblock(od_w_in, 0, 512)
        for m in range(4):
            b = proj_fm(wv, m * 128, wk)
            evac(br(8 + m), b, (0, 512), bk(8 + m))
        for (seg_tok, seg_len, zcol) in zoff:
            scr_w(z_s[:, zcol:zcol + seg_len].rearrange("(c p) t -> p c t", p=128),
                  BIG[:, 8:12, seg_tok:seg_tok + seg_len].bitcast(F32R), bk(8, 4), "ZS")
        wv, wk = load_wblock(od_w_in, 512, 512)
        for m in range(4):
            b = proj_fm(wv, m * 128, wk)
            evac(br(m), b, (0, 512), bk(m))
        wv, wk = load_wblock(od_w_in, 1024, 256)
        b = proj_fm(wv, 0, wk)
        evac(br(4), b, (0, 512), bk(4))
        vdup = brs(5, 2).rearrange("p a (t f) -> p (a t) f", f=256)
        for tt in range(4):
            b = proj_tm(wv, 128, 128, tt, wk)
            veng = "act" if tt % 2 == 0 else "dve"
            for g in range(2):
                for dup in range(2):
                    evac(vdup[:, tt, (2 * g + dup) * 64:(2 * g + dup + 1) * 64], b, (g * 64, g * 64 + 64), bk(5, 2), eng=veng)
            if kind == "p":
                evac(br(7)[:, tt * 128:(tt + 1) * 128], b, (0, 128), bk(7), eng=veng)
        scr_w(v_s[tok0:tok0 + 512, 0:256].rearrange("(t p) f -> p t f", p=128), vdup, bk(5, 2), "VS")
        ktm_banks = []
        if kind == "p":
            for tt in range(4):
                ktm_banks.append(proj_tm(wv, 0, 128, tt, wk))
                b = ktm_banks[-1]
                for g in range(2):
                    act([("ps", b)], bk(15) + ["TMPS"], out=br(15)[:, g * 64:(g + 1) * 64], in_=ps[b][:, g * 64:(g + 1) * 64],
                        func=AF.Square, accum_out=TMPS[:, g:g + 1])
                act(["TMPS"] + KC, ["TMPS"], out=TMPS[:, 2:4], in_=TMPS[:, 0:2], func=AF.Sqrt, scale=1.0 / 64.0, bias=sm(EPSC))
                dve("reciprocal", ["TMPS"], ["TMPS"], out=TMPS[:, 4:6], in_=TMPS[:, 2:4])
                for g in range(2):
                    dve("scalar_tensor_tensor", [("ps", b), "TMPS"] + KC, bk(14),
                        out=br(14)[:, tt * 128 + g * 64:tt * 128 + (g + 1) * 64], in0=ps[b][:, g * 64:(g + 1) * 64],
                        scalar=TMPS[:, 4 + g:5 + g], in1=GKR[:], op0=ALU.mult, op1=ALU.mult)
        qk_norm_rope([0, 1], QN, rope, True)
        qk_norm_rope([2, 3], QN, rope, True)
        scr_w(q_s[0:4, :, tok0:tok0 + 512].rearrange("c p t -> p c t"), brs(0, 4), bk(0, 4), "QS")
        qk_norm_rope([4], KN, rope, False)
        scr_w(k_s[0:128, tok0:tok0 + 512], br(4), bk(4), "KS")
        if kind == "p":
            for tt in range(4):
                P.dma("pool", ("so", "dv", tt), sdv[bl + tt // 2][:, (tt % 2) * 128:(tt % 2 + 1) * 128, :].rearrange("g p d -> p g d"),
                      bf(7)[:, tt * 128:(tt + 1) * 128].rearrange("p (g d) -> p g d", g=2),
                      reads=bk(7), writes=[("SDV", bl, tt)])
            for tt in range(4):
                P.dma("pool", ("so", "dk", tt), sdk[bl + tt // 2][:, (tt % 2) * 128:(tt % 2 + 1) * 128, :].rearrange("g p d -> p g d"),
                      bf(14)[:, tt * 128:(tt + 1) * 128].rearrange("p (g d) -> p g d", g=2),
                      reads=bk(14), writes=[("SDK", bl, tt)])

    def odd_loop2(l, kind, col0, ci, tok0, zoff):
        wide[0] = False
        inv_src = (invp_d.rearrange("g p t -> p g t") if kind == "p" else invs_d[:, :, tok0:tok0 + 512].rearrange("g p t -> p g t"))
        ld(brs(19, 4), inv_src, [], bk(19, 4))
        for (seg_tok, seg_len, zcol) in zoff:
            Ls = seg_len
            Lp = Ls + 16

            def v3(b0, nblk, ng, r=False):
                base = flat(brs(b0, nblk)) if r else flat(bfs(b0, nblk))
                return base[:, 0:ng * Lp].rearrange("p (g t) -> p g t", g=ng)
            ku, ka, kb, kc_, kd = bk(0, 5), bk(5, 5), bk(10, 4), bk(15, 3), bk(18)
            ld(v3(0, 5, 4, True), z_s[:, zcol - 8:zcol + Ls + 8].rearrange("(g p) t -> p g t", p=128), ["ZS"], ku)
            U, A, B, C = v3(0, 5, 4), v3(5, 5, 4), v3(10, 4, 3), v3(15, 3, 2)
            Ar, Br, Cr = v3(5, 5, 4, True), v3(10, 4, 3, True), v3(15, 3, 2, True)
            dve("tensor_tensor", ku, ka, out=Ar[:, :, 0:Ls + 15], in0=U[:, :, 0:Ls + 15], in1=U[:, :, 1:Ls + 16], op=ALU.add)
            dve("tensor_tensor", ka, kb, out=Br[:, :, 0:Ls + 13], in0=A[:, 1:4, 0:Ls + 13], in1=A[:, 1:4, 2:Ls + 15], op=ALU.add)
            dve("tensor_tensor", kb, kc_, out=Cr[:, :, 0:Ls + 9], in0=B[:, 1:3, 0:Ls + 9], in1=B[:, 1:3, 4:Ls + 13], op=ALU.add)
            dve("tensor_tensor", kc_, kd, out=br(18)[:, 0:Ls], in0=C[:, 1, 0:Ls], in1=C[:, 1, 8:8 + Ls], op=ALU.add)
            ssrc = [(A[:, 0, 7:7 + Ls], ka), (B[:, 0, 6:6 + Ls], kb), (C[:, 0, 4:4 + Ls], kc_), (bf(18)[:, 0:Ls], kd)]
            for gi in range(4):
                sv, skey = ssrc[gi]
                pb = 23 if gi % 2 == 0 else 14
                tmpb = 14 if gi % 2 == 0 else 23
                dve("tensor_tensor", skey + bk(19 + gi), bk(24 + gi), out=br(24 + gi)[:, seg_tok:seg_tok + Ls], in0=sv,
                    in1=bf(19 + gi)[:, seg_tok:seg_tok + Ls], op=ALU.mult)
                dve("tensor_tensor", bk(24 + gi) + ku, bk(28 + gi), out=br(28 + gi)[:, seg_tok:seg_tok + Ls],
                    in0=bf(24 + gi)[:, seg_tok:seg_tok + Ls], in1=U[:, gi, 8:8 + Ls], op=ALU.subtract)
        for gi in range(4):
            b = nb()
            mm(b, (0, 512), PW[:, gi, :], br(28 + gi), True, True, reads=bk(28 + gi) + KC)
            dve("tensor_scalar", [("ps", b)] + KC, bk(24 + gi), out=br(24 + gi), in0=ps[b][:], scalar1=sm(PSCALE + gi),
                scalar2=None, op0=ALU.mult)
        v8 = brs(8, 8).rearrange("p a (t f) -> p (a t) f", f=256)
        if kind == "p":
            ld(br(4), k_s[0:128, tok0:tok0 + 512], ["KS"], bk(4))
            ld(v8[:, 0:4, :], v_s[tok0:tok0 + 512, 0:256].rearrange("(t p) f -> p t f", p=128), ["VS"], bk(8, 2))
            k4 = br(4)
        else:
            k4 = flat(brs(4, 4))
            ld(k4, k_s[0:128, 0:2048], ["KS"], bk(4, 4))
            ld(v8, v_s[0:2048, 0:256].rearrange("(t p) f -> p t f", p=128), ["VS"], bk(8, 8))
            for tt in range(2):
                ld(br(16)[:, tt * 128:(tt + 1) * 128].rearrange("p (g d) -> p g d", g=2),
                   cd_k[:, tt * 128:(tt + 1) * 128, :].rearrange("g p d -> p g d"), [], bk(16))
            vc = br(17).rearrange("p (t f) -> p t f", f=256)
            for g in range(2):
                for dup in range(2):
                    P.dma("pool", ("d", "B", 17), vc[:, :, (2 * g + dup) * 64:(2 * g + dup + 1) * 64],
                          cd_v[g].rearrange("(t p) d -> p t d", p=128), writes=bk(17))
            b = nb()
            for tt in range(2):
                tr(b, (tt * 128, (tt + 1) * 128), bf(16)[:, tt * 128:(tt + 1) * 128], bk(16), tt == 1)
            evac(br(16)[:, 256:512], b, (0, 256), bk(16))
        for qblk_ in (0, 1):
            zero_half(qblk_, 64)
        for qblk_ in (2, 3):
            zero_half(qblk_, 0)
        for h in range(8):
            g = h // 4
            half = h % 2
            c = h // 2
            qb_ = 2 * g + (h % 2)
            ld(br(qb_)[g * 64:(g + 1) * 64, :], q_s[h // 2, (h % 2) * 64:(h % 2) * 64 + 64, tok0:tok0 + 512], ["QS"], bk(qb_))
            vcol = 2 * g * 64
            jobs = []
            if kind == "p":
                for sg in range(2):
                    kts = []
                    for kt in range(2):
                        ktile = sg * 2 + kt
                        kts.append((k4[:, ktile * 128:(ktile + 1) * 128], v8[:, ktile, vcol:vcol + 128], None, bk(4) + bk(8, 2)))
                    jobs.append((qb_, half, sg * 256, (sg + 1) * 256, kts))
            else:
                kts = []
                for kt in range(16):
                    kts.append((k4[:, kt * 128:(kt + 1) * 128], v8[:, kt, vcol:vcol + 128], None, bk(4, 4) + bk(8, 8)))
                for tt in range(2):
                    kts.append((br(16)[:, 256 + tt * 128:256 + (tt + 1) * 128], vc[:, tt, vcol:vcol + 128], None, bk(16) + bk(17)))
                jobs.append((qb_, half, 0, 512, kts))
            attention(jobs, 28 + c, pts=(18, 19, 20))
        wide[0] = True
        mixer_out(l, od_w_out, col0, ci)

    out_keys = []

    def zoff_for(kind, tile_idx, pad):
        if kind == "p":
            return [(0, 256, pad), (256, 256, 3 * pad + 256)]
        return [(0, 512, pad + tile_idx * 512)]

    for pp in range(N_PROMPT_PASS):
        bl = pp * 2
        load_x(xp, pp * 512, 0)
        ffn(0, 0, 0, 0)
        if pp == 0:
            mod_layer(1)
        zero_z()
        zo = zoff_for("p", 0, 1)
        even_loop1(0, "p", 0, 0, 0, zo, bl)
        even_loop2(0, "p", 0, 0, 0, zo, 0)
        ffn(0, 1, 0, 0)
        ffn(1, 0, 0, 0)
        zero_z()
        zo = zoff_for("p", 0, 8)
        odd_loop1(1, "p", 0, 0, 0, zo, bl)
        odd_loop2(1, "p", 0, 0, 0, zo)
        ffn(1, 1, 0, 0)
        out_keys.append(store_y(yp, "yp", pp * 512, 0))
        out_keys += [(nm, bl, tt) for nm in ("SAK", "SAV", "SDK", "SDV") for tt in range(4)]
    if DO_SAMPLE:
        for t in range(4):
            load_x(xs, t * 512, t * 512)
        zero_z()
        for t in range(4):
            ffn(0, 0, t * 512, 1)
            even_loop1(0, "s", t * 512, 1, t * 512, zoff_for("s", t, 1), 0)
        for t in range(4):
            even_loop2(0, "s", t * 512, 1, t * 512, zoff_for("s", t, 1), t)
            ffn(0, 1, t * 512, 1)
        zero_z()
        for t in range(4):
            ffn(1, 0, t * 512, 1)
            odd_loop1(1, "s", t * 512, 1, t * 512, zoff_for("s", t, 8), 0)
        for t in range(4):
            odd_loop2(1, "s", t * 512, 1, t * 512, zoff_for("s", t, 8))
            if t > 0:
                out_keys.append(store_y(ys, "ys", (t - 1) * 512, (t - 1) * 512))
            ffn(1, 1, t * 512, 1)
        out_keys.append(store_y(ys, "ys", 3 * 512, 3 * 512))
    P.final_wait("pool", out_keys)
    P.emit()
    return nc


def _na_bias(rpb):
    H = rpb.shape[0]
    r = np.arange(32)
    rs = np.clip(r - 4, 0, 24)
    cq = np.arange(64)
    cs = np.clip(cq - 8, 0, 48)
    out = np.full((H, 4, 8, 128, 512), NEG, np.float32)
    for qt in range(4):
        kts = na_ktiles(qt)
        for j, kt in enumerate(kts):
            kr = (2 * kt + np.arange(2))[:, None, None, None]
            kc = np.arange(64)[None, :, None, None]
            qr = (8 * qt + np.arange(8))[None, None, :, None]
            qc = np.arange(64)[None, None, None, :]
            valid = (kr >= rs[qr]) & (kr < rs[qr] + 8) & (kc >= cs[qc]) & (kc < cs[qc] + 16)
            ro = np.clip(kr - qr + 7, 0, 14)
            co = np.clip(kc - qc + 15, 0, 30)
            ro_b, co_b, valid_b = np.broadcast_arrays(ro, co, valid)
            for h in range(H):
                vals = rpb[h][ro_b, co_b]
                tile = np.where(valid_b, vals, np.float32(NEG)).astype(np.float32)
                out[h, qt, j] = tile.reshape(128, 512)
    return out


def _consts():
    ident = np.eye(128, dtype=np.float32)
    ones = np.ones((128, 128), np.float32)
    o64 = np.zeros((128, 128), np.float32)
    o64[:64, :64] = 1.0
    o64[64:, 64:] = 1.0
    psw = np.zeros((128, 128), np.float32)
    for i in range(64):
        psw[2 * i + 1, 2 * i] = -1.0
        psw[2 * i, 2 * i + 1] = 1.0
    cst = np.stack([ident, ones, o64, psw], axis=1)
    t = np.arange(2048)
    row = (t // 64).astype(np.float32)
    col = (t % 64).astype(np.float32)
    inv = (np.float32(10000.0) ** (-np.arange(16, dtype=np.float32) / np.float32(16))).astype(np.float32)
    ang = np.concatenate([row[:, None] * inv, col[:, None] * inv], axis=-1).astype(np.float32)
    idx = (np.arange(128) % 64) // 2
    cosT = np.cos(ang).astype(np.float32)[:, idx].T.copy()
    sinT = np.sin(ang).astype(np.float32)[:, idx].T.copy()

    def invcnt(L, w):
        tt = np.arange(L)
        lo = np.maximum(tt - w // 2, 0)
        hi = np.minimum(tt + w - w // 2, L)
        return (1.0 / (hi - lo).astype(np.float32)).astype(np.float32)
    invp = np.stack([np.tile(np.tile(invcnt(256, w), 2)[None, :], (128, 1)) for w in (2, 4, 8, 16)]).astype(np.float32)
    invs = np.stack([np.tile(invcnt(2048, w)[None, :], (128, 1)) for w in (2, 4, 8, 16)]).astype(np.float32)
    return cst, ident, cosT, sinT, invp, invs


def kernel(x_prompt, x_sample, cache_a_k, cache_a_v, cache_d_k, cache_d_v, c, c_ctx,
           mod_w, mod_b, norm_w, ffn_w1, ffn_w2,
           ev_w_in, ev_rpb, ev_conv_w, ev_conv_b, ev_w_out,
           od_w_in, od_pool_w, od_pool_scale, od_q_norm, od_k_norm, od_w_out):
    f = lambda a: np.ascontiguousarray(np.asarray(a, dtype=np.float32))
    x_prompt, x_sample = f(x_prompt), f(x_sample)
    cst, ident, cosT, sinT, invp, invs = _consts()
    nab = _na_bias(f(ev_rpb)[0])
    norm_w, mod_b = f(norm_w), f(mod_b)
    base = np.zeros((128, 400), np.float32)
    base[:, 0:96] = norm_w.reshape(2, 6, 8, 128).transpose(3, 0, 1, 2).reshape(128, 96)
    base[:, 96:240] = mod_b.reshape(2, 72, 128).transpose(2, 0, 1).reshape(128, 144)
    base[:, 240:252] = f(ev_conv_w)[0].reshape(3, 4, 128).transpose(2, 0, 1).reshape(128, 12)
    base[:, 252:256] = f(ev_conv_b)[0].reshape(4, 128).T
    base[:, 256:260] = f(od_pool_scale)[0].reshape(4, 128).T
    qn = f(od_q_norm)[0]
    kn = f(od_k_norm)[0]
    base[:, 260] = np.tile(qn, 2)
    base[:, 261] = np.tile(kn, 2)
    base[:, 399] = EPS
    base[:, 398] = 64.0 * EPS
    gk_row = np.tile(kn[None, :], (128, 1)).astype(np.float32)
    pool_w = f(od_pool_w)[0].transpose(1, 0, 2).copy()
    shared = {
        "mod_w": f(mod_w), "ffn_w1": f(ffn_w1), "ffn_w2": f(ffn_w2),
        "ev_w_in": f(ev_w_in)[0], "ev_w_out": f(ev_w_out)[0], "od_w_in": f(od_w_in)[0], "od_w_out": f(od_w_out)[0],
        "pool_w": pool_w, "cst": cst, "identf": ident, "gk_row": gk_row, "nab": nab,
        "cosT": cosT, "sinT": sinT, "invp": invp, "invs": invs,
    }
    c, c_ctx = f(c), f(c_ctx)
    in_maps = []
    for core in range(N_CORES):
        b = core % 2
        sm_ = base.copy()
        cond = np.stack([c_ctx.reshape(8, 128).T, c[b].reshape(8, 128).T], axis=-1)
        sm_[:, 262:278] = cond.reshape(128, 16)
        m = dict(shared)
        m["small"] = sm_
        m["xp"] = x_prompt[4 * core:4 * core + 4].reshape(1024, D)
        m["xs"] = x_sample[b]
        m["ca_k"] = f(cache_a_k)[b, 0]
        m["ca_v"] = f(cache_a_v)[b, 0]
        m["cd_k"] = f(cache_d_k)[b, 0]
        m["cd_v"] = f(cache_d_v)[b, 0]
        in_maps.append(m)
    nc = build_program()
    res = run_bass_kernel_spmd(nc, in_maps, core_ids=list(range(N_CORES)))
    r = res.results
    y_prompt = np.concatenate([r[i]["yp"].reshape(4, 256, D) for i in range(N_CORES)], axis=0)
    y_sample = np.stack([r[0]["ys"], r[1]["ys"]], axis=0)
    sak = np.concatenate([r[i]["sak"] for i in range(N_CORES)], axis=0)[:, None]
    sav = np.concatenate([r[i]["sav"] for i in range(N_CORES)], axis=0)[:, None]
    sdk = np.concatenate([r[i]["sdk"] for i in range(N_CORES)], axis=0)[:, None]
    sdv = np.concatenate([r[i]["sdv"] for i in range(N_CORES)], axis=0)[:, None]
    return (y_prompt.astype(np.float32), y_sample.astype(np.float32), sak.astype(np.float32),
            sav.astype(np.float32), sdk.astype(np.float32), sdv.astype(np.float32))
```

```python
import numpy as np
import concourse.bass as bass
import concourse.mybir as mybir
from concourse.bass_utils import run_bass_kernel_spmd

F32 = mybir.dt.float32
F32R = mybir.dt.float32r
ALU = mybir.AluOpType
AF = mybir.ActivationFunctionType

D = 1024
DFF = 2816
NEG = -1e30
EPS = 1e-6
NB = 32
N_PROMPT_PASS = 2
DO_SAMPLE = True
N_CORES = 8


class Prog:
    def __init__(self, nc):
        self.nc = nc
        self.ops = {e: [] for e in ("pe", "act", "dve", "pool", "sp")}
        self.sems = {}
        self.cnt = {}
        self.known = {e: {} for e in self.ops}
        self.last_w = {}
        self.readers = {}
        self.pe_pending = False

    def _sem(self, key):
        if key not in self.sems:
            self.sems[key] = self.nc.alloc_semaphore("s_" + str(key))
            self.cnt[key] = 0
        return self.sems[key]

    def _deps(self, eng, reads, writes):
        need = {}

        def add(st):
            if st is None:
                return
            k, v = st
            if need.get(k, 0) < v:
                need[k] = v
        for r in reads:
            add(self.last_w.get(r))
        for w in writes:
            add(self.last_w.get(w))
            for st in self.readers.get(w, ()):
                add(st)
        waits = []
        for k, v in need.items():
            if k == "pe" and eng == "pe":
                continue
            if self.known[eng].get(k, 0) >= v:
                continue
            self.known[eng][k] = v
            waits.append((k, v))
        return waits

    def _record(self, stamp, reads, writes):
        for r in reads:
            self.readers.setdefault(r, []).append(stamp)
        for w in writes:
            self.last_w[w] = stamp
            self.readers[w] = []

    def op(self, eng, fn, reads=(), writes=(), signal=True):
        waits = self._deps(eng, reads, writes)
        self._sem(eng)
        if signal:
            self.cnt[eng] += 1
            stamp = (eng, self.cnt[eng])
            inc = (eng, 1)
            if eng == "pe":
                self.pe_pending = False
        else:
            assert eng == "pe"
            stamp = (eng, self.cnt[eng] + 1)
            inc = None
            self.pe_pending = True
        self.ops[eng].append((waits, fn, inc))
        self._record(stamp, reads, writes)

    def dma(self, q, semkey, out, in_, reads=(), writes=(), **kw):
        if semkey is None:
            semkey = ("d",) + tuple(writes[0]) if isinstance(writes[0], tuple) else ("d", writes[0])
        waits = self._deps(q, reads, writes)
        self._sem(semkey)
        self.cnt[semkey] += 16
        stamp = (semkey, self.cnt[semkey])
        self.ops[q].append((waits, lambda e: e.dma_start(out=out, in_=in_, **kw), (semkey, 16)))
        self._record(stamp, reads, writes)

    def final_wait(self, eng, keys):
        waits = self._deps(eng, keys, ())
        self.ops[eng].append((waits, None, None))

    def emit(self):
        nc = self.nc
        assert not self.pe_pending
        with nc.Block() as blk:
            def run(name):
                def _f(e):
                    for waits, fn, inc in self.ops[name]:
                        for k, v in waits:
                            e.wait_ge(self.sems[k], v)
                        if fn is None:
                            continue
                        ins = fn(e)
                        if inc is not None:
                            ins.then_inc(self.sems[inc[0]], inc[1])
                return _f
            blk.tensor(run("pe"))
            blk.scalar(run("act"))
            blk.vector(run("dve"))
            blk.gpsimd(run("pool"))
            blk.sync(run("sp"))


def L(method, **kw):
    return lambda e: getattr(e, method)(**kw)


def na_ktiles(qt):
    lo = min(max(8 * qt - 4, 0), 24)
    hi = min(max(8 * qt + 3, 0), 24) + 7
    return list(range(lo // 2, hi // 2 + 1))


def build_program():
    nc = bass.Bass("TRN2", target_bir_lowering=False)
    nc.dge_precook = False
    P = Prog(nc)

    def din(name, shape, dt=F32):
        return nc.dram_tensor(name, list(shape), dt, kind="ExternalInput").ap()

    def dout(name, shape):
        return nc.dram_tensor(name, list(shape), F32, kind="ExternalOutput").ap()

    def dint(name, shape, dt=F32):
        return nc.dram_tensor(name, list(shape), dt).ap()

    xp = din("xp", [1024, D], F32R)
    xs = din("xs", [2048, D], F32R)
    ca_k = din("ca_k", [8, 256, 64], F32R)
    ca_v = din("ca_v", [8, 256, 64], F32R)
    cd_k = din("cd_k", [2, 256, 64], F32R)
    cd_v = din("cd_v", [2, 256, 64], F32R)
    mod_w = din("mod_w", [2, D, 9 * D], F32R)
    w1_d = din("ffn_w1", [2, 2, D, 2 * DFF], F32R)
    w2_d = din("ffn_w2", [2, 2, DFF, D], F32R)
    ev_w_in = din("ev_w_in", [D, 3072], F32R)
    ev_w_out = din("ev_w_out", [D, D], F32R)
    od_w_in = din("od_w_in", [D, 1280], F32R)
    od_w_out = din("od_w_out", [D, D], F32R)
    pool_w_d = din("pool_w", [128, 4, 128], F32R)
    cst_d = din("cst", [128, 4, 128], F32R)
    identf_d = din("identf", [128, 128])
    small_d = din("small", [128, 400])
    gk_row_d = din("gk_row", [128, 64])
    nab_d = din("nab", [8, 4, 8, 128, 512], F32R)
    cos_d = din("cosT", [128, 2048], F32R)
    sin_d = din("sinT", [128, 2048], F32R)
    invp_d = din("invp", [4, 128, 512], F32R)
    invs_d = din("invs", [4, 128, 2048], F32R)

    yp = dout("yp", [1024, D])
    ys = dout("ys", [2048, D])
    sak = dout("sak", [4, 8, 256, 64])
    sav = dout("sav", [4, 8, 256, 64])
    sdk = dout("sdk", [4, 2, 256, 64])
    sdv = dout("sdv", [4, 2, 256, 64])

    LP = 2048 + 32
    q_s = dint("q_s", [8, 128, 2048], F32R)
    k_s = dint("k_s", [512, 2048], F32R)
    v_s = dint("v_s", [2048, 512], F32R)
    z_s = dint("z_s", [512, LP], F32R)
    g_s = dint("g_s", [512, 2048], F32R)

    X = nc.alloc_sbuf_tensor("X", [128, 8, 2048], F32)
    Hb = nc.alloc_sbuf_tensor("Hb", [128, 8, 512], F32R)
    WS = nc.alloc_sbuf_tensor("WS", [128, 3, 4096], F32R)
    BIG = nc.alloc_sbuf_tensor("BIG", [128, NB, 512], F32)
    CST = nc.alloc_sbuf_tensor("CST", [128, 4, 128], F32R)
    IDF = nc.alloc_sbuf_tensor("IDF", [128, 128], F32)
    SM = nc.alloc_sbuf_tensor("SM", [128, 400], F32)
    GKR = nc.alloc_sbuf_tensor("GKR", [128, 64], F32)
    PW = nc.alloc_sbuf_tensor("PW", [128, 4, 128], F32R)
    SC = nc.alloc_sbuf_tensor("SC", [128, 16], F32R)
    MODT = nc.alloc_sbuf_tensor("MODT", [128, 2, 72, 2], F32)
    COEF = nc.alloc_sbuf_tensor("COEF", [128, 2, 9, 8, 2], F32)
    TMPS = nc.alloc_sbuf_tensor("TMPS", [128, 8], F32)
    RS = nc.alloc_sbuf_tensor("RS", [128, 2, 512], F32)
    ZT = nc.alloc_sbuf_tensor("ZT", [128, 520], F32)
    ps = [nc.alloc_psum_tensor(f"ps{i}", [128, 512], F32) for i in range(8)]

    IDR = CST[:, 0, :]
    ONES = CST[:, 1, :]
    ONES64 = CST[:, 2, :]
    PSWAP = CST[:, 3, :]
    Hf = Hb[:].bitcast(F32)
    KC = ["CONST"]

    NORMW, MODB, CONVW, CONVB, PSCALE, QN, KN, CONDC = 0, 96, 240, 252, 256, 260, 261, 262
    EPS64C, EPSC = 398, 399

    def sm(col, n=1):
        return SM[:, col:col + n]

    def bf(i):
        return BIG[:, i, :]

    def br(i):
        return BIG[:, i, :].bitcast(F32R)

    def bfs(i, n):
        return BIG[:, i:i + n, :]

    def brs(i, n):
        return BIG[:, i:i + n, :].bitcast(F32R)

    def flat(ap3):
        return ap3.rearrange("p a b -> p (a b)")

    def bk(i, n=1):
        return [("B", j) for j in range(i, i + n)]

    bank_ctr = [0]

    def nb():
        b = bank_ctr[0] % 4
        bank_ctr[0] += 1
        return b

    ws_ctr = [0]

    def wstage():
        s = ws_ctr[0] % 3
        ws_ctr[0] += 1
        return s

    def mm(bank, cols, lhsT, rhs, start, stop, reads, signal=None):
        if signal is None:
            signal = stop
        P.op("pe", L("matmul", out=ps[bank][:, cols[0]:cols[1]], lhsT=lhsT, rhs=rhs, start=start, stop=stop),
             reads=reads, writes=[("ps", bank)], signal=signal)

    def tr(bank, cols, in_, reads, signal):
        P.op("pe", L("transpose", out=ps[bank][:, cols[0]:cols[1]], in_=in_, identity=IDF[:]),
             reads=reads + KC, writes=[("ps", bank)], signal=signal)

    evac_ctr = [0]

    def evac(out, bank, cols, writes, scale=None, eng=None, rows=(0, 128)):
        src = ps[bank][rows[0]:rows[1], cols[0]:cols[1]]
        if eng is None:
            eng = "act" if evac_ctr[0] % 2 == 0 else "dve"
            evac_ctr[0] += 1
        if eng == "act":
            kw = dict(out=out, in_=src, func=AF.Copy)
            if scale is not None:
                kw["scale"] = scale
            P.op("act", L("activation", **kw), reads=[("ps", bank)], writes=writes)
        else:
            if scale is None:
                P.op("dve", L("tensor_copy", out=out, in_=src), reads=[("ps", bank)], writes=writes)
            else:
                P.op("dve", L("tensor_scalar", out=out, in0=src, scalar1=scale, scalar2=None, op0=ALU.mult),
                     reads=[("ps", bank)], writes=writes)

    def dve(method, reads, writes, **kw):
        P.op("dve", L(method, **kw), reads=reads, writes=writes)

    def act(reads, writes, **kw):
        P.op("act", L("activation", **kw), reads=reads, writes=writes)

    def ld(out, in_, reads, writes):
        P.dma("pool", None, out, in_, reads=reads, writes=writes)

    P.dma("pool", "c0", CST[:], cst_d, writes=KC)
    P.dma("pool", "c0", IDF[:], identf_d, writes=KC)
    P.dma("pool", "c0", SM[:], small_d, writes=KC)
    P.dma("pool", "c0", GKR[:], gk_row_d, writes=KC)
    P.dma("pool", "c0", PW[:], pool_w_d, writes=KC)

    P.op("pool", L("memset", ap=ZT[:], constant=0.0), writes=["ZT"])

    def zero_z():
        for r in range(4):
            for q4 in range(4):
                P.dma("pool", "zs", z_s[r * 128:(r + 1) * 128, q4 * 520:(q4 + 1) * 520], ZT[:].bitcast(F32R),
                      reads=["ZT"], writes=["ZS"])

    def zero_half(blk_id, r0):
        dve("tensor_copy", ["ZT"], bk(blk_id), out=br(blk_id)[r0:r0 + 64, :], in_=ZT[r0:r0 + 64, 0:512])

    act(KC, ["SC"], out=SC[:], in_=sm(CONDC, 16), func=AF.Silu)
    SC3 = SC[:].rearrange("p (k n) -> p k n", n=2)
    def mod_layer(l):
        for nblk in range(18):
            st = wstage()
            wv = WS[:, st, :].rearrange("p (k n) -> p k n", k=8)
            P.dma("sp", ("ws", st), wv, mod_w[l, :, nblk * 512:(nblk + 1) * 512].rearrange("(k p) n -> p k n", p=128),
                  writes=[("W", st)])
            for mi in range(4):
                j = nblk * 4 + mi
                b = nb()
                for kc in range(8):
                    mm(b, (0, 2), wv[:, kc, mi * 128:(mi + 1) * 128], SC3[:, kc, :], kc == 0, kc == 7,
                       reads=[("W", st), "SC"])
                dve("tensor_scalar", [("ps", b)] + KC, [f"MODT{l}"], out=MODT[:, l, j, :], in0=ps[b][:, 0:2],
                    scalar1=sm(MODB + l * 72 + j), scalar2=None, op0=ALU.add)
        for ci in range(2):
            for sub in range(3):
                sh, scl, gt = 3 * sub, 3 * sub + 1, 3 * sub + 2
                gpre, gpost = 2 * sub, 2 * sub + 1
                half = 0.5 if sub != 1 else 1.0
                gw_pre = sm(NORMW + l * 48 + gpre * 8, 8)
                gw_post = sm(NORMW + l * 48 + gpost * 8, 8)
                dve("scalar_tensor_tensor", [f"MODT{l}"] + KC, [f"COEF{l}"], out=COEF[:, l, 3 * sub, :, ci],
                    in0=MODT[:, l, scl * 8:(scl + 1) * 8, ci], scalar=1.0, in1=gw_pre, op0=ALU.add, op1=ALU.mult)
                dve("tensor_copy", [f"MODT{l}"], [f"COEF{l}"], out=COEF[:, l, 3 * sub + 1, :, ci],
                    in_=MODT[:, l, sh * 8:(sh + 1) * 8, ci])
                dve("scalar_tensor_tensor", [f"MODT{l}"] + KC, [f"COEF{l}"], out=COEF[:, l, 3 * sub + 2, :, ci],
                    in0=MODT[:, l, gt * 8:(gt + 1) * 8, ci], scalar=half, in1=gw_post, op0=ALU.mult, op1=ALU.mult)

    mod_layer(0)

    def coef(l, k, c, ci):
        return COEF[:, l, k, c, ci:ci + 1]

    SQ0 = 24
    RSTD = 32

    def Xc(c, col0):
        return X[:, c, col0:col0 + 512]

    def stats(src, skey, nch=8, sq0=None, presq=False):
        sq0 = SQ0 if sq0 is None else sq0
        if not presq:
            for c in range(nch):
                act([skey(c)], bk(sq0 + c), out=br(sq0 + c), in_=src(c), func=AF.Square)
        b = nb()
        for c in range(nch):
            mm(b, (0, 512), ONES, br(sq0 + c), c == 0, c == nch - 1, reads=bk(sq0 + c) + KC)
        act([("ps", b)] + KC, ["RSTD"], out=RS[:, 0, :], in_=ps[b][:], func=AF.Ln, scale=1.0 / 1024.0, bias=sm(EPSC))
        act(["RSTD"], ["RSTD"], out=RS[:, 0, :], in_=RS[:, 0, :], func=AF.Exp, scale=-0.5)

    def pre_norm(l, sub, col0, ci):
        stats(lambda c: Xc(c, col0), lambda c: ("X", c, col0))
        for c in range(8):
            dve("tensor_tensor", [("X", c, col0)] + ["RSTD"], bk(SQ0 + c), out=br(SQ0 + c), in0=Xc(c, col0),
                in1=RS[:, 0, :], op=ALU.mult)
            act(bk(SQ0 + c) + [f"COEF{l}"], [("H", c)], out=Hb[:, c, :], in_=bf(SQ0 + c), func=AF.Identity,
                scale=coef(l, 3 * sub, c, ci), bias=coef(l, 3 * sub + 1, c, ci))

    def post_norm(l, sub, col0, ci, sq0=None):
        stats(lambda c: Hf[:, c, :], lambda c: ("H", c), sq0=sq0, presq=(sq0 is not None))
        for c in range(8):
            dve("tensor_tensor", [("H", c), "RSTD"], bk(SQ0 + c), out=br(SQ0 + c), in0=Hf[:, c, :], in1=RS[:, 0, :],
                op=ALU.mult)
            dve("scalar_tensor_tensor", bk(SQ0 + c) + [f"COEF{l}", ("X", c, col0)], [("X", c, col0)], out=Xc(c, col0),
                in0=bf(SQ0 + c), scalar=coef(l, 3 * sub + 2, c, ci), in1=Xc(c, col0), op0=ALU.mult, op1=ALU.add)

    def evac_sq(m, bank, sqblk):
        P.op("dve", L("tensor_copy", out=Hb[:, m, :], in_=ps[bank][:]), reads=[("ps", bank)], writes=[("H", m)])
        act([("H", m)], bk(sqblk), out=br(sqblk), in_=Hf[:, m, :], func=AF.Square)

    def ffn(l, s, col0, ci):
        sub = 0 if s == 0 else 2
        pre_norm(l, sub, col0, ci)
        w1 = w1_d[l, s]
        w2 = w2_d[l, s]
        for jb in range(11):
            st = wstage()
            wv = WS[:, st, :].rearrange("p (k n) -> p k n", k=8)
            P.dma("sp", ("ws", st), wv[:, :, 0:256], w1[:, jb * 256:(jb + 1) * 256].rearrange("(k p) n -> p k n", p=128),
                  writes=[("W", st)])
            P.dma("sp", ("ws", st), wv[:, :, 256:512],
                  w1[:, DFF + jb * 256:DFF + (jb + 1) * 256].rearrange("(k p) n -> p k n", p=128), writes=[("W", st)])
            for jj in range(2):
                j = jb * 2 + jj
                bg_, bu_ = nb(), nb()
                for kc in range(8):
                    mm(bg_, (0, 512), wv[:, kc, jj * 128:(jj + 1) * 128], Hb[:, kc, :], kc == 0, kc == 7,
                       reads=[("W", st), ("H", kc)])
                for kc in range(8):
                    mm(bu_, (0, 512), wv[:, kc, 256 + jj * 128:256 + (jj + 1) * 128], Hb[:, kc, :], kc == 0, kc == 7,
                       reads=[("W", st), ("H", kc)])
                tb = 22 + (j % 2)
                act([("ps", bg_)], bk(tb), out=br(tb), in_=ps[bg_][:], func=AF.Silu)
                dve("tensor_tensor", [("ps", bu_)] + bk(tb), bk(j), out=br(j), in0=bf(tb), in1=ps[bu_][:], op=ALU.mult)
        for mb in range(8):
            st = wstage()
            wv = WS[:, st, 0:2816].rearrange("p (j n) -> p j n", j=22)
            P.dma("sp", ("ws", st), wv, w2[:, mb * 128:(mb + 1) * 128].rearrange("(j p) n -> p j n", p=128),
                  writes=[("W", st)])
            b = nb()
            for j in range(22):
                mm(b, (0, 512), wv[:, j, :], br(j), j == 0, j == 21, reads=[("W", st)] + bk(j))
            evac(Hb[:, mb, :], b, (0, 512), [("H", mb)])
        post_norm(l, sub, col0, ci)

    def stg_view(r=False):
        v = BIG[:, 0:8, :]
        if r:
            v = v.bitcast(F32R)
        return v.rearrange("p (t h) n -> p t (h n)", t=4)

    def load_x(src, r0, col0):
        stg = stg_view()
        P.dma("pool", "xin", stg_view(True), src[r0:r0 + 512, :].rearrange("(t p) d -> p t d", p=128), writes=bk(0, 8))
        for c in range(8):
            b = nb()
            for tt in range(4):
                tr(b, (tt * 128, (tt + 1) * 128), stg[:, tt, c * 128:(c + 1) * 128], bk(2 * tt, 2), tt == 3)
            evac(Xc(c, col0), b, (0, 512), [("X", c, col0)])

    def store_y(dst, name, r0, col0):
        stg = stg_view()
        for tt in range(4):
            for hf in range(2):
                b = nb()
                for i in range(4):
                    c = hf * 4 + i
                    tr(b, (i * 128, (i + 1) * 128), X[:, c, col0 + tt * 128:col0 + (tt + 1) * 128], [("X", c, col0)], i == 3)
                evac(stg_view(True)[:, tt, hf * 512:(hf + 1) * 512], b, (0, 512), bk(2 * tt + hf))
        key = ("Y", name, r0)
        P.dma("pool", "yout", dst[r0:r0 + 512, :].rearrange("(t p) d -> p t d", p=128), stg, reads=bk(0, 8), writes=[key])
        return key

    def proj_fm(wv, col_lo, wkey):
        b = nb()
        for kc in range(8):
            mm(b, (0, 512), wv[:, kc, col_lo:col_lo + 128], Hb[:, kc, :], kc == 0, kc == 7, reads=[wkey, ("H", kc)])
        return b

    def proj_tm(wv, col_lo, ncol, tt, wkey):
        b = nb()
        for kc in range(8):
            mm(b, (0, ncol), Hb[:, kc, tt * 128:(tt + 1) * 128], wv[:, kc, col_lo:col_lo + ncol], kc == 0, kc == 7,
               reads=[wkey, ("H", kc)])
        return b

    def load_wblock(w, col_lo, ncol):
        st = wstage()
        wv = WS[:, st, 0:8 * ncol].rearrange("p (k n) -> p k n", k=8)
        P.dma("sp", ("ws", st), wv, w[:, col_lo:col_lo + ncol].rearrange("(k p) n -> p k n", p=128), writes=[("W", st)])
        return wv, ("W", st)

    bias_ctr = [0]
    acc_ctr = [0]

    def attention(jobs, mix_blk, pts=(18, 19), bias_blks=(20, 21, 22)):
        npt = len(pts)
        for ji, (qblk, half, q0, q1, ktiles) in enumerate(jobs):
            n = q1 - q0
            ob, db = (4, 5) if acc_ctr[0] % 2 == 0 else (6, 7)
            acc_ctr[0] += 1
            nk = len(ktiles)
            bias_blk = {}
            sbank = {}

            def prefetch(ki):
                if ki < nk and ktiles[ki][2] is not None:
                    bb = bias_blks[bias_ctr[0] % len(bias_blks)]
                    bias_ctr[0] += 1
                    ld(br(bb), ktiles[ki][2], [], bk(bb))
                    bias_blk[ki] = bb

            def score(ki):
                if ki >= nk:
                    return
                kT, vv, bias, rkeys = ktiles[ki]
                b = nb()
                sbank[ki] = b
                mm(b, (0, n), kT, br(qblk)[:, q0:q1], True, bias is None, reads=rkeys + bk(qblk))
                if bias is not None:
                    bb = bias_blk[ki]
                    mm(b, (0, n), IDR, br(bb)[:, 0:n], False, True, reads=bk(bb) + KC)
            prefetch(0)
            prefetch(1)
            score(0)
            score(1)
            for ki, (kT, vv, bias, rkeys) in enumerate(ktiles):
                prefetch(ki + 2)
                b = sbank[ki]
                pt = pts[ki % npt]
                act([("ps", b)], bk(pt), out=br(pt)[:, 0:n], in_=ps[b][:, 0:n], func=AF.Exp)
                score(ki + 2)
                mm(ob, (0, n), vv, br(pt)[:, 0:n], ki == 0, ki == nk - 1, reads=rkeys + bk(pt))
                mm(db, (0, n), ONES, br(pt)[:, 0:n], ki == 0, ki == nk - 1, reads=bk(pt) + KC, signal=True)
            r0, r1 = half * 64, half * 64 + 64
            dve("reciprocal", [("ps", db)], ["RDEN"], out=RS[r0:r1, 1, 0:n], in_=ps[db][r0:r1, 0:n])
            dve("tensor_tensor", [("ps", ob), "RDEN"], bk(mix_blk), out=br(mix_blk)[r0:r1, q0:q1],
                in0=ps[ob][r0:r1, 0:n], in1=RS[r0:r1, 1, 0:n], op=ALU.mult)

    def mixer_out(l, w_out, col0, ci):
        for nblk in range(2):
            wv, wk = load_wblock(w_out, nblk * 512, 512)
            for mi in range(4):
                m = nblk * 4 + mi
                b = nb()
                for kc in range(8):
                    mm(b, (0, 512), wv[:, kc, mi * 128:(mi + 1) * 128], br(24 + kc), kc == 0, kc == 7, reads=[wk] + bk(24 + kc))
                evac_sq(m, b, m)
        post_norm(l, 1, col0, ci, sq0=0)

    def scr_w(dst, src, reads, key):
        P.dma("pool", ("scr", key), dst, src, reads=reads, writes=[key])

    def even_loop1(l, kind, col0, ci, tok0, zoff, bl):
        pre_norm(l, 1, col0, ci)
        wv, wk = load_wblock(ev_w_in, 0, 512)
        for m in range(4):
            b = proj_fm(wv, m * 128, wk)
            evac(br(m), b, (0, 512), bk(m), scale=0.125)
        scr_w(q_s[0:4, :, tok0:tok0 + 512].rearrange("c p t -> p c t"), brs(0, 4), bk(0, 4), "QS")
        wv, wk = load_wblock(ev_w_in, 512, 512)
        for m in range(4):
            b = proj_fm(wv, m * 128, wk)
            evac(br(4 + m), b, (0, 512), bk(4 + m))
        scr_w(k_s[:, tok0:tok0 + 512].rearrange("(c p) t -> p c t", p=128), brs(4, 4), bk(4, 4), "KS")
        if kind == "p":
            for tt in range(4):
                b = proj_tm(wv, 0, 512, tt, wk)
                evac(br(8 + tt), b, (0, 512), bk(8 + tt))
            for tt in range(4):
                P.dma("pool", ("so", "ak", tt), sak[bl + tt // 2][:, (tt % 2) * 128:(tt % 2 + 1) * 128, :].rearrange("h p d -> p h d"),
                      bf(8 + tt).rearrange("p (h d) -> p h d", d=64), reads=bk(8 + tt), writes=[("SAK", bl, tt)])
        wv, wk = load_wblock(ev_w_in, 1024, 512)
        for tt in range(4):
            b = proj_tm(wv, 0, 512, tt, wk)
            evac(br(12 + tt), b, (0, 512), bk(12 + tt))
        scr_w(v_s[tok0:tok0 + 512, :].rearrange("(t p) f -> p t f", p=128), brs(12, 4), bk(12, 4), "VS")
        if kind == "p":
            for tt in range(4):
                P.dma("pool", ("so", "av", tt), sav[bl + tt // 2][:, (tt % 2) * 128:(tt % 2 + 1) * 128, :].rearrange("h p d -> p h d"),
                      bf(12 + tt).rearrange("p (h d) -> p h d", d=64), reads=bk(12 + tt), writes=[("SAV", bl, tt)])
        wv, wk = load_wblock(ev_w_in, 1536, 512)
        for m in range(4):
            b = proj_fm(wv, m * 128, wk)
            evac(br(16 + m), b, (0, 512), bk(16 + m))
        scr_w(g_s[:, tok0:tok0 + 512].rearrange("(c p) t -> p c t", p=128), brs(16, 4), bk(16, 4), "GS")
        wv, wk = load_wblock(ev_w_in, 2048, 512)
        for m in range(4):
            b = proj_fm(wv, m * 128, wk)
            evac(br(8 + m), b, (0, 512), bk(8 + m))
        wv, wk = load_wblock(ev_w_in, 2560, 512)
        for m in range(4):
            b = proj_fm(wv, m * 128, wk)
            dve("tensor_tensor", [("ps", b)] + bk(8 + m), bk(8 + m), out=br(8 + m), in0=bf(8 + m), in1=ps[b][:], op=ALU.mult)
        for (seg_tok, seg_len, zcol) in zoff:
            scr_w(z_s[:, zcol:zcol + seg_len].rearrange("(c p) t -> p c t", p=128),
                  BIG[:, 8:12, seg_tok:seg_tok + seg_len].bitcast(F32R), bk(8, 4), "ZS")

    def even_loop2(l, kind, col0, ci, tok0, zoff, qt):
        kts_q = na_ktiles(qt)
        klo, nkt = kts_q[0], len(kts_q)

        def conv_chunk(c):
            ld(br(15), g_s[c * 128:(c + 1) * 128, tok0:tok0 + 512], ["GS"], bk(15))
            mixb = 28 + c
            for (seg_tok, seg_len, zcol) in zoff:
                zl = seg_len + 2
                ztr = flat(brs(16, 2))[:, 0:zl]
                zt = flat(bfs(16, 2))[:, 0:zl]
                zk = bk(16, 2)
                ld(ztr, z_s[c * 128:(c + 1) * 128, zcol - 1:zcol - 1 + zl], ["ZS"], zk)
                acc = bf(mixb)[:, seg_tok:seg_tok + seg_len]
                accr = br(mixb)[:, seg_tok:seg_tok + seg_len]
                dve("tensor_scalar", zk + KC, bk(mixb), out=accr, in0=zt[:, 0:seg_len], scalar1=sm(CONVW + c), scalar2=None,
                    op0=ALU.mult)
                for jx in (1, 2):
                    dve("scalar_tensor_tensor", zk + KC + bk(mixb), bk(mixb), out=accr, in0=zt[:, jx:jx + seg_len],
                        scalar=sm(CONVW + jx * 4 + c), in1=acc, op0=ALU.mult, op1=ALU.add)
            dve("scalar_tensor_tensor", bk(mixb) + bk(15) + KC, bk(mixb), out=br(mixb), in0=bf(mixb), scalar=sm(CONVB + c),
                in1=bf(15), op0=ALU.add, op1=ALU.mult)

        for qa_, qb_ in ((0, 1), (2, 3)):
            zero_half(qa_, 64)
            zero_half(qb_, 0)

        def chunk_loads(c):
            sset = c % 2
            qa, qb = (0, 1) if sset == 0 else (2, 3)
            kb0, vb0 = 4 + 2 * sset, 8 + 2 * sset
            cx0, cx1 = (12, 13) if sset == 0 else (14, 23)
            ld(br(qa)[0:64, :], q_s[c, 0:64, tok0:tok0 + 512], ["QS"], bk(qa))
            ld(br(qb)[64:128, :], q_s[c, 64:128, tok0:tok0 + 512], ["QS"], bk(qb))
            if kind == "p":
                ld(br(kb0), k_s[c * 128:(c + 1) * 128, tok0:tok0 + 512], ["KS"], bk(kb0))
                v8 = br(vb0).rearrange("p (t f) -> p t f", f=128)
                ld(v8, v_s[tok0:tok0 + 512, c * 128:(c + 1) * 128].rearrange("(t p) f -> p t f", p=128), ["VS"], bk(vb0))
            else:
                k2 = flat(brs(kb0, 2))[:, 0:nkt * 128]
                v2 = brs(vb0, 2).rearrange("p a (t f) -> p (a t) f", f=128)[:, 0:nkt, :]
                ld(k2, k_s[c * 128:(c + 1) * 128, klo * 128:(klo + nkt) * 128], ["KS"], bk(kb0, 2))
                ld(v2, v_s[klo * 128:(klo + nkt) * 128, c * 128:(c + 1) * 128].rearrange("(t p) f -> p t f", p=128), ["VS"],
                   bk(vb0, 2))
                for tt in range(2):
                    ld(br(cx0)[:, tt * 128:(tt + 1) * 128].rearrange("p (h d) -> p h d", h=2),
                       ca_k[2 * c:2 * c + 2, tt * 128:(tt + 1) * 128, :].rearrange("h p d -> p h d"), [], bk(cx0))
                    ld(br(cx1)[:, tt * 128:(tt + 1) * 128].rearrange("p (h d) -> p h d", h=2),
                       ca_v[2 * c:2 * c + 2, tt * 128:(tt + 1) * 128, :].rearrange("h p d -> p h d"), [], bk(cx1))

        chunk_loads(0)
        for c in range(4):
            if c + 1 < 4:
                chunk_loads(c + 1)
            sset = c % 2
            qa, qb = (0, 1) if sset == 0 else (2, 3)
            kb0, vb0 = 4 + 2 * sset, 8 + 2 * sset
            cx0, cx1 = (12, 13) if sset == 0 else (14, 23)
            jobs = []
            if kind == "p":
                v8 = br(vb0).rearrange("p (t f) -> p t f", f=128)
                for half in range(2):
                    for sg in range(2):
                        kts = []
                        for kt in range(2):
                            ktile = sg * 2 + kt
                            kts.append((br(kb0)[:, ktile * 128:(ktile + 1) * 128], v8[:, ktile, :], None, bk(kb0) + bk(vb0)))
                        jobs.append((qa if half == 0 else qb, half, sg * 256, (sg + 1) * 256, kts))
            else:
                k2 = flat(brs(kb0, 2))[:, 0:nkt * 128]
                v2 = brs(vb0, 2).rearrange("p a (t f) -> p (a t) f", f=128)[:, 0:nkt, :]
                b = nb()
                for tt in range(2):
                    tr(b, (tt * 128, (tt + 1) * 128), bf(cx0)[:, tt * 128:(tt + 1) * 128], bk(cx0), tt == 1)
                evac(br(cx1)[:, 256:512], b, (0, 256), bk(cx1))
                vc = br(cx1)[:, 0:256].rearrange("p (t f) -> p t f", f=128)
                for half in range(2):
                    h = 2 * c + half
                    kts = []
                    for j in range(nkt):
                        kts.append((k2[:, j * 128:(j + 1) * 128], v2[:, j, :], nab_d[h, qt, j], bk(kb0, 2) + bk(vb0, 2)))
                    for tt in range(2):
                        kts.append((br(cx1)[:, 256 + tt * 128:256 + (tt + 1) * 128], vc[:, tt, :], None, bk(cx1)))
                    jobs.append((qa if half == 0 else qb, half, 0, 512, kts))
            attention(jobs, 24 + c)
            conv_chunk(c)
        mixer_out(l, ev_w_out, col0, ci)

    def qk_norm_rope(blks, gcol, rope, qscale):
        nbk = len(blks)
        banks = []
        for i, bid in enumerate(blks):
            act(bk(bid), bk(SQ0 + i), out=br(SQ0 + i), in_=bf(bid), func=AF.Square)
        for i, bid in enumerate(blks):
            b = nb()
            banks.append(b)
            mm(b, (0, 512), ONES64, br(SQ0 + i), True, True, reads=bk(SQ0 + i) + KC)
            rk = ["RSTD", "RDEN"][i]
            if qscale:
                act([("ps", b)] + KC, [rk], out=RS[:, i, :], in_=ps[b][:], func=AF.Ln, scale=1.0, bias=sm(EPS64C))
            else:
                act([("ps", b)] + KC, [rk], out=RS[:, i, :], in_=ps[b][:], func=AF.Ln, scale=1.0 / 64.0, bias=sm(EPSC))
            act([rk], [rk], out=RS[:, i, :], in_=RS[:, i, :], func=AF.Exp, scale=-0.5)
        assert nbk <= 2
        rkeys = ["RSTD", "RDEN"][:nbk]
        for i, bid in enumerate(blks):
            if not rope:
                dve("scalar_tensor_tensor", bk(bid) + rkeys + KC, bk(bid), out=br(bid), in0=bf(bid), scalar=sm(gcol),
                    in1=RS[:, i, :], op0=ALU.mult, op1=ALU.mult)
                continue
            s1, t1, t2 = SQ0 + i, 15 + i, 19 + i
            dve("scalar_tensor_tensor", bk(bid) + rkeys + KC, bk(s1), out=br(s1), in0=bf(bid), scalar=sm(gcol),
                in1=RS[:, i, :], op0=ALU.mult, op1=ALU.mult)
            b = nb()
            mm(b, (0, 512), PSWAP, br(s1), True, True, reads=bk(s1) + KC)
            dve("tensor_tensor", bk(s1) + bk(12), bk(t1), out=br(t1), in0=bf(s1), in1=bf(12), op=ALU.mult)
            dve("tensor_tensor", [("ps", b)] + bk(13), bk(t2), out=br(t2), in0=ps[b][:], in1=bf(13), op=ALU.mult)
            dve("tensor_tensor", bk(t1) + bk(t2), bk(bid), out=br(bid), in0=bf(t1), in1=bf(t2), op=ALU.add)

    def odd_loop1(l, kind, col0, ci, tok0, zoff, bl):
        pre_norm(l, 1, col0, ci)
        rope = (kind == "s")
        if rope:
            ld(br(12), cos_d[:, tok0:tok0 + 512], [], bk(12))
            ld(br(13), sin_d[:, tok0:tok0 + 512], [], bk(13))
        wv, wk = load_wblock(od_w_in, 0, 512)
        for m in range(4):
            b = proj_fm(wv, m * 128, wk)
            evac(br(8 + m), b, (0, 512), bk(8 + m))
        for (seg_tok, seg_len, zcol) in zoff:
            scr_w(z_s[:, zcol:zcol + seg_len].rearrange("(c p) t -> p c t", p=128),
                  BIG[:, 8:12, seg_tok:seg_tok + seg_len].bitcast(F32R), bk(8, 4), "ZS")
        wv, wk = load_wblock(od_w_in, 512, 512)
        for m in range(4):
            b = proj_fm(wv, m * 128, wk)
            evac(br(m), b, (0, 512), bk(m))
        wv, wk = load_wblock(od_w_in, 1024, 256)
        b = proj_fm(wv, 0, wk)
        evac(br(4), b, (0, 512), bk(4))
        vdup = brs(5, 2).rearrange("p a (t f) -> p (a t) f", f=256)
        for tt in range(4):
            b = proj_tm(wv, 128, 128, tt, wk)
            veng = "act" if tt % 2 == 0 else "dve"
            for g in range(2):
                for dup in range(2):
                    evac(vdup[:, tt, (2 * g + dup) * 64:(2 * g + dup + 1) * 64], b, (g * 64, g * 64 + 64), bk(5, 2), eng=veng)
            if kind == "p":
                evac(br(7)[:, tt * 128:(tt + 1) * 128], b, (0, 128), bk(7), eng=veng)
        scr_w(v_s[tok0:tok0 + 512, 0:256].rearrange("(t p) f -> p t f", p=128), vdup, bk(5, 2), "VS")
        ktm_banks = []
        if kind == "p":
            for tt in range(4):
                ktm_banks.append(proj_tm(wv, 0, 128, tt, wk))
                b = ktm_banks[-1]
                for g in range(2):
                    act([("ps", b)], bk(15) + ["TMPS"], out=br(15)[:, g * 64:(g + 1) * 64], in_=ps[b][:, g * 64:(g + 1) * 64],
                        func=AF.Square, accum_out=TMPS[:, g:g + 1])
                act(["TMPS"] + KC, ["TMPS"], out=TMPS[:, 2:4], in_=TMPS[:, 0:2], func=AF.Ln, scale=1.0 / 64.0, bias=sm(EPSC))
                act(["TMPS"], ["TMPS"], out=TMPS[:, 4:6], in_=TMPS[:, 2:4], func=AF.Exp, scale=-0.5)
                for g in range(2):
                    dve("scalar_tensor_tensor", [("ps", b), "TMPS"] + KC, bk(14),
                        out=br(14)[:, tt * 128 + g * 64:tt * 128 + (g + 1) * 64], in0=ps[b][:, g * 64:(g + 1) * 64],
                        scalar=TMPS[:, 4 + g:5 + g], in1=GKR[:], op0=ALU.mult, op1=ALU.mult)
        qk_norm_rope([0, 1], QN, rope, True)
        qk_norm_rope([2, 3], QN, rope, True)
        scr_w(q_s[0:4, :, tok0:tok0 + 512].rearrange("c p t -> p c t"), brs(0, 4), bk(0, 4), "QS")
        qk_norm_rope([4], KN, rope, False)
        scr_w(k_s[0:128, tok0:tok0 + 512], br(4), bk(4), "KS")
        if kind == "p":
            for tt in range(4):
                P.dma("pool", ("so", "dv", tt), sdv[bl + tt // 2][:, (tt % 2) * 128:(tt % 2 + 1) * 128, :].rearrange("g p d -> p g d"),
                      bf(7)[:, tt * 128:(tt + 1) * 128].rearrange("p (g d) -> p g d", g=2),
                      reads=bk(7), writes=[("SDV", bl, tt)])
            for tt in range(4):
                P.dma("pool", ("so", "dk", tt), sdk[bl + tt // 2][:, (tt % 2) * 128:(tt % 2 + 1) * 128, :].rearrange("g p d -> p g d"),
                      bf(14)[:, tt * 128:(tt + 1) * 128].rearrange("p (g d) -> p g d", g=2),
                      reads=bk(14), writes=[("SDK", bl, tt)])

    def odd_loop2(l, kind, col0, ci, tok0, zoff):
        inv_src = (invp_d.rearrange("g p t -> p g t") if kind == "p" else invs_d[:, :, tok0:tok0 + 512].rearrange("g p t -> p g t"))
        ld(brs(19, 4), inv_src, [], bk(19, 4))
        for (seg_tok, seg_len, zcol) in zoff:
            Ls = seg_len
            Lp = Ls + 16

            def v3(b0, nblk, ng, r=False):
                base = flat(brs(b0, nblk)) if r else flat(bfs(b0, nblk))
                return base[:, 0:ng * Lp].rearrange("p (g t) -> p g t", g=ng)
            ku, ka, kb, kc_, kd = bk(0, 5), bk(5, 5), bk(10, 4), bk(15, 3), bk(18)
            ld(v3(0, 5, 4, True), z_s[:, zcol - 8:zcol + Ls + 8].rearrange("(g p) t -> p g t", p=128), ["ZS"], ku)
            U, A, B, C = v3(0, 5, 4), v3(5, 5, 4), v3(10, 4, 3), v3(15, 3, 2)
            Ar, Br, Cr = v3(5, 5, 4, True), v3(10, 4, 3, True), v3(15, 3, 2, True)
            dve("tensor_tensor", ku, ka, out=Ar[:, :, 0:Ls + 15], in0=U[:, :, 0:Ls + 15], in1=U[:, :, 1:Ls + 16], op=ALU.add)
            dve("tensor_tensor", ka, kb, out=Br[:, :, 0:Ls + 13], in0=A[:, 1:4, 0:Ls + 13], in1=A[:, 1:4, 2:Ls + 15], op=ALU.add)
            dve("tensor_tensor", kb, kc_, out=Cr[:, :, 0:Ls + 9], in0=B[:, 1:3, 0:Ls + 9], in1=B[:, 1:3, 4:Ls + 13], op=ALU.add)
            dve("tensor_tensor", kc_, kd, out=br(18)[:, 0:Ls], in0=C[:, 1, 0:Ls], in1=C[:, 1, 8:8 + Ls], op=ALU.add)
            ssrc = [(A[:, 0, 7:7 + Ls], ka), (B[:, 0, 6:6 + Ls], kb), (C[:, 0, 4:4 + Ls], kc_), (bf(18)[:, 0:Ls], kd)]
            for gi in range(4):
                sv, skey = ssrc[gi]
                pb = 23 if gi % 2 == 0 else 14
                tmpb = 14 if gi % 2 == 0 else 23
                dve("tensor_tensor", skey + bk(19 + gi), bk(24 + gi), out=br(24 + gi)[:, seg_tok:seg_tok + Ls], in0=sv,
                    in1=bf(19 + gi)[:, seg_tok:seg_tok + Ls], op=ALU.mult)
                dve("tensor_tensor", bk(24 + gi) + ku, bk(28 + gi), out=br(28 + gi)[:, seg_tok:seg_tok + Ls],
                    in0=bf(24 + gi)[:, seg_tok:seg_tok + Ls], in1=U[:, gi, 8:8 + Ls], op=ALU.subtract)
        for gi in range(4):
            b = nb()
            mm(b, (0, 512), PW[:, gi, :], br(28 + gi), True, True, reads=bk(28 + gi) + KC)
            dve("tensor_scalar", [("ps", b)] + KC, bk(24 + gi), out=br(24 + gi), in0=ps[b][:], scalar1=sm(PSCALE + gi),
                scalar2=None, op0=ALU.mult)
        v8 = brs(8, 8).rearrange("p a (t f) -> p (a t) f", f=256)
        if kind == "p":
            ld(br(4), k_s[0:128, tok0:tok0 + 512], ["KS"], bk(4))
            ld(v8[:, 0:4, :], v_s[tok0:tok0 + 512, 0:256].rearrange("(t p) f -> p t f", p=128), ["VS"], bk(8, 2))
            k4 = br(4)
        else:
            k4 = flat(brs(4, 4))
            ld(k4, k_s[0:128, 0:2048], ["KS"], bk(4, 4))
            ld(v8, v_s[0:2048, 0:256].rearrange("(t p) f -> p t f", p=128), ["VS"], bk(8, 8))
            for tt in range(2):
                ld(br(16)[:, tt * 128:(tt + 1) * 128].rearrange("p (g d) -> p g d", g=2),
                   cd_k[:, tt * 128:(tt + 1) * 128, :].rearrange("g p d -> p g d"), [], bk(16))
            vc = br(17).rearrange("p (t f) -> p t f", f=256)
            for g in range(2):
                for dup in range(2):
                    P.dma("pool", ("d", "B", 17), vc[:, :, (2 * g + dup) * 64:(2 * g + dup + 1) * 64],
                          cd_v[g].rearrange("(t p) d -> p t d", p=128), writes=bk(17))
            b = nb()
            for tt in range(2):
                tr(b, (tt * 128, (tt + 1) * 128), bf(16)[:, tt * 128:(tt + 1) * 128], bk(16), tt == 1)
            evac(br(16)[:, 256:512], b, (0, 256), bk(16))
        for qblk_ in (0, 1):
            zero_half(qblk_, 64)
        for qblk_ in (2, 3):
            zero_half(qblk_, 0)
        for h in range(8):
            g = h // 4
            half = h % 2
            c = h // 2
            qb_ = 2 * g + (h % 2)
            ld(br(qb_)[g * 64:(g + 1) * 64, :], q_s[h // 2, (h % 2) * 64:(h % 2) * 64 + 64, tok0:tok0 + 512], ["QS"], bk(qb_))
            vcol = 2 * g * 64
            jobs = []
            if kind == "p":
                for sg in range(2):
                    kts = []
                    for kt in range(2):
                        ktile = sg * 2 + kt
                        kts.append((k4[:, ktile * 128:(ktile + 1) * 128], v8[:, ktile, vcol:vcol + 128], None, bk(4) + bk(8, 2)))
                    jobs.append((qb_, half, sg * 256, (sg + 1) * 256, kts))
            else:
                kts = []
                for kt in range(16):
                    kts.append((k4[:, kt * 128:(kt + 1) * 128], v8[:, kt, vcol:vcol + 128], None, bk(4, 4) + bk(8, 8)))
                for tt in range(2):
                    kts.append((br(16)[:, 256 + tt * 128:256 + (tt + 1) * 128], vc[:, tt, vcol:vcol + 128], None, bk(16) + bk(17)))
                jobs.append((qb_, half, 0, 512, kts))
            attention(jobs, 28 + c, pts=(18, 19, 20))
        mixer_out(l, od_w_out, col0, ci)

    out_keys = []

    def zoff_for(kind, tile_idx, pad):
        if kind == "p":
            return [(0, 256, pad), (256, 256, 3 * pad + 256)]
        return [(0, 512, pad + tile_idx * 512)]

    for pp in range(N_PROMPT_PASS):
        bl = pp * 2
        load_x(xp, pp * 512, 0)
        ffn(0, 0, 0, 0)
        if pp == 0:
            mod_layer(1)
        zero_z()
        zo = zoff_for("p", 0, 1)
        even_loop1(0, "p", 0, 0, 0, zo, bl)
        even_loop2(0, "p", 0, 0, 0, zo, 0)
        ffn(0, 1, 0, 0)
        ffn(1, 0, 0, 0)
        zero_z()
        zo = zoff_for("p", 0, 8)
        odd_loop1(1, "p", 0, 0, 0, zo, bl)
        odd_loop2(1, "p", 0, 0, 0, zo)
        ffn(1, 1, 0, 0)
        out_keys.append(store_y(yp, "yp", pp * 512, 0))
        out_keys += [(nm, bl, tt) for nm in ("SAK", "SAV", "SDK", "SDV") for tt in range(4)]
    if DO_SAMPLE:
        for t in range(4):
            load_x(xs, t * 512, t * 512)
        zero_z()
        for t in range(4):
            ffn(0, 0, t * 512, 1)
            even_loop1(0, "s", t * 512, 1, t * 512, zoff_for("s", t, 1), 0)
        for t in range(4):
            even_loop2(0, "s", t * 512, 1, t * 512, zoff_for("s", t, 1), t)
            ffn(0, 1, t * 512, 1)
        zero_z()
        for t in range(4):
            ffn(1, 0, t * 512, 1)
            odd_loop1(1, "s", t * 512, 1, t * 512, zoff_for("s", t, 8), 0)
        for t in range(4):
            odd_loop2(1, "s", t * 512, 1, t * 512, zoff_for("s", t, 8))
            if t > 0:
                out_keys.append(store_y(ys, "ys", (t - 1) * 512, (t - 1) * 512))
            ffn(1, 1, t * 512, 1)
        out_keys.append(store_y(ys, "ys", 3 * 512, 3 * 512))
    P.final_wait("pool", out_keys)
    P.emit()
    return nc


def _na_bias(rpb):
    H = rpb.shape[0]
    r = np.arange(32)
    rs = np.clip(r - 4, 0, 24)
    cq = np.arange(64)
    cs = np.clip(cq - 8, 0, 48)
    out = np.full((H, 4, 8, 128, 512), NEG, np.float32)
    for qt in range(4):
        kts = na_ktiles(qt)
        for j, kt in enumerate(kts):
            kr = (2 * kt + np.arange(2))[:, None, None, None]
            kc = np.arange(64)[None, :, None, None]
            qr = (8 * qt + np.arange(8))[None, None, :, None]
            qc = np.arange(64)[None, None, None, :]
            valid = (kr >= rs[qr]) & (kr < rs[qr] + 8) & (kc >= cs[qc]) & (kc < cs[qc] + 16)
            ro = np.clip(kr - qr + 7, 0, 14)
            co = np.clip(kc - qc + 15, 0, 30)
            ro_b, co_b, valid_b = np.broadcast_arrays(ro, co, valid)
            for h in range(H):
                vals = rpb[h][ro_b, co_b]
                tile = np.where(valid_b, vals, np.float32(NEG)).astype(np.float32)
                out[h, qt, j] = tile.reshape(128, 512)
    return out


def _consts():
    ident = np.eye(128, dtype=np.float32)
    ones = np.ones((128, 128), np.float32)
    o64 = np.zeros((128, 128), np.float32)
    o64[:64, :64] = 1.0
    o64[64:, 64:] = 1.0
    psw = np.zeros((128, 128), np.float32)
    for i in range(64):
        psw[2 * i + 1, 2 * i] = -1.0
        psw[2 * i, 2 * i + 1] = 1.0
    cst = np.stack([ident, ones, o64, psw], axis=1)
    t = np.arange(2048)
    row = (t // 64).astype(np.float32)
    col = (t % 64).astype(np.float32)
    inv = (np.float32(10000.0) ** (-np.arange(16, dtype=np.float32) / np.float32(16))).astype(np.float32)
    ang = np.concatenate([row[:, None] * inv, col[:, None] * inv], axis=-1).astype(np.float32)
    idx = (np.arange(128) % 64) // 2
    cosT = np.cos(ang).astype(np.float32)[:, idx].T.copy()
    sinT = np.sin(ang).astype(np.float32)[:, idx].T.copy()

    def invcnt(L, w):
        tt = np.arange(L)
        lo = np.maximum(tt - w // 2, 0)
        hi = np.minimum(tt + w - w // 2, L)
        return (1.0 / (hi - lo).astype(np.float32)).astype(np.float32)
    invp = np.stack([np.tile(np.tile(invcnt(256, w), 2)[None, :], (128, 1)) for w in (2, 4, 8, 16)]).astype(np.float32)
    invs = np.stack([np.tile(invcnt(2048, w)[None, :], (128, 1)) for w in (2, 4, 8, 16)]).astype(np.float32)
    return cst, ident, cosT, sinT, invp, invs


def kernel(x_prompt, x_sample, cache_a_k, cache_a_v, cache_d_k, cache_d_v, c, c_ctx,
           mod_w, mod_b, norm_w, ffn_w1, ffn_w2,
           ev_w_in, ev_rpb, ev_conv_w, ev_conv_b, ev_w_out,
           od_w_in, od_pool_w, od_pool_scale, od_q_norm, od_k_norm, od_w_out):
    f = lambda a: np.ascontiguousarray(np.asarray(a, dtype=np.float32))
    x_prompt, x_sample = f(x_prompt), f(x_sample)
    cst, ident, cosT, sinT, invp, invs = _consts()
    nab = _na_bias(f(ev_rpb)[0])
    norm_w, mod_b = f(norm_w), f(mod_b)
    base = np.zeros((128, 400), np.float32)
    base[:, 0:96] = norm_w.reshape(2, 6, 8, 128).transpose(3, 0, 1, 2).reshape(128, 96)
    base[:, 96:240] = mod_b.reshape(2, 72, 128).transpose(2, 0, 1).reshape(128, 144)
    base[:, 240:252] = f(ev_conv_w)[0].reshape(3, 4, 128).transpose(2, 0, 1).reshape(128, 12)
    base[:, 252:256] = f(ev_conv_b)[0].reshape(4, 128).T
    base[:, 256:260] = f(od_pool_scale)[0].reshape(4, 128).T
    qn = f(od_q_norm)[0]
    kn = f(od_k_norm)[0]
    base[:, 260] = np.tile(qn, 2)
    base[:, 261] = np.tile(kn, 2)
    base[:, 399] = EPS
    base[:, 398] = 64.0 * EPS
    gk_row = np.tile(kn[None, :], (128, 1)).astype(np.float32)
    pool_w = f(od_pool_w)[0].transpose(1, 0, 2).copy()
    shared = {
        "mod_w": f(mod_w), "ffn_w1": f(ffn_w1), "ffn_w2": f(ffn_w2),
        "ev_w_in": f(ev_w_in)[0], "ev_w_out": f(ev_w_out)[0], "od_w_in": f(od_w_in)[0], "od_w_out": f(od_w_out)[0],
        "pool_w": pool_w, "cst": cst, "identf": ident, "gk_row": gk_row, "nab": nab,
        "cosT": cosT, "sinT": sinT, "invp": invp, "invs": invs,
    }
    c, c_ctx = f(c), f(c_ctx)
    in_maps = []
    for core in range(N_CORES):
        b = core % 2
        sm_ = base.copy()
        cond = np.stack([c_ctx.reshape(8, 128).T, c[b].reshape(8, 128).T], axis=-1)
        sm_[:, 262:278] = cond.reshape(128, 16)
        m = dict(shared)
        m["small"] = sm_
        m["xp"] = x_prompt[4 * core:4 * core + 4].reshape(1024, D)
        m["xs"] = x_sample[b]
        m["ca_k"] = f(cache_a_k)[b, 0]
        m["ca_v"] = f(cache_a_v)[b, 0]
        m["cd_k"] = f(cache_d_k)[b, 0]
        m["cd_v"] = f(cache_d_v)[b, 0]
        in_maps.append(m)
    nc = build_program()
    res = run_bass_kernel_spmd(nc, in_maps, core_ids=list(range(N_CORES)))
    r = res.results
    y_prompt = np.concatenate([r[i]["yp"].reshape(4, 256, D) for i in range(N_CORES)], axis=0)
    y_sample = np.stack([r[0]["ys"], r[1]["ys"]], axis=0)
    sak = np.concatenate([r[i]["sak"] for i in range(N_CORES)], axis=0)[:, None]
    sav = np.concatenate([r[i]["sav"] for i in range(N_CORES)], axis=0)[:, None]
    sdk = np.concatenate([r[i]["sdk"] for i in range(N_CORES)], axis=0)[:, None]
    sdv = np.concatenate([r[i]["sdv"] for i in range(N_CORES)], axis=0)[:, None]
    return (y_prompt.astype(np.float32), y_sample.astype(np.float32), sak.astype(np.float32),
            sav.astype(np.float32), sdk.astype(np.float32), sdv.astype(np.float32))
```

```python
import numpy as np
import concourse.bass as bass
import concourse.mybir as mybir
from concourse.bass_utils import run_bass_kernel_spmd

F32 = mybir.dt.float32
F32R = mybir.dt.float32r
ALU = mybir.AluOpType
AF = mybir.ActivationFunctionType

D = 1024
DFF = 2816
NEG = -1e30
EPS = 1e-6
NB = 32
N_PROMPT_PASS = 2
DO_SAMPLE = True
N_CORES = 8


class Prog:
    def __init__(self, nc):
        self.nc = nc
        self.ops = {e: [] for e in ("pe", "act", "dve", "pool", "sp")}
        self.sems = {}
        self.cnt = {}
        self.known = {e: {} for e in self.ops}
        self.last_w = {}
        self.readers = {}
        self.pe_pending = False

    def _sem(self, key):
        if key not in self.sems:
            self.sems[key] = self.nc.alloc_semaphore("s_" + str(key))
            self.cnt[key] = 0
        return self.sems[key]

    def _deps(self, eng, reads, writes):
        need = {}

        def add(st):
            if st is None:
                return
            k, v = st
            if need.get(k, 0) < v:
                need[k] = v
        for r in reads:
            add(self.last_w.get(r))
        for w in writes:
            add(self.last_w.get(w))
            for st in self.readers.get(w, ()):
                add(st)
        waits = []
        for k, v in need.items():
            if k == "pe" and eng == "pe":
                continue
            if self.known[eng].get(k, 0) >= v:
                continue
            self.known[eng][k] = v
            waits.append((k, v))
        return waits

    def _record(self, stamp, reads, writes):
        for r in reads:
            self.readers.setdefault(r, []).append(stamp)
        for w in writes:
            self.last_w[w] = stamp
            self.readers[w] = []

    def op(self, eng, fn, reads=(), writes=(), signal=True):
        waits = self._deps(eng, reads, writes)
        self._sem(eng)
        if signal:
            self.cnt[eng] += 1
            stamp = (eng, self.cnt[eng])
            inc = (eng, 1)
            if eng == "pe":
                self.pe_pending = False
        else:
            assert eng == "pe"
            stamp = (eng, self.cnt[eng] + 1)
            inc = None
            self.pe_pending = True
        self.ops[eng].append((waits, fn, inc))
        self._record(stamp, reads, writes)

    def dma(self, q, semkey, out, in_, reads=(), writes=(), **kw):
        if semkey is None:
            semkey = ("d",) + tuple(writes[0]) if isinstance(writes[0], tuple) else ("d", writes[0])
        waits = self._deps(q, reads, writes)
        self._sem(semkey)
        self.cnt[semkey] += 16
        stamp = (semkey, self.cnt[semkey])
        self.ops[q].append((waits, lambda e: e.dma_start(out=out, in_=in_, **kw), (semkey, 16)))
        self._record(stamp, reads, writes)

    def final_wait(self, eng, keys):
        waits = self._deps(eng, keys, ())
        self.ops[eng].append((waits, None, None))

    def emit(self):
        nc = self.nc
        assert not self.pe_pending
        with nc.Block() as blk:
            def run(name):
                def _f(e):
                    for waits, fn, inc in self.ops[name]:
                        for k, v in waits:
                            e.wait_ge(self.sems[k], v)
                        if fn is None:
                            continue
                        ins = fn(e)
                        if inc is not None:
                            ins.then_inc(self.sems[inc[0]], inc[1])
                return _f
            blk.tensor(run("pe"))
            blk.scalar(run("act"))
            blk.vector(run("dve"))
            blk.gpsimd(run("pool"))
            blk.sync(run("sp"))


def L(method, **kw):
    return lambda e: getattr(e, method)(**kw)


def na_ktiles(qt):
    lo = min(max(8 * qt - 4, 0), 24)
    hi = min(max(8 * qt + 3, 0), 24) + 7
    return list(range(lo // 2, hi // 2 + 1))


def build_program():
    nc = bass.Bass("TRN2", target_bir_lowering=False)
    nc.dge_precook = False
    P = Prog(nc)

    def din(name, shape, dt=F32):
        return nc.dram_tensor(name, list(shape), dt, kind="ExternalInput").ap()

    def dout(name, shape):
        return nc.dram_tensor(name, list(shape), F32, kind="ExternalOutput").ap()

    def dint(name, shape, dt=F32):
        return nc.dram_tensor(name, list(shape), dt).ap()

    xp = din("xp", [1024, D], F32R)
    xs = din("xs", [2048, D], F32R)
    ca_k = din("ca_k", [8, 256, 64], F32R)
    ca_v = din("ca_v", [8, 256, 64], F32R)
    cd_k = din("cd_k", [2, 256, 64], F32R)
    cd_v = din("cd_v", [2, 256, 64], F32R)
    mod_w = din("mod_w", [2, D, 9 * D], F32R)
    w1_d = din("ffn_w1", [2, 2, D, 2 * DFF], F32R)
    w2_d = din("ffn_w2", [2, 2, DFF, D], F32R)
    ev_w_in = din("ev_w_in", [D, 3072], F32R)
    ev_w_out = din("ev_w_out", [D, D], F32R)
    od_w_in = din("od_w_in", [D, 1280], F32R)
    od_w_out = din("od_w_out", [D, D], F32R)
    pool_w_d = din("pool_w", [128, 4, 128], F32R)
    cst_d = din("cst", [128, 4, 128], F32R)
    identf_d = din("identf", [128, 128])
    small_d = din("small", [128, 400])
    gk_row_d = din("gk_row", [128, 64])
    nab_d = din("nab", [8, 4, 8, 128, 512], F32R)
    cos_d = din("cosT", [128, 2048], F32R)
    sin_d = din("sinT", [128, 2048], F32R)
    invp_d = din("invp", [4, 128, 512], F32R)
    invs_d = din("invs", [4, 128, 2048], F32R)

    yp = dout("yp", [1024, D])
    ys = dout("ys", [2048, D])
    sak = dout("sak", [4, 8, 256, 64])
    sav = dout("sav", [4, 8, 256, 64])
    sdk = dout("sdk", [4, 2, 256, 64])
    sdv = dout("sdv", [4, 2, 256, 64])

    LP = 2048 + 32
    q_s = dint("q_s", [8, 128, 2048], F32R)
    k_s = dint("k_s", [512, 2048], F32R)
    v_s = dint("v_s", [2048, 512], F32R)
    z_s = dint("z_s", [512, LP], F32R)
    g_s = dint("g_s", [512, 2048], F32R)

    X = nc.alloc_sbuf_tensor("X", [128, 8, 2048], F32)
    Hb = nc.alloc_sbuf_tensor("Hb", [128, 8, 512], F32R)
    WS = nc.alloc_sbuf_tensor("WS", [128, 3, 4096], F32R)
    BIG = nc.alloc_sbuf_tensor("BIG", [128, NB, 512], F32)
    CST = nc.alloc_sbuf_tensor("CST", [128, 4, 128], F32R)
    IDF = nc.alloc_sbuf_tensor("IDF", [128, 128], F32)
    SM = nc.alloc_sbuf_tensor("SM", [128, 400], F32)
    GKR = nc.alloc_sbuf_tensor("GKR", [128, 64], F32)
    PW = nc.alloc_sbuf_tensor("PW", [128, 4, 128], F32R)
    SC = nc.alloc_sbuf_tensor("SC", [128, 16], F32R)
    MODT = nc.alloc_sbuf_tensor("MODT", [128, 2, 72, 2], F32)
    COEF = nc.alloc_sbuf_tensor("COEF", [128, 2, 9, 8, 2], F32)
    TMPS = nc.alloc_sbuf_tensor("TMPS", [128, 8], F32)
    RS = nc.alloc_sbuf_tensor("RS", [128, 2, 512], F32)
    ZT = nc.alloc_sbuf_tensor("ZT", [128, 520], F32)
    ps = [nc.alloc_psum_tensor(f"ps{i}", [128, 512], F32) for i in range(8)]

    IDR = CST[:, 0, :]
    ONES = CST[:, 1, :]
    ONES64 = CST[:, 2, :]
    PSWAP = CST[:, 3, :]
    Hf = Hb[:].bitcast(F32)
    KC = ["CONST"]

    NORMW, MODB, CONVW, CONVB, PSCALE, QN, KN, CONDC = 0, 96, 240, 252, 256, 260, 261, 262
    EPS64C, EPSC = 398, 399

    def sm(col, n=1):
        return SM[:, col:col + n]

    def bf(i):
        return BIG[:, i, :]

    def br(i):
        return BIG[:, i, :].bitcast(F32R)

    def bfs(i, n):
        return BIG[:, i:i + n, :]

    def brs(i, n):
        return BIG[:, i:i + n, :].bitcast(F32R)

    def flat(ap3):
        return ap3.rearrange("p a b -> p (a b)")

    def bk(i, n=1):
        return [("B", j) for j in range(i, i + n)]

    bank_ctr = [0]

    def nb():
        b = bank_ctr[0] % 4
        bank_ctr[0] += 1
        return b

    ws_ctr = [0]

    def wstage():
        s = ws_ctr[0] % 3
        ws_ctr[0] += 1
        return s

    def mm(bank, cols, lhsT, rhs, start, stop, reads, signal=None):
        if signal is None:
            signal = stop
        P.op("pe", L("matmul", out=ps[bank][:, cols[0]:cols[1]], lhsT=lhsT, rhs=rhs, start=start, stop=stop),
             reads=reads, writes=[("ps", bank)], signal=signal)

    def tr(bank, cols, in_, reads, signal):
        P.op("pe", L("transpose", out=ps[bank][:, cols[0]:cols[1]], in_=in_, identity=IDF[:]),
             reads=reads + KC, writes=[("ps", bank)], signal=signal)

    evac_ctr = [0]

    def evac(out, bank, cols, writes, scale=None, eng=None, rows=(0, 128)):
        src = ps[bank][rows[0]:rows[1], cols[0]:cols[1]]
        if eng is None:
            eng = "act" if evac_ctr[0] % 2 == 0 else "dve"
            evac_ctr[0] += 1
        if eng == "act":
            kw = dict(out=out, in_=src, func=AF.Copy)
            if scale is not None:
                kw["scale"] = scale
            P.op("act", L("activation", **kw), reads=[("ps", bank)], writes=writes)
        else:
            if scale is None:
                P.op("dve", L("tensor_copy", out=out, in_=src), reads=[("ps", bank)], writes=writes)
            else:
                P.op("dve", L("tensor_scalar", out=out, in0=src, scalar1=scale, scalar2=None, op0=ALU.mult),
                     reads=[("ps", bank)], writes=writes)

    def dve(method, reads, writes, **kw):
        P.op("dve", L(method, **kw), reads=reads, writes=writes)

    def act(reads, writes, **kw):
        P.op("act", L("activation", **kw), reads=reads, writes=writes)

    def ld(out, in_, reads, writes):
        P.dma("pool", None, out, in_, reads=reads, writes=writes)

    P.dma("pool", "c0", CST[:], cst_d, writes=KC)
    P.dma("pool", "c0", IDF[:], identf_d, writes=KC)
    P.dma("pool", "c0", SM[:], small_d, writes=KC)
    P.dma("pool", "c0", GKR[:], gk_row_d, writes=KC)
    P.dma("pool", "c0", PW[:], pool_w_d, writes=KC)

    P.op("pool", L("memset", ap=ZT[:], constant=0.0), writes=["ZT"])

    def zero_z():
        for r in range(4):
            for q4 in range(4):
                P.dma("pool", "zs", z_s[r * 128:(r + 1) * 128, q4 * 520:(q4 + 1) * 520], ZT[:].bitcast(F32R),
                      reads=["ZT"], writes=["ZS"])

    def zero_half(blk_id, r0):
        dve("tensor_copy", ["ZT"], bk(blk_id), out=br(blk_id)[r0:r0 + 64, :], in_=ZT[r0:r0 + 64, 0:512])

    act(KC, ["SC"], out=SC[:], in_=sm(CONDC, 16), func=AF.Silu)
    SC3 = SC[:].rearrange("p (k n) -> p k n", n=2)
    def mod_layer(l):
        for nblk in range(18):
            st = wstage()
            wv = WS[:, st, :].rearrange("p (k n) -> p k n", k=8)
            P.dma("sp", ("ws", st), wv, mod_w[l, :, nblk * 512:(nblk + 1) * 512].rearrange("(k p) n -> p k n", p=128),
                  writes=[("W", st)])
            for mi in range(4):
                j = nblk * 4 + mi
                b = nb()
                for kc in range(8):
                    mm(b, (0, 2), wv[:, kc, mi * 128:(mi + 1) * 128], SC3[:, kc, :], kc == 0, kc == 7,
                       reads=[("W", st), "SC"])
                dve("tensor_scalar", [("ps", b)] + KC, [f"MODT{l}"], out=MODT[:, l, j, :], in0=ps[b][:, 0:2],
                    scalar1=sm(MODB + l * 72 + j), scalar2=None, op0=ALU.add)
        for ci in range(2):
            for sub in range(3):
                sh, scl, gt = 3 * sub, 3 * sub + 1, 3 * sub + 2
                gpre, gpost = 2 * sub, 2 * sub + 1
                half = 0.5 if sub != 1 else 1.0
                gw_pre = sm(NORMW + l * 48 + gpre * 8, 8)
                gw_post = sm(NORMW + l * 48 + gpost * 8, 8)
                dve("scalar_tensor_tensor", [f"MODT{l}"] + KC, [f"COEF{l}"], out=COEF[:, l, 3 * sub, :, ci],
                    in0=MODT[:, l, scl * 8:(scl + 1) * 8, ci], scalar=1.0, in1=gw_pre, op0=ALU.add, op1=ALU.mult)
                dve("tensor_copy", [f"MODT{l}"], [f"COEF{l}"], out=COEF[:, l, 3 * sub + 1, :, ci],
                    in_=MODT[:, l, sh * 8:(sh + 1) * 8, ci])
                dve("scalar_tensor_tensor", [f"MODT{l}"] + KC, [f"COEF{l}"], out=COEF[:, l, 3 * sub + 2, :, ci],
                    in0=MODT[:, l, gt * 8:(gt + 1) * 8, ci], scalar=half, in1=gw_post, op0=ALU.mult, op1=ALU.mult)

    mod_layer(0)

    def coef(l, k, c, ci):
        return COEF[:, l, k, c, ci:ci + 1]

    SQ0 = 24
    RSTD = 32

    def Xc(c, col0):
        return X[:, c, col0:col0 + 512]

    def stats(src, skey, nch=8, sq0=None, presq=False):
        sq0 = SQ0 if sq0 is None else sq0
        if not presq:
            for c in range(nch):
                act([skey(c)], bk(sq0 + c), out=br(sq0 + c), in_=src(c), func=AF.Square)
        b = nb()
        for c in range(nch):
            mm(b, (0, 512), ONES, br(sq0 + c), c == 0, c == nch - 1, reads=bk(sq0 + c) + KC)
        act([("ps", b)] + KC, ["RSTD"], out=RS[:, 0, :], in_=ps[b][:], func=AF.Ln, scale=1.0 / 1024.0, bias=sm(EPSC))
        act(["RSTD"], ["RSTD"], out=RS[:, 0, :], in_=RS[:, 0, :], func=AF.Exp, scale=-0.5)

    def pre_norm(l, sub, col0, ci):
        stats(lambda c: Xc(c, col0), lambda c: ("X", c, col0))
        for c in range(8):
            dve("tensor_tensor", [("X", c, col0)] + ["RSTD"], bk(SQ0 + c), out=br(SQ0 + c), in0=Xc(c, col0),
                in1=RS[:, 0, :], op=ALU.mult)
            act(bk(SQ0 + c) + [f"COEF{l}"], [("H", c)], out=Hb[:, c, :], in_=bf(SQ0 + c), func=AF.Identity,
                scale=coef(l, 3 * sub, c, ci), bias=coef(l, 3 * sub + 1, c, ci))

    def post_norm(l, sub, col0, ci, sq0=None):
        stats(lambda c: Hf[:, c, :], lambda c: ("H", c), sq0=sq0, presq=(sq0 is not None))
        for c in range(8):
            dve("tensor_tensor", [("H", c), "RSTD"], bk(SQ0 + c), out=br(SQ0 + c), in0=Hf[:, c, :], in1=RS[:, 0, :],
                op=ALU.mult)
            dve("scalar_tensor_tensor", bk(SQ0 + c) + [f"COEF{l}", ("X", c, col0)], [("X", c, col0)], out=Xc(c, col0),
                in0=bf(SQ0 + c), scalar=coef(l, 3 * sub + 2, c, ci), in1=Xc(c, col0), op0=ALU.mult, op1=ALU.add)

    def evac_sq(m, bank, sqblk):
        P.op("dve", L("tensor_copy", out=Hb[:, m, :], in_=ps[bank][:]), reads=[("ps", bank)], writes=[("H", m)])
        act([("H", m)], bk(sqblk), out=br(sqblk), in_=Hf[:, m, :], func=AF.Square)

    def ffn(l, s, col0, ci):
        sub = 0 if s == 0 else 2
        pre_norm(l, sub, col0, ci)
        w1 = w1_d[l, s]
        w2 = w2_d[l, s]
        for jb in range(11):
            st = wstage()
            wv = WS[:, st, :].rearrange("p (k n) -> p k n", k=8)
            P.dma("sp", ("ws", st), wv[:, :, 0:256], w1[:, jb * 256:(jb + 1) * 256].rearrange("(k p) n -> p k n", p=128),
                  writes=[("W", st)])
            P.dma("sp", ("ws", st), wv[:, :, 256:512],
                  w1[:, DFF + jb * 256:DFF + (jb + 1) * 256].rearrange("(k p) n -> p k n", p=128), writes=[("W", st)])
            if jb == 0:
                first_banks = [nb() for _ in range(4)]
                for kc in range(8):
                    for gi_, (jj_, off_) in enumerate(((0, 0), (0, 256), (1, 0), (1, 256))):
                        mm(first_banks[gi_], (0, 512), wv[:, kc, off_ + jj_ * 128:off_ + (jj_ + 1) * 128], Hb[:, kc, :],
                           kc == 0, kc == 7, reads=[("W", st), ("H", kc)])
            for jj in range(2):
                j = jb * 2 + jj
                if jb == 0:
                    bg_, bu_ = first_banks[2 * jj], first_banks[2 * jj + 1]
                else:
                    bg_, bu_ = nb(), nb()
                    for kc in range(8):
                        mm(bg_, (0, 512), wv[:, kc, jj * 128:(jj + 1) * 128], Hb[:, kc, :], kc == 0, kc == 7,
                           reads=[("W", st), ("H", kc)])
                    for kc in range(8):
                        mm(bu_, (0, 512), wv[:, kc, 256 + jj * 128:256 + (jj + 1) * 128], Hb[:, kc, :], kc == 0, kc == 7,
                           reads=[("W", st), ("H", kc)])
                tb = 22 + (j % 2)
                act([("ps", bg_)], bk(tb), out=br(tb), in_=ps[bg_][:], func=AF.Silu)
                dve("tensor_tensor", [("ps", bu_)] + bk(tb), bk(j), out=br(j), in0=bf(tb), in1=ps[bu_][:], op=ALU.mult)
        for mb in range(8):
            st = wstage()
            wv = WS[:, st, 0:2816].rearrange("p (j n) -> p j n", j=22)
            P.dma("sp", ("ws", st), wv, w2[:, mb * 128:(mb + 1) * 128].rearrange("(j p) n -> p j n", p=128),
                  writes=[("W", st)])
            b = nb()
            for j in range(22):
                mm(b, (0, 512), wv[:, j, :], br(j), j == 0, j == 21, reads=[("W", st)] + bk(j))
            evac(Hb[:, mb, :], b, (0, 512), [("H", mb)])
        post_norm(l, sub, col0, ci)

    def stg_view(r=False):
        v = BIG[:, 0:8, :]
        if r:
            v = v.bitcast(F32R)
        return v.rearrange("p (t h) n -> p t (h n)", t=4)

    def load_x(src, r0, col0):
        stg = stg_view()
        P.dma("pool", "xin", stg_view(True), src[r0:r0 + 512, :].rearrange("(t p) d -> p t d", p=128), writes=bk(0, 8))
        for c in range(8):
            b = nb()
            for tt in range(4):
                tr(b, (tt * 128, (tt + 1) * 128), stg[:, tt, c * 128:(c + 1) * 128], bk(2 * tt, 2), tt == 3)
            evac(Xc(c, col0), b, (0, 512), [("X", c, col0)])

    def store_y(dst, name, r0, col0):
        stg = stg_view()
        for tt in range(4):
            for hf in range(2):
                b = nb()
                for i in range(4):
                    c = hf * 4 + i
                    tr(b, (i * 128, (i + 1) * 128), X[:, c, col0 + tt * 128:col0 + (tt + 1) * 128], [("X", c, col0)], i == 3)
                evac(stg_view(True)[:, tt, hf * 512:(hf + 1) * 512], b, (0, 512), bk(2 * tt + hf))
        key = ("Y", name, r0)
        P.dma("pool", "yout", dst[r0:r0 + 512, :].rearrange("(t p) d -> p t d", p=128), stg, reads=bk(0, 8), writes=[key])
        return key

    def proj_fm(wv, col_lo, wkey):
        b = nb()
        for kc in range(8):
            mm(b, (0, 512), wv[:, kc, col_lo:col_lo + 128], Hb[:, kc, :], kc == 0, kc == 7, reads=[wkey, ("H", kc)])
        return b

    def proj_fm4(wv, wkey):
        banks = [nb() for _ in range(4)]
        for kc in range(8):
            for m in range(4):
                mm(banks[m], (0, 512), wv[:, kc, m * 128:(m + 1) * 128], Hb[:, kc, :], kc == 0, kc == 7,
                   reads=[wkey, ("H", kc)])
        return banks

    def proj_tm(wv, col_lo, ncol, tt, wkey):
        b = nb()
        for kc in range(8):
            mm(b, (0, ncol), Hb[:, kc, tt * 128:(tt + 1) * 128], wv[:, kc, col_lo:col_lo + ncol], kc == 0, kc == 7,
               reads=[wkey, ("H", kc)])
        return b

    def load_wblock(w, col_lo, ncol):
        st = wstage()
        wv = WS[:, st, 0:8 * ncol].rearrange("p (k n) -> p k n", k=8)
        P.dma("sp", ("ws", st), wv, w[:, col_lo:col_lo + ncol].rearrange("(k p) n -> p k n", p=128), writes=[("W", st)])
        return wv, ("W", st)

    bias_ctr = [0]
    acc_ctr = [0]

    def attention(jobs, mix_blk, pts=(18, 19), bias_blks=(20, 21, 22)):
        npt = len(pts)
        for ji, (qblk, half, q0, q1, ktiles) in enumerate(jobs):
            n = q1 - q0
            ob, db = (4, 5) if acc_ctr[0] % 2 == 0 else (6, 7)
            acc_ctr[0] += 1
            nk = len(ktiles)
            bias_blk = {}
            sbank = {}

            def prefetch(ki):
                if ki < nk and ktiles[ki][2] is not None:
                    bb = bias_blks[bias_ctr[0] % len(bias_blks)]
                    bias_ctr[0] += 1
                    ld(br(bb), ktiles[ki][2], [], bk(bb))
                    bias_blk[ki] = bb

            def score(ki):
                if ki >= nk:
                    return
                kT, vv, bias, rkeys = ktiles[ki]
                b = nb()
                sbank[ki] = b
                mm(b, (0, n), kT, br(qblk)[:, q0:q1], True, bias is None, reads=rkeys + bk(qblk))
                if bias is not None:
                    bb = bias_blk[ki]
                    mm(b, (0, n), IDR, br(bb)[:, 0:n], False, True, reads=bk(bb) + KC)
            prefetch(0)
            prefetch(1)
            score(0)
            score(1)
            for ki, (kT, vv, bias, rkeys) in enumerate(ktiles):
                prefetch(ki + 2)
                b = sbank[ki]
                pt = pts[ki % npt]
                act([("ps", b)], bk(pt), out=br(pt)[:, 0:n], in_=ps[b][:, 0:n], func=AF.Exp)
                score(ki + 2)
                mm(ob, (0, n), vv, br(pt)[:, 0:n], ki == 0, ki == nk - 1, reads=rkeys + bk(pt))
                mm(db, (0, n), ONES, br(pt)[:, 0:n], ki == 0, ki == nk - 1, reads=bk(pt) + KC, signal=True)
            r0, r1 = half * 64, half * 64 + 64
            dve("reciprocal", [("ps", db)], ["RDEN"], out=RS[r0:r1, 1, 0:n], in_=ps[db][r0:r1, 0:n])
            dve("tensor_tensor", [("ps", ob), "RDEN"], bk(mix_blk), out=br(mix_blk)[r0:r1, q0:q1],
                in0=ps[ob][r0:r1, 0:n], in1=RS[r0:r1, 1, 0:n], op=ALU.mult)

    def mixer_out(l, w_out, col0, ci):
        for nblk in range(2):
            wv, wk = load_wblock(w_out, nblk * 512, 512)
            for mi in range(4):
                m = nblk * 4 + mi
                b = nb()
                for kc in range(8):
                    mm(b, (0, 512), wv[:, kc, mi * 128:(mi + 1) * 128], br(24 + kc), kc == 0, kc == 7, reads=[wk] + bk(24 + kc))
                evac_sq(m, b, m)
        post_norm(l, 1, col0, ci, sq0=0)

    def scr_w(dst, src, reads, key):
        P.dma("pool", ("scr", key), dst, src, reads=reads, writes=[key])

    def even_loop1(l, kind, col0, ci, tok0, zoff, bl):
        pre_norm(l, 1, col0, ci)
        wv, wk = load_wblock(ev_w_in, 0, 512)
        qbanks = proj_fm4(wv, wk)
        for m in range(4):
            evac(br(m), qbanks[m], (0, 512), bk(m), scale=0.125)
        scr_w(q_s[0:4, :, tok0:tok0 + 512].rearrange("c p t -> p c t"), brs(0, 4), bk(0, 4), "QS")
        wv, wk = load_wblock(ev_w_in, 512, 512)
        for m in range(4):
            b = proj_fm(wv, m * 128, wk)
            evac(br(4 + m), b, (0, 512), bk(4 + m))
        scr_w(k_s[:, tok0:tok0 + 512].rearrange("(c p) t -> p c t", p=128), brs(4, 4), bk(4, 4), "KS")
        if kind == "p":
            for tt in range(4):
                b = proj_tm(wv, 0, 512, tt, wk)
                evac(br(8 + tt), b, (0, 512), bk(8 + tt))
            for tt in range(4):
                P.dma("pool", ("so", "ak", tt), sak[bl + tt // 2][:, (tt % 2) * 128:(tt % 2 + 1) * 128, :].rearrange("h p d -> p h d"),
                      bf(8 + tt).rearrange("p (h d) -> p h d", d=64), reads=bk(8 + tt), writes=[("SAK", bl, tt)])
        wv, wk = load_wblock(ev_w_in, 1024, 512)
        for tt in range(4):
            b = proj_tm(wv, 0, 512, tt, wk)
            evac(br(12 + tt), b, (0, 512), bk(12 + tt))
        scr_w(v_s[tok0:tok0 + 512, :].rearrange("(t p) f -> p t f", p=128), brs(12, 4), bk(12, 4), "VS")
        if kind == "p":
            for tt in range(4):
                P.dma("pool", ("so", "av", tt), sav[bl + tt // 2][:, (tt % 2) * 128:(tt % 2 + 1) * 128, :].rearrange("h p d -> p h d"),
                      bf(12 + tt).rearrange("p (h d) -> p h d", d=64), reads=bk(12 + tt), writes=[("SAV", bl, tt)])
        wv, wk = load_wblock(ev_w_in, 1536, 512)
        for m in range(4):
            b = proj_fm(wv, m * 128, wk)
            evac(br(16 + m), b, (0, 512), bk(16 + m))
        scr_w(g_s[:, tok0:tok0 + 512].rearrange("(c p) t -> p c t", p=128), brs(16, 4), bk(16, 4), "GS")
        wv, wk = load_wblock(ev_w_in, 2048, 512)
        for m in range(4):
            b = proj_fm(wv, m * 128, wk)
            evac(br(8 + m), b, (0, 512), bk(8 + m))
        wv, wk = load_wblock(ev_w_in, 2560, 512)
        for m in range(4):
            b = proj_fm(wv, m * 128, wk)
            dve("tensor_tensor", [("ps", b)] + bk(8 + m), bk(8 + m), out=br(8 + m), in0=bf(8 + m), in1=ps[b][:], op=ALU.mult)
        for (seg_tok, seg_len, zcol) in zoff:
            scr_w(z_s[:, zcol:zcol + seg_len].rearrange("(c p) t -> p c t", p=128),
                  BIG[:, 8:12, seg_tok:seg_tok + seg_len].bitcast(F32R), bk(8, 4), "ZS")

    def even_loop2(l, kind, col0, ci, tok0, zoff, qt):
        kts_q = na_ktiles(qt)
        klo, nkt = kts_q[0], len(kts_q)

        def conv_chunk(c):
            ld(br(15), g_s[c * 128:(c + 1) * 128, tok0:tok0 + 512], ["GS"], bk(15))
            mixb = 28 + c
            for (seg_tok, seg_len, zcol) in zoff:
                zl = seg_len + 2
                ztr = flat(brs(16, 2))[:, 0:zl]
                zt = flat(bfs(16, 2))[:, 0:zl]
                zk = bk(16, 2)
                ld(ztr, z_s[c * 128:(c + 1) * 128, zcol - 1:zcol - 1 + zl], ["ZS"], zk)
                acc = bf(mixb)[:, seg_tok:seg_tok + seg_len]
                accr = br(mixb)[:, seg_tok:seg_tok + seg_len]
                dve("tensor_scalar", zk + KC, bk(mixb), out=accr, in0=zt[:, 0:seg_len], scalar1=sm(CONVW + c), scalar2=None,
                    op0=ALU.mult)
                for jx in (1, 2):
                    dve("scalar_tensor_tensor", zk + KC + bk(mixb), bk(mixb), out=accr, in0=zt[:, jx:jx + seg_len],
                        scalar=sm(CONVW + jx * 4 + c), in1=acc, op0=ALU.mult, op1=ALU.add)
            dve("scalar_tensor_tensor", bk(mixb) + bk(15) + KC, bk(mixb), out=br(mixb), in0=bf(mixb), scalar=sm(CONVB + c),
                in1=bf(15), op0=ALU.add, op1=ALU.mult)

        for qa_, qb_ in ((0, 1), (2, 3)):
            zero_half(qa_, 64)
            zero_half(qb_, 0)

        def chunk_loads(c):
            sset = c % 2
            qa, qb = (0, 1) if sset == 0 else (2, 3)
            kb0, vb0 = 4 + 2 * sset, 8 + 2 * sset
            cx0, cx1 = (12, 13) if sset == 0 else (14, 23)
            ld(br(qa)[0:64, :], q_s[c, 0:64, tok0:tok0 + 512], ["QS"], bk(qa))
            ld(br(qb)[64:128, :], q_s[c, 64:128, tok0:tok0 + 512], ["QS"], bk(qb))
            if kind == "p":
                ld(br(kb0), k_s[c * 128:(c + 1) * 128, tok0:tok0 + 512], ["KS"], bk(kb0))
                v8 = br(vb0).rearrange("p (t f) -> p t f", f=128)
                ld(v8, v_s[tok0:tok0 + 512, c * 128:(c + 1) * 128].rearrange("(t p) f -> p t f", p=128), ["VS"], bk(vb0))
            else:
                k2 = flat(brs(kb0, 2))[:, 0:nkt * 128]
                v2 = brs(vb0, 2).rearrange("p a (t f) -> p (a t) f", f=128)[:, 0:nkt, :]
                ld(k2, k_s[c * 128:(c + 1) * 128, klo * 128:(klo + nkt) * 128], ["KS"], bk(kb0, 2))
                ld(v2, v_s[klo * 128:(klo + nkt) * 128, c * 128:(c + 1) * 128].rearrange("(t p) f -> p t f", p=128), ["VS"],
                   bk(vb0, 2))
                for tt in range(2):
                    ld(br(cx0)[:, tt * 128:(tt + 1) * 128].rearrange("p (h d) -> p h d", h=2),
                       ca_k[2 * c:2 * c + 2, tt * 128:(tt + 1) * 128, :].rearrange("h p d -> p h d"), [], bk(cx0))
                    ld(br(cx1)[:, tt * 128:(tt + 1) * 128].rearrange("p (h d) -> p h d", h=2),
                       ca_v[2 * c:2 * c + 2, tt * 128:(tt + 1) * 128, :].rearrange("h p d -> p h d"), [], bk(cx1))

        chunk_loads(0)
        for c in range(4):
            if c + 1 < 4:
                chunk_loads(c + 1)
            sset = c % 2
            qa, qb = (0, 1) if sset == 0 else (2, 3)
            kb0, vb0 = 4 + 2 * sset, 8 + 2 * sset
            cx0, cx1 = (12, 13) if sset == 0 else (14, 23)
            jobs = []
            if kind == "p":
                v8 = br(vb0).rearrange("p (t f) -> p t f", f=128)
                for half in range(2):
                    for sg in range(2):
                        kts = []
                        for kt in range(2):
                            ktile = sg * 2 + kt
                            kts.append((br(kb0)[:, ktile * 128:(ktile + 1) * 128], v8[:, ktile, :], None, bk(kb0) + bk(vb0)))
                        jobs.append((qa if half == 0 else qb, half, sg * 256, (sg + 1) * 256, kts))
            else:
                k2 = flat(brs(kb0, 2))[:, 0:nkt * 128]
                v2 = brs(vb0, 2).rearrange("p a (t f) -> p (a t) f", f=128)[:, 0:nkt, :]
                b = nb()
                for tt in range(2):
                    tr(b, (tt * 128, (tt + 1) * 128), bf(cx0)[:, tt * 128:(tt + 1) * 128], bk(cx0), tt == 1)
                evac(br(cx1)[:, 256:512], b, (0, 256), bk(cx1))
                vc = br(cx1)[:, 0:256].rearrange("p (t f) -> p t f", f=128)
                for half in range(2):
                    h = 2 * c + half
                    kts = []
                    for j in range(nkt):
                        kts.append((k2[:, j * 128:(j + 1) * 128], v2[:, j, :], nab_d[h, qt, j], bk(kb0, 2) + bk(vb0, 2)))
                    for tt in range(2):
                        kts.append((br(cx1)[:, 256 + tt * 128:256 + (tt + 1) * 128], vc[:, tt, :], None, bk(cx1)))
                    jobs.append((qa if half == 0 else qb, half, 0, 512, kts))
            attention(jobs, 24 + c)
            conv_chunk(c)
        mixer_out(l, ev_w_out, col0, ci)

    def qk_norm_rope(blks, gcol, rope, qscale):
        nbk = len(blks)
        banks = []
        for i, bid in enumerate(blks):
            act(bk(bid), bk(SQ0 + i), out=br(SQ0 + i), in_=bf(bid), func=AF.Square)
        for i, bid in enumerate(blks):
            b = nb()
            banks.append(b)
            mm(b, (0, 512), ONES64, br(SQ0 + i), True, True, reads=bk(SQ0 + i) + KC)
            rk = ["RSTD", "RDEN"][i]
            if qscale:
                act([("ps", b)] + KC, [rk], out=RS[:, i, :], in_=ps[b][:], func=AF.Ln, scale=1.0, bias=sm(EPS64C))
            else:
                act([("ps", b)] + KC, [rk], out=RS[:, i, :], in_=ps[b][:], func=AF.Ln, scale=1.0 / 64.0, bias=sm(EPSC))
            act([rk], [rk], out=RS[:, i, :], in_=RS[:, i, :], func=AF.Exp, scale=-0.5)
        assert nbk <= 2
        rkeys = ["RSTD", "RDEN"][:nbk]
        for i, bid in enumerate(blks):
            if not rope:
                dve("scalar_tensor_tensor", bk(bid) + rkeys + KC, bk(bid), out=br(bid), in0=bf(bid), scalar=sm(gcol),
                    in1=RS[:, i, :], op0=ALU.mult, op1=ALU.mult)
                continue
            s1, t1, t2 = SQ0 + i, 15 + i, 19 + i
            dve("scalar_tensor_tensor", bk(bid) + rkeys + KC, bk(s1), out=br(s1), in0=bf(bid), scalar=sm(gcol),
                in1=RS[:, i, :], op0=ALU.mult, op1=ALU.mult)
            b = nb()
            mm(b, (0, 512), PSWAP, br(s1), True, True, reads=bk(s1) + KC)
            dve("tensor_tensor", bk(s1) + bk(12), bk(t1), out=br(t1), in0=bf(s1), in1=bf(12), op=ALU.mult)
            dve("tensor_tensor", [("ps", b)] + bk(13), bk(t2), out=br(t2), in0=ps[b][:], in1=bf(13), op=ALU.mult)
            dve("tensor_tensor", bk(t1) + bk(t2), bk(bid), out=br(bid), in0=bf(t1), in1=bf(t2), op=ALU.add)

    def odd_loop1(l, kind, col0, ci, tok0, zoff, bl):
        pre_norm(l, 1, col0, ci)
        rope = (kind == "s")
        if rope:
            ld(br(12), cos_d[:, tok0:tok0 + 512], [], bk(12))
            ld(br(13), sin_d[:, tok0:tok0 + 512], [], bk(13))
        wv, wk = load_wblock(od_w_in, 0, 512)
        ubanks = proj_fm4(wv, wk)
        for m in range(4):
            evac(br(8 + m), ubanks[m], (0, 512), bk(8 + m))
        for (seg_tok, seg_len, zcol) in zoff:
            scr_w(z_s[:, zcol:zcol + seg_len].rearrange("(c p) t -> p c t", p=128),
                  BIG[:, 8:12, seg_tok:seg_tok + seg_len].bitcast(F32R), bk(8, 4), "ZS")
        wv, wk = load_wblock(od_w_in, 512, 512)
        for m in range(4):
            b = proj_fm(wv, m * 128, wk)
            evac(br(m), b, (0, 512), bk(m))
        wv, wk = load_wblock(od_w_in, 1024, 256)
        b = proj_fm(wv, 0, wk)
        evac(br(4), b, (0, 512), bk(4))
        vdup = brs(5, 2).rearrange("p a (t f) -> p (a t) f", f=256)
        for tt in range(4):
            b = proj_tm(wv, 128, 128, tt, wk)
            veng = "act" if tt % 2 == 0 else "dve"
            for g in range(2):
                for dup in range(2):
                    evac(vdup[:, tt, (2 * g + dup) * 64:(2 * g + dup + 1) * 64], b, (g * 64, g * 64 + 64), bk(5, 2), eng=veng)
            if kind == "p":
                evac(br(7)[:, tt * 128:(tt + 1) * 128], b, (0, 128), bk(7), eng=veng)
        scr_w(v_s[tok0:tok0 + 512, 0:256].rearrange("(t p) f -> p t f", p=128), vdup, bk(5, 2), "VS")
        ktm_banks = []
        if kind == "p":
            for tt in range(4):
                ktm_banks.append(proj_tm(wv, 0, 128, tt, wk))
                b = ktm_banks[-1]
                for g in range(2):
                    act([("ps", b)], bk(15) + ["TMPS"], out=br(15)[:, g * 64:(g + 1) * 64], in_=ps[b][:, g * 64:(g + 1) * 64],
                        func=AF.Square, accum_out=TMPS[:, g:g + 1])
                act(["TMPS"] + KC, ["TMPS"], out=TMPS[:, 2:4], in_=TMPS[:, 0:2], func=AF.Ln, scale=1.0 / 64.0, bias=sm(EPSC))
                act(["TMPS"], ["TMPS"], out=TMPS[:, 4:6], in_=TMPS[:, 2:4], func=AF.Exp, scale=-0.5)
                for g in range(2):
                    dve("scalar_tensor_tensor", [("ps", b), "TMPS"] + KC, bk(14),
                        out=br(14)[:, tt * 128 + g * 64:tt * 128 + (g + 1) * 64], in0=ps[b][:, g * 64:(g + 1) * 64],
                        scalar=TMPS[:, 4 + g:5 + g], in1=GKR[:], op0=ALU.mult, op1=ALU.mult)
        qk_norm_rope([0, 1], QN, rope, True)
        qk_norm_rope([2, 3], QN, rope, True)
        scr_w(q_s[0:4, :, tok0:tok0 + 512].rearrange("c p t -> p c t"), brs(0, 4), bk(0, 4), "QS")
        qk_norm_rope([4], KN, rope, False)
        scr_w(k_s[0:128, tok0:tok0 + 512], br(4), bk(4), "KS")
        if kind == "p":
            for tt in range(4):
                P.dma("pool", ("so", "dv", tt), sdv[bl + tt // 2][:, (tt % 2) * 128:(tt % 2 + 1) * 128, :].rearrange("g p d -> p g d"),
                      bf(7)[:, tt * 128:(tt + 1) * 128].rearrange("p (g d) -> p g d", g=2),
                      reads=bk(7), writes=[("SDV", bl, tt)])
            for tt in range(4):
                P.dma("pool", ("so", "dk", tt), sdk[bl + tt // 2][:, (tt % 2) * 128:(tt % 2 + 1) * 128, :].rearrange("g p d -> p g d"),
                      bf(14)[:, tt * 128:(tt + 1) * 128].rearrange("p (g d) -> p g d", g=2),
                      reads=bk(14), writes=[("SDK", bl, tt)])

    def odd_loop2(l, kind, col0, ci, tok0, zoff):
        inv_src = (invp_d.rearrange("g p t -> p g t") if kind == "p" else invs_d[:, :, tok0:tok0 + 512].rearrange("g p t -> p g t"))
        ld(brs(19, 4), inv_src, [], bk(19, 4))
        for (seg_tok, seg_len, zcol) in zoff:
            Ls = seg_len
            Lp = Ls + 16

            def v3(b0, nblk, ng, r=False):
                base = flat(brs(b0, nblk)) if r else flat(bfs(b0, nblk))
                return base[:, 0:ng * Lp].rearrange("p (g t) -> p g t", g=ng)
            ku, ka, kb, kc_, kd = bk(0, 5), bk(5, 5), bk(10, 4), bk(15, 3), bk(18)
            ld(v3(0, 5, 4, True), z_s[:, zcol - 8:zcol + Ls + 8].rearrange("(g p) t -> p g t", p=128), ["ZS"], ku)
            U, A, B, C = v3(0, 5, 4), v3(5, 5, 4), v3(10, 4, 3), v3(15, 3, 2)
            Ar, Br, Cr = v3(5, 5, 4, True), v3(10, 4, 3, True), v3(15, 3, 2, True)
            dve("tensor_tensor", ku, ka, out=Ar[:, :, 0:Ls + 15], in0=U[:, :, 0:Ls + 15], in1=U[:, :, 1:Ls + 16], op=ALU.add)
            dve("tensor_tensor", ka, kb, out=Br[:, :, 0:Ls + 13], in0=A[:, 1:4, 0:Ls + 13], in1=A[:, 1:4, 2:Ls + 15], op=ALU.add)
            dve("tensor_tensor", kb, kc_, out=Cr[:, :, 0:Ls + 9], in0=B[:, 1:3, 0:Ls + 9], in1=B[:, 1:3, 4:Ls + 13], op=ALU.add)
            dve("tensor_tensor", kc_, kd, out=br(18)[:, 0:Ls], in0=C[:, 1, 0:Ls], in1=C[:, 1, 8:8 + Ls], op=ALU.add)
            ssrc = [(A[:, 0, 7:7 + Ls], ka), (B[:, 0, 6:6 + Ls], kb), (C[:, 0, 4:4 + Ls], kc_), (bf(18)[:, 0:Ls], kd)]
            for gi in range(4):
                sv, skey = ssrc[gi]
                pb = 23 if gi % 2 == 0 else 14
                tmpb = 14 if gi % 2 == 0 else 23
                dve("tensor_tensor", skey + bk(19 + gi), bk(24 + gi), out=br(24 + gi)[:, seg_tok:seg_tok + Ls], in0=sv,
                    in1=bf(19 + gi)[:, seg_tok:seg_tok + Ls], op=ALU.mult)
                dve("tensor_tensor", bk(24 + gi) + ku, bk(28 + gi), out=br(28 + gi)[:, seg_tok:seg_tok + Ls],
                    in0=bf(24 + gi)[:, seg_tok:seg_tok + Ls], in1=U[:, gi, 8:8 + Ls], op=ALU.subtract)
        for gi in range(4):
            b = nb()
            mm(b, (0, 512), PW[:, gi, :], br(28 + gi), True, True, reads=bk(28 + gi) + KC)
            dve("tensor_scalar", [("ps", b)] + KC, bk(24 + gi), out=br(24 + gi), in0=ps[b][:], scalar1=sm(PSCALE + gi),
                scalar2=None, op0=ALU.mult)
        v8 = brs(8, 8).rearrange("p a (t f) -> p (a t) f", f=256)
        if kind == "p":
            ld(br(4), k_s[0:128, tok0:tok0 + 512], ["KS"], bk(4))
            ld(v8[:, 0:4, :], v_s[tok0:tok0 + 512, 0:256].rearrange("(t p) f -> p t f", p=128), ["VS"], bk(8, 2))
            k4 = br(4)
        else:
            k4 = flat(brs(4, 4))
            ld(k4, k_s[0:128, 0:2048], ["KS"], bk(4, 4))
            ld(v8, v_s[0:2048, 0:256].rearrange("(t p) f -> p t f", p=128), ["VS"], bk(8, 8))
            for tt in range(2):
                ld(br(16)[:, tt * 128:(tt + 1) * 128].rearrange("p (g d) -> p g d", g=2),
                   cd_k[:, tt * 128:(tt + 1) * 128, :].rearrange("g p d -> p g d"), [], bk(16))
            vc = br(17).rearrange("p (t f) -> p t f", f=256)
            for g in range(2):
                for dup in range(2):
                    P.dma("pool", ("d", "B", 17), vc[:, :, (2 * g + dup) * 64:(2 * g + dup + 1) * 64],
                          cd_v[g].rearrange("(t p) d -> p t d", p=128), writes=bk(17))
            b = nb()
            for tt in range(2):
                tr(b, (tt * 128, (tt + 1) * 128), bf(16)[:, tt * 128:(tt + 1) * 128], bk(16), tt == 1)
            evac(br(16)[:, 256:512], b, (0, 256), bk(16))
        for qblk_ in (0, 1):
            zero_half(qblk_, 64)
        for qblk_ in (2, 3):
            zero_half(qblk_, 0)
        for h in range(8):
            g = h // 4
            half = h % 2
            c = h // 2
            qb_ = 2 * g + (h % 2)
            ld(br(qb_)[g * 64:(g + 1) * 64, :], q_s[h // 2, (h % 2) * 64:(h % 2) * 64 + 64, tok0:tok0 + 512], ["QS"], bk(qb_))
            vcol = 2 * g * 64
            jobs = []
            if kind == "p":
                for sg in range(2):
                    kts = []
                    for kt in range(2):
                        ktile = sg * 2 + kt
                        kts.append((k4[:, ktile * 128:(ktile + 1) * 128], v8[:, ktile, vcol:vcol + 128], None, bk(4) + bk(8, 2)))
                    jobs.append((qb_, half, sg * 256, (sg + 1) * 256, kts))
            else:
                kts = []
                for kt in range(16):
                    kts.append((k4[:, kt * 128:(kt + 1) * 128], v8[:, kt, vcol:vcol + 128], None, bk(4, 4) + bk(8, 8)))
                for tt in range(2):
                    kts.append((br(16)[:, 256 + tt * 128:256 + (tt + 1) * 128], vc[:, tt, vcol:vcol + 128], None, bk(16) + bk(17)))
                jobs.append((qb_, half, 0, 512, kts))
            attention(jobs, 28 + c, pts=(18, 19, 20))
        mixer_out(l, od_w_out, col0, ci)

    out_keys = []

    def zoff_for(kind, tile_idx, pad):
        if kind == "p":
            return [(0, 256, pad), (256, 256, 3 * pad + 256)]
        return [(0, 512, pad + tile_idx * 512)]

    for pp in range(N_PROMPT_PASS):
        bl = pp * 2
        load_x(xp, pp * 512, 0)
        ffn(0, 0, 0, 0)
        if pp == 0:
            mod_layer(1)
        zero_z()
        zo = zoff_for("p", 0, 1)
        even_loop1(0, "p", 0, 0, 0, zo, bl)
        even_loop2(0, "p", 0, 0, 0, zo, 0)
        ffn(0, 1, 0, 0)
        ffn(1, 0, 0, 0)
        zero_z()
        zo = zoff_for("p", 0, 8)
        odd_loop1(1, "p", 0, 0, 0, zo, bl)
        odd_loop2(1, "p", 0, 0, 0, zo)
        ffn(1, 1, 0, 0)
        out_keys.append(store_y(yp, "yp", pp * 512, 0))
        out_keys += [(nm, bl, tt) for nm in ("SAK", "SAV", "SDK", "SDV") for tt in range(4)]
    if DO_SAMPLE:
        for t in range(4):
            load_x(xs, t * 512, t * 512)
        zero_z()
        for t in range(4):
            ffn(0, 0, t * 512, 1)
            even_loop1(0, "s", t * 512, 1, t * 512, zoff_for("s", t, 1), 0)
        for t in range(4):
            even_loop2(0, "s", t * 512, 1, t * 512, zoff_for("s", t, 1), t)
            ffn(0, 1, t * 512, 1)
        zero_z()
        for t in range(4):
            ffn(1, 0, t * 512, 1)
            odd_loop1(1, "s", t * 512, 1, t * 512, zoff_for("s", t, 8), 0)
        for t in range(4):
            odd_loop2(1, "s", t * 512, 1, t * 512, zoff_for("s", t, 8))
            if t > 0:
                out_keys.append(store_y(ys, "ys", (t - 1) * 512, (t - 1) * 512))
            ffn(1, 1, t * 512, 1)
        out_keys.append(store_y(ys, "ys", 3 * 512, 3 * 512))
    P.final_wait("pool", out_keys)
    P.emit()
    return nc


def _na_bias(rpb):
    H = rpb.shape[0]
    r = np.arange(32)
    rs = np.clip(r - 4, 0, 24)
    cq = np.arange(64)
    cs = np.clip(cq - 8, 0, 48)
    out = np.full((H, 4, 8, 128, 512), NEG, np.float32)
    for qt in range(4):
        kts = na_ktiles(qt)
        for j, kt in enumerate(kts):
            kr = (2 * kt + np.arange(2))[:, None, None, None]
            kc = np.arange(64)[None, :, None, None]
            qr = (8 * qt + np.arange(8))[None, None, :, None]
            qc = np.arange(64)[None, None, None, :]
            valid = (kr >= rs[qr]) & (kr < rs[qr] + 8) & (kc >= cs[qc]) & (kc < cs[qc] + 16)
            ro = np.clip(kr - qr + 7, 0, 14)
            co = np.clip(kc - qc + 15, 0, 30)
            ro_b, co_b, valid_b = np.broadcast_arrays(ro, co, valid)
            for h in range(H):
                vals = rpb[h][ro_b, co_b]
                tile = np.where(valid_b, vals, np.float32(NEG)).astype(np.float32)
                out[h, qt, j] = tile.reshape(128, 512)
    return out


def _consts():
    ident = np.eye(128, dtype=np.float32)
    ones = np.ones((128, 128), np.float32)
    o64 = np.zeros((128, 128), np.float32)
    o64[:64, :64] = 1.0
    o64[64:, 64:] = 1.0
    psw = np.zeros((128, 128), np.float32)
    for i in range(64):
        psw[2 * i + 1, 2 * i] = -1.0
        psw[2 * i, 2 * i + 1] = 1.0
    cst = np.stack([ident, ones, o64, psw], axis=1)
    t = np.arange(2048)
    row = (t // 64).astype(np.float32)
    col = (t % 64).astype(np.float32)
    inv = (np.float32(10000.0) ** (-np.arange(16, dtype=np.float32) / np.float32(16))).astype(np.float32)
    ang = np.concatenate([row[:, None] * inv, col[:, None] * inv], axis=-1).astype(np.float32)
    idx = (np.arange(128) % 64) // 2
    cosT = np.cos(ang).astype(np.float32)[:, idx].T.copy()
    sinT = np.sin(ang).astype(np.float32)[:, idx].T.copy()

    def invcnt(L, w):
        tt = np.arange(L)
        lo = np.maximum(tt - w // 2, 0)
        hi = np.minimum(tt + w - w // 2, L)
        return (1.0 / (hi - lo).astype(np.float32)).astype(np.float32)
    invp = np.stack([np.tile(np.tile(invcnt(256, w), 2)[None, :], (128, 1)) for w in (2, 4, 8, 16)]).astype(np.float32)
    invs = np.stack([np.tile(invcnt(2048, w)[None, :], (128, 1)) for w in (2, 4, 8, 16)]).astype(np.float32)
    return cst, ident, cosT, sinT, invp, invs


def kernel(x_prompt, x_sample, cache_a_k, cache_a_v, cache_d_k, cache_d_v, c, c_ctx,
           mod_w, mod_b, norm_w, ffn_w1, ffn_w2,
           ev_w_in, ev_rpb, ev_conv_w, ev_conv_b, ev_w_out,
           od_w_in, od_pool_w, od_pool_scale, od_q_norm, od_k_norm, od_w_out):
    f = lambda a: np.ascontiguousarray(np.asarray(a, dtype=np.float32))
    x_prompt, x_sample = f(x_prompt), f(x_sample)
    cst, ident, cosT, sinT, invp, invs = _consts()
    nab = _na_bias(f(ev_rpb)[0])
    norm_w, mod_b = f(norm_w), f(mod_b)
    base = np.zeros((128, 400), np.float32)
    base[:, 0:96] = norm_w.reshape(2, 6, 8, 128).transpose(3, 0, 1, 2).reshape(128, 96)
    base[:, 96:240] = mod_b.reshape(2, 72, 128).transpose(2, 0, 1).reshape(128, 144)
    base[:, 240:252] = f(ev_conv_w)[0].reshape(3, 4, 128).transpose(2, 0, 1).reshape(128, 12)
    base[:, 252:256] = f(ev_conv_b)[0].reshape(4, 128).T
    base[:, 256:260] = f(od_pool_scale)[0].reshape(4, 128).T
    qn = f(od_q_norm)[0]
    kn = f(od_k_norm)[0]
    base[:, 260] = np.tile(qn, 2)
    base[:, 261] = np.tile(kn, 2)
    base[:, 399] = EPS
    base[:, 398] = 64.0 * EPS
    gk_row = np.tile(kn[None, :], (128, 1)).astype(np.float32)
    pool_w = f(od_pool_w)[0].transpose(1, 0, 2).copy()
    shared = {
        "mod_w": f(mod_w), "ffn_w1": f(ffn_w1), "ffn_w2": f(ffn_w2),
        "ev_w_in": f(ev_w_in)[0], "ev_w_out": f(ev_w_out)[0], "od_w_in": f(od_w_in)[0], "od_w_out": f(od_w_out)[0],
        "pool_w": pool_w, "cst": cst, "identf": ident, "gk_row": gk_row, "nab": nab,
        "cosT": cosT, "sinT": sinT, "invp": invp, "invs": invs,
    }
    c, c_ctx = f(c), f(c_ctx)
    in_maps = []
    for core in range(N_CORES):
        b = core % 2
        sm_ = base.copy()
        cond = np.stack([c_ctx.reshape(8, 128).T, c[b].reshape(8, 128).T], axis=-1)
        sm_[:, 262:278] = cond.reshape(128, 16)
        m = dict(shared)
        m["small"] = sm_
        m["xp"] = x_prompt[4 * core:4 * core + 4].reshape(1024, D)
        m["xs"] = x_sample[b]
        m["ca_k"] = f(cache_a_k)[b, 0]
        m["ca_v"] = f(cache_a_v)[b, 0]
        m["cd_k"] = f(cache_d_k)[b, 0]
        m["cd_v"] = f(cache_d_v)[b, 0]
        in_maps.append(m)
    nc = build_program()
    res = run_bass_kernel_spmd(nc, in_maps, core_ids=list(range(N_CORES)))
    r = res.results
    y_prompt = np.concatenate([r[i]["yp"].reshape(4, 256, D) for i in range(N_CORES)], axis=0)
    y_sample = np.stack([r[0]["ys"], r[1]["ys"]], axis=0)
    sak = np.concatenate([r[i]["sak"] for i in range(N_CORES)], axis=0)[:, None]
    sav = np.concatenate([r[i]["sav"] for i in range(N_CORES)], axis=0)[:, None]
    sdk = np.concatenate([r[i]["sdk"] for i in range(N_CORES)], axis=0)[:, None]
    sdv = np.concatenate([r[i]["sdv"] for i in range(N_CORES)], axis=0)[:, None]
    return (y_prompt.astype(np.float32), y_sample.astype(np.float32), sak.astype(np.float32),
            sav.astype(np.float32), sdk.astype(np.float32), sdv.astype(np.float32))
```
